# Optimizing a Trainium2 kernel written in Bass

```python
import functools
import jax, jax.numpy as jnp
from jax import lax
import numpy as np

D_MODEL = 1024
BATCH = 8
SEQ = 2048
DEPTH = 1
DEC_BATCH = 128
DEC_SEQ = 8
PAST_LEN = 16384
PAGE_SIZE = 128

PLE_DIM = 256
NORM_EPS = 1e-6
GLA_HEADS = 4
GLA_DK = 64
GLA_DV = 128
GLA_KEY = GLA_HEADS * GLA_DK
GLA_VAL = GLA_HEADS * GLA_DV
GLA_GATE_RANK = 16
GLA_GATE_TEMP = 16.0
GLA_CHUNK = 16
GDN_HEADS = 4
GDN_DK = 128
GDN_DV = 128
GDN_KEY = GDN_HEADS * GDN_DK
GDN_VAL = GDN_HEADS * GDN_DV
GDN_CHUNK = 64
CONV_WIDTH = 4
CONV_CH = 2 * GDN_KEY + GDN_VAL
IN_SPLITS = (GLA_KEY, GLA_KEY, GLA_VAL, GLA_GATE_RANK, GLA_VAL,
             CONV_CH, GDN_HEADS, GDN_HEADS, GDN_VAL,
             D_MODEL, D_MODEL)
IN_COLS = sum(IN_SPLITS)

kernel_name = "hybrid_gla_gated_deltanet_decode_step"


def _split_offsets():
    return [int(v) for v in np.cumsum(IN_SPLITS)[:-1]]


def rms_norm(x, w):
    xf = x.astype(jnp.float32)
    y = xf * lax.rsqrt(jnp.mean(xf * xf, axis=-1, keepdims=True) + NORM_EPS)
    return (y * w.astype(jnp.float32)).astype(x.dtype)


def _l2norm(x):
    return x * lax.rsqrt(jnp.sum(x * x, axis=-1, keepdims=True) + NORM_EPS)


def _heads(x, n_heads):
    b, t, _ = x.shape
    return x.reshape(b, t, n_heads, -1).transpose(0, 2, 1, 3)


def _chunk(x, c):
    t = x.shape[2]
    n = -(-t // c)
    pad = n * c - t
    if pad:
        x = jnp.pad(x, [(0, 0), (0, 0), (0, pad)] + [(0, 0)] * (x.ndim - 3))
    return x.reshape(x.shape[:2] + (n, c) + x.shape[3:])


def gla_recurrence(q, k, v, gk, s0):
    t_len = q.shape[2]
    c = min(GLA_CHUNK, t_len)
    q, k, v, gk = (_chunk(a_.astype(jnp.float32), c) for a_ in (q, k, v, gk))
    b = jnp.cumsum(gk, axis=3)
    b_last = b[:, :, :, -1:, :]
    q_e = q * jnp.exp(b)
    k_e = k * jnp.exp(-b)
    k_end = k * jnp.exp(b_last - b)
    causal = jnp.tril(jnp.ones((c, c), dtype=bool))
    att = jnp.where(causal, jnp.einsum('bhncd,bhnsd->bhncs', q_e, k_e), 0.0)
    o_intra = jnp.einsum('bhncs,bhnsv->bhncv', att, v)
    chunk_decay = jnp.exp(b_last[:, :, :, 0, :])

    def step(s, inp):
        q_c, k_c, v_c, d_c = inp
        o_c = jnp.einsum('bhcd,bhdv->bhcv', q_c, s)
        s = s * d_c[..., None] + jnp.einsum('bhcd,bhcv->bhdv', k_c, v_c)
        return s, o_c

    xs = tuple(jnp.moveaxis(a_, 2, 0) for a_ in (q_e, k_end, v, chunk_decay))
    s_fin, o_inter = lax.scan(step, s0.astype(jnp.float32), xs)
    o = o_intra + jnp.moveaxis(o_inter, 0, 2)
    bsz, nh, n, _, dv = o.shape
    return o.reshape(bsz, nh, n * c, dv)[:, :, :t_len], s_fin


def gdn_recurrence(q, k, v, g, beta, s0):
    t_len = q.shape[2]
    c = min(GDN_CHUNK, t_len)
    q, k, v, g, beta = (_chunk(a_.astype(jnp.float32), c) for a_ in (q, k, v, g, beta))
    decay = jnp.cumsum(g, axis=-1)
    diff = decay[..., :, None] - decay[..., None, :]
    causal = jnp.tril(jnp.ones((c, c), dtype=bool))
    strict = jnp.tril(jnp.ones((c, c), dtype=bool), -1)
    gamma = jnp.where(causal, jnp.exp(jnp.where(causal, diff, 0.0)), 0.0)
    k_beta = k * beta[..., None]
    v_beta = v * beta[..., None]
    a_mat = jnp.where(strict, jnp.einsum('bhnid,bhnjd->bhnij', k_beta, k) * gamma, 0.0)
    t_mat = a_mat + jnp.eye(c, dtype=jnp.float32)
    solve = functools.partial(lax.linalg.triangular_solve, left_side=True, lower=True,
                              unit_diagonal=True)
    u = solve(t_mat, v_beta)
    w = solve(t_mat, k_beta * jnp.exp(decay)[..., None])
    qk = jnp.einsum('bhnid,bhnjd->bhnij', q, k) * gamma
    q_e = q * jnp.exp(decay)[..., None]
    d_last = decay[..., -1:]
    k_end = k * jnp.exp(d_last - decay)[..., None]
    chunk_decay = jnp.exp(d_last[..., 0])

    def step(s, inp):
        w_c, u_c, q_c, qk_c, k_c, d_c = inp
        v_new = u_c - jnp.einsum('bhcd,bhdv->bhcv', w_c, s)
        o_c = jnp.einsum('bhcd,bhdv->bhcv', q_c, s) + jnp.einsum('bhcs,bhsv->bhcv', qk_c, v_new)
        s = s * d_c[..., None, None] + jnp.einsum('bhcd,bhcv->bhdv', k_c, v_new)
        return s, o_c

    xs = tuple(jnp.moveaxis(a_, 2, 0) for a_ in (w, u, q_e, qk, k_end, chunk_decay))
    s_fin, o = lax.scan(step, s0.astype(jnp.float32), xs)
    o = jnp.moveaxis(o, 0, 2)
    bsz, nh, n, _, dv = o.shape
    return o.reshape(bsz, nh, n * c, dv)[:, :, :t_len], s_fin


def mixer_layer(h, p, s_gla, s_gdn, conv_buf, norm_w, w_in, w_gla_gate, b_gla_gate, gla_norm_w,
                conv_w, gdn_a_log, gdn_dt_bias, gdn_norm_w, w_up_gla, w_up_gdn, w_out,
                w_ple_gate, w_ple):
    bsz, t_len, _ = h.shape
    f32 = jnp.float32
    xn = rms_norm(h, norm_w)
    proj = xn @ w_in
    (q_a, k_a, v_a, g_a, z_a, qkv_b, a_b, b_b, z_b, gate_a, gate_b) = jnp.split(
        proj, _split_offsets(), axis=-1)

    gk = jax.nn.log_sigmoid((g_a @ w_gla_gate + b_gla_gate).astype(f32)) / GLA_GATE_TEMP
    o_a, s_gla_new = gla_recurrence(_heads(q_a, GLA_HEADS) * GLA_DK ** -0.5,
                                    _heads(k_a, GLA_HEADS), _heads(v_a, GLA_HEADS),
                                    _heads(gk, GLA_HEADS), s_gla)
    o_a = rms_norm(o_a.transpose(0, 2, 1, 3), gla_norm_w).reshape(bsz, t_len, GLA_VAL)
    y_a = (o_a.astype(h.dtype) * jax.nn.silu(z_a)) @ w_up_gla

    xpad = jnp.concatenate([conv_buf.astype(qkv_b.dtype), qkv_b], axis=1)
    conv = lax.conv_general_dilated(xpad.astype(f32), conv_w.astype(f32)[:, None, :], (1,), 'VALID',
                                    dimension_numbers=('NWC', 'WIO', 'NWC'),
                                    feature_group_count=CONV_CH)
    conv = jax.nn.silu(conv)
    conv_new = xpad[:, -(CONV_WIDTH - 1):, :]
    q_b, k_b, v_b = jnp.split(conv, [GDN_KEY, 2 * GDN_KEY], axis=-1)
    q_b = _l2norm(_heads(q_b, GDN_HEADS)) * GDN_DK ** -0.5
    k_b = _l2norm(_heads(k_b, GDN_HEADS))
    v_b = _heads(v_b, GDN_HEADS)
    beta = jax.nn.sigmoid(b_b.astype(f32)).transpose(0, 2, 1)
    g = (-jnp.exp(gdn_a_log.astype(f32)) *
         jax.nn.softplus((a_b + gdn_dt_bias).astype(f32))).transpose(0, 2, 1)
    o_b, s_gdn_new = gdn_recurrence(q_b, k_b, v_b, g, beta, s_gdn)
    o_b = rms_norm(o_b.transpose(0, 2, 1, 3), gdn_norm_w).reshape(bsz, t_len, GDN_VAL)
    y_b = (o_b.astype(h.dtype) * jax.nn.silu(z_b)) @ w_up_gdn

    merged = jax.nn.sigmoid(gate_a) * y_a + jax.nn.sigmoid(gate_b) * y_b
    h = h + merged @ w_out
    h = h + jax.nn.sigmoid(h @ w_ple_gate) * (p @ w_ple)
    return (h, s_gla_new.astype(h.dtype), s_gdn_new.astype(h.dtype), conv_new.astype(h.dtype))


def trunk(h, p, s_gla, s_gdn, conv_buf, norm_w, w_in, w_gla_gate, b_gla_gate, gla_norm_w, conv_w,
          gdn_a_log, gdn_dt_bias, gdn_norm_w, w_up_gla, w_up_gdn, w_out, w_ple_gate, w_ple,
          final_norm_w):
    new_gla, new_gdn, new_conv = [], [], []
    for i in range(DEPTH):
        h, sg, sd, cb = mixer_layer(h, p[i], s_gla[i], s_gdn[i], conv_buf[i], norm_w[i], w_in[i],
                                    w_gla_gate[i], b_gla_gate[i], gla_norm_w[i], conv_w[i],
                                    gdn_a_log[i], gdn_dt_bias[i], gdn_norm_w[i], w_up_gla[i],
                                    w_up_gdn[i], w_out[i], w_ple_gate[i], w_ple[i])
        new_gla.append(sg)
        new_gdn.append(sd)
        new_conv.append(cb)
    y = rms_norm(h, final_norm_w)
    return y, jnp.stack(new_gla), jnp.stack(new_gdn), jnp.stack(new_conv)


def setup_inputs(seed: int = 0) -> dict:
    key = jax.random.key(seed)
    ks = jax.random.split(key, 24)
    f32 = jnp.float32
    nrm = lambda k, shape, scale: jax.random.normal(k, shape, f32) * scale
    dt = jnp.exp(jax.random.uniform(ks[12], (DEPTH, GDN_HEADS), f32,
                                    minval=float(np.log(1e-3)), maxval=float(np.log(1e-1))))
    return {
        "x_prompt": nrm(ks[0], (BATCH, SEQ, D_MODEL), 1.0),
        "x_sample": nrm(ks[1], (DEC_BATCH, DEC_SEQ, D_MODEL), 1.0),
        "state_gla": nrm(ks[2], (DEPTH, DEC_BATCH, GLA_HEADS, GLA_DK, GLA_DV), 0.5),
        "state_gdn": nrm(ks[3], (DEPTH, DEC_BATCH, GDN_HEADS, GDN_DK, GDN_DV), 0.5),
        "state_conv": nrm(ks[4], (DEPTH, DEC_BATCH, CONV_WIDTH - 1, CONV_CH), 1.0),
        "p_prompt": nrm(ks[5], (DEPTH, BATCH, SEQ, PLE_DIM), 1.0),
        "p_sample": nrm(ks[6], (DEPTH, DEC_BATCH, DEC_SEQ, PLE_DIM), 1.0),
        "norm_w": 1.0 + nrm(ks[7], (DEPTH, D_MODEL), 0.02),
        "w_in": nrm(ks[8], (DEPTH, D_MODEL, IN_COLS), D_MODEL ** -0.5),
        "w_gla_gate": nrm(ks[9], (DEPTH, GLA_GATE_RANK, GLA_KEY), GLA_GATE_RANK ** -0.5),
        "b_gla_gate": nrm(ks[10], (DEPTH, GLA_KEY), 0.1),
        "gla_norm_w": 1.0 + nrm(ks[11], (DEPTH, GLA_DV), 0.02),
        "conv_w": nrm(ks[13], (DEPTH, CONV_WIDTH, CONV_CH), CONV_WIDTH ** -0.5),
        "gdn_a_log": jnp.log(jax.random.uniform(ks[14], (DEPTH, GDN_HEADS), f32, minval=1.0, maxval=16.0)),
        "gdn_dt_bias": dt + jnp.log(-jnp.expm1(-dt)),
        "gdn_norm_w": 1.0 + nrm(ks[15], (DEPTH, GDN_DV), 0.02),
        "w_up_gla": nrm(ks[16], (DEPTH, GLA_VAL, D_MODEL), GLA_VAL ** -0.5),
        "w_up_gdn": nrm(ks[17], (DEPTH, GDN_VAL, D_MODEL), GDN_VAL ** -0.5),
        "w_out": nrm(ks[18], (DEPTH, D_MODEL, D_MODEL), D_MODEL ** -0.5),
        "w_ple_gate": nrm(ks[19], (DEPTH, D_MODEL, D_MODEL), D_MODEL ** -0.5),
        "w_ple": nrm(ks[20], (DEPTH, PLE_DIM, D_MODEL), PLE_DIM ** -0.5),
        "final_norm_w": 1.0 + nrm(ks[21], (D_MODEL,), 0.02),
    }


def reference(x_prompt, x_sample, state_gla, state_gdn, state_conv, p_prompt, p_sample, norm_w, w_in,
              w_gla_gate, b_gla_gate, gla_norm_w, conv_w, gdn_a_log, gdn_dt_bias, gdn_norm_w,
              w_up_gla, w_up_gdn, w_out, w_ple_gate, w_ple, final_norm_w):
    bsz = x_prompt.shape[0]
    dt_ = x_prompt.dtype
    zero_gla = jnp.zeros((DEPTH, bsz, GLA_HEADS, GLA_DK, GLA_DV), dt_)
    zero_gdn = jnp.zeros((DEPTH, bsz, GDN_HEADS, GDN_DK, GDN_DV), dt_)
    zero_conv = jnp.zeros((DEPTH, bsz, CONV_WIDTH - 1, CONV_CH), dt_)
    y_prompt, gla_p, gdn_p, conv_p = trunk(
        x_prompt, p_prompt, zero_gla, zero_gdn, zero_conv, norm_w, w_in, w_gla_gate, b_gla_gate,
        gla_norm_w, conv_w, gdn_a_log, gdn_dt_bias, gdn_norm_w, w_up_gla, w_up_gdn, w_out,
        w_ple_gate, w_ple, final_norm_w)
    y_sample, gla_s, gdn_s, conv_s = trunk(
        x_sample, p_sample, state_gla, state_gdn, state_conv, norm_w, w_in, w_gla_gate, b_gla_gate,
        gla_norm_w, conv_w, gdn_a_log, gdn_dt_bias, gdn_norm_w, w_up_gla, w_up_gdn, w_out,
        w_ple_gate, w_ple, final_norm_w)
    return (y_prompt, y_sample, gla_p, gdn_p, conv_p, gla_s, gdn_s, conv_s)
```

```python
import numpy as np
from contextlib import ExitStack
import concourse.bass as bass
import concourse.mybir as mybir
from concourse.bass_utils import run_bass_kernel_spmd

F32 = mybir.dt.float32
BF16 = mybir.dt.bfloat16
AF = mybir.ActivationFunctionType
ALU = mybir.AluOpType
EPS = 1e-6
BIG = 30000.0
NT = 16
DBG_TILES = None
DBG_PHASE = 9
DBG_NOW = False
DBG_A = 99
DBG_BUDGET = None
DBG_DUMP = False
DBG_NOILV = False
KEEPWARM = 0
INC = 5656


class V:
    def __init__(self, t, name, lo, hi):
        self.t, self.name, self.lo, self.hi = t, name, lo, hi
        self.key = (name, lo, hi)

    def __call__(self, pat=None, **kw):
        ap = self.t[:, self.lo:self.hi]
        return ap.rearrange(pat, **kw) if pat else ap

    def sub(self, a, b):
        return V(self.t, self.name, self.lo + a, self.lo + b)


PSUM_NAMES = {"P0", "P1", "P2", "P3", "P4", "P5", "P6", "PT"}


def _key(k):
    if isinstance(k, V):
        k = k.key
    if isinstance(k, tuple):
        if k[0] in PSUM_NAMES:
            return (k[0], 0, 1 << 30)
        return k
    return (k, 0, 1 << 30)


class Sched:
    ENGS = ('pe', 'act', 'dve', 'pool', 'sp')

    def __init__(self, nc, es):
        self.nc = nc
        self.streams = {e: [] for e in self.ENGS}
        self.ncomp = {e: 0 for e in self.ENGS}
        self.sems = {e: es.enter_context(nc.semaphore("s_" + e)) for e in self.ENGS}
        self.qsems = {'sp': list(range(0, 8)), 'pool': list(range(8, 14)), 'act': list(range(14, 16))}
        self.dma_sems = [es.enter_context(nc.semaphore("s_dma%d" % i)) for i in range(16)]
        self.dma_cnt = [0] * 16
        self.rr = {'sp': 0, 'pool': 0, 'act': 0}
        self.acc = {}
        self.known = {e: {} for e in self.ENGS}
        self.budget = None
        self._lane = None
        self._stack = []

    def _collect(self, r, w):
        deps = {}

        def add(tok):
            if tok is None:
                return
            s, v = tok
            if deps.get(s, 0) < v:
                deps[s] = v
        for k in r:
            n, lo, hi = _key(k)
            for ent in self.acc.get(n, ()):
                if ent[0] < hi and lo < ent[1]:
                    add(ent[2])
        for k in w:
            n, lo, hi = _key(k)
            for ent in self.acc.get(n, ()):
                if ent[0] < hi and lo < ent[1]:
                    add(ent[2])
                    for t in ent[3]:
                        add(t)
        return deps

    def _finish(self, tok, r, w):
        for k in r:
            n, lo, hi = _key(k)
            hit = False
            for ent in self.acc.setdefault(n, []):
                if ent[0] < hi and lo < ent[1]:
                    ent[3].append(tok)
                    hit = True
            if not hit:
                self.acc[n].append([lo, hi, None, [tok]])
        for k in w:
            n, lo, hi = _key(k)
            lst = self.acc.setdefault(n, [])
            new = []
            for ent in lst:
                if ent[0] < hi and lo < ent[1]:
                    if ent[0] < lo:
                        new.append([ent[0], lo, ent[2], list(ent[3])])
                    if hi < ent[1]:
                        new.append([hi, ent[1], ent[2], list(ent[3])])
                else:
                    new.append(ent)
            new.append([lo, hi, tok, []])
            self.acc[n] = new

    def _waits(self, eng, deps):
        kn = self.known[eng]
        waits = []
        for s, v in deps.items():
            if kn.get(s, 0) < v:
                kn[s] = v
                waits.append((s, v))
        return waits

    @staticmethod
    def _excl(r, w):
        r2, w2 = [], list(w)
        for k in r:
            kk = _key(k)
            if kk[0] in PSUM_NAMES:
                w2.append(k)
            else:
                r2.append(k)
        return r2, w2

    def begin_lane(self):
        self._stack.append(self._lane)
        self._lane = []

    def end_lane(self):
        l = self._lane
        self._lane = self._stack.pop()
        return l

    def emit_list(self, lst):
        for kind, eng, fn, r, w in lst:
            (self.op if kind == 'c' else self.dma)(eng, fn, r, w)

    def merge(self, lanes):
        idx = [0] * len(lanes)
        tot = [max(1, len(l)) for l in lanes]
        while any(idx[i] < len(lanes[i]) for i in range(len(lanes))):
            best = min((i for i in range(len(lanes)) if idx[i] < len(lanes[i])), key=lambda i: idx[i] / tot[i])
            kind, eng, fn, r, w = lanes[best][idx[best]]
            idx[best] += 1
            (self.op if kind == 'c' else self.dma)(eng, fn, r, w)

    def op(self, eng, fn, r=(), w=()):
        if self._lane is not None:
            self._lane.append(('c', eng, fn, r, w))
            return
        r, w = self._excl(r, w)
        if self.budget is not None:
            if self.budget <= 0:
                return
            self.budget -= 1
        deps = self._collect(r, w)
        if eng == 'pe':
            deps.pop('pe', None)
        self.ncomp[eng] += 1
        self.streams[eng].append((fn, self._waits(eng, deps), 'c', None))
        self._finish((eng, self.ncomp[eng]), r, w)

    def dma(self, eng, fn, r=(), w=()):
        if self._lane is not None:
            self._lane.append(('d', eng, fn, r, w))
            return
        if self.budget is not None:
            if self.budget <= 0:
                return
            self.budget -= 1
        deps = self._collect(r, w)
        q = self.qsems[eng]
        i = q[self.rr[eng] % len(q)]
        self.rr[eng] += 1
        sname = 'dma%d' % i
        if self.dma_cnt[i] > 0:
            deps[sname] = max(deps.get(sname, 0), 16 * self.dma_cnt[i])
        self.dma_cnt[i] += 1
        self.streams[eng].append((fn, self._waits(eng, deps), 'd', i))
        self._finish((sname, 16 * self.dma_cnt[i]), r, w)

    def _sem(self, s):
        if s.startswith('dma'):
            return self.dma_sems[int(s[3:])]
        return self.sems[s]

    def final_wait_all(self, eng='sp'):
        waits = []
        for i, c in enumerate(self.dma_cnt):
            if c:
                waits.append(('dma%d' % i, 16 * c))
        for e in self.ENGS:
            if self.ncomp[e] and e != eng:
                waits.append((e, self.ncomp[e]))
        self.streams[eng].append((None, waits, 'w', None))

    def emit(self):
        nc = self.nc
        with nc.Block() as block:
            def mk(ename):
                def body(e):
                    for fn, waits, kind, di in self.streams[ename]:
                        for s, v in waits:
                            e.wait_ge(self._sem(s), v)
                        if fn is None:
                            continue
                        ins = fn(e)
                        if kind == 'c':
                            ins.then_inc(self.sems[ename], 1)
                        else:
                            ins.then_inc(self.dma_sems[di], 16)
                return body
            block.tensor(mk('pe'))
            block.scalar(mk('act'))
            block.vector(mk('dve'))
            block.gpsimd(mk('pool'))
            block.sync(mk('sp'))


def build():
    nc = bass.Bass("TRN2", target_bir_lowering=False)
    di = lambda n, s: nc.dram_tensor(n, s, F32, kind="ExternalInput").ap()
    do = lambda n, s: nc.dram_tensor(n, s, F32, kind="ExternalOutput").ap()
    xp, xsm = di("xp", [2048, 1024]), di("xsm", [128, 1024])
    pp, psm = di("pp", [2048, 256]), di("psm", [128, 256])
    sgla, sgdn, sconv = di("sgla", [16, 4, 64, 128]), di("sgdn", [16, 4, 128, 128]), di("sconv", [48, 1536])
    normw, win = di("normw", [1, 1024]), di("win", [1024, INC])
    wgg, bgg, glanw = di("wgg", [16, 256]), di("bgg", [1, 256]), di("glanw", [1, 128])
    convw, alog, dtb, gdnnw = di("convw", [4, 1536]), di("alog", [1, 4]), di("dtb", [1, 4]), di("gdnnw", [1, 128])
    wupa, wupb = di("wupa", [512, 1024]), di("wupb", [512, 1024])
    wout, wpg, wple = di("wout", [1024, 1024]), di("wpg", [1024, 1024]), di("wple", [256, 1024])
    fnw = di("fnw", [1, 1024])
    yp, ysm = do("yp", [2048, 1024]), do("ysm", [128, 1024])
    glap, gdnp, convp = do("glap", [4, 64, 128]), do("gdnp", [4, 128, 128]), do("convp", [3, 1536])
    glas, gdns, convs = do("glas", [16, 4, 64, 128]), do("gdns", [16, 4, 128, 128]), do("convs", [16, 3, 1536])

    if DBG_DUMP:
        dbgf = nc.dram_tensor("dbgf", [24, 128, 1024], F32, kind="ExternalOutput").ap()
        dbgb = nc.dram_tensor("dbgb", [24, 128, 1024], BF16, kind="ExternalOutput").ap()
    with ExitStack() as es:
        S = Sched(nc, es)
        sbt = lambda n, s, d: es.enter_context(nc.sbuf_tensor(n, s, d))
        pst = lambda n, s, d: es.enter_context(nc.psum_tensor(n, s, d))

        def sv(n, w, d):
            return V(sbt(n, [128, w], d), n, 0, w)

        Win = sbt("Win", [128, 8, INC], BF16)
        Wupa, Wupb = sbt("Wupa", [128, 4, 1024], BF16), sbt("Wupb", [128, 4, 1024], BF16)
        Wout, Wpg = sbt("Wout", [128, 8, 1024], BF16), sbt("Wpg", [128, 8, 1024], BF16)
        Wple = sbt("Wple", [128, 2, 1024], BF16)
        fnwbc = sv("fnwbc", 1024, F32)
        identF, onesF = sv("identF", 128, F32), sv("onesF", 128, F32)
        identB, onesB, zB = sv("identB", 128, BF16), sv("onesB", 128, BF16), sv("zB", 128, BF16)
        Lm = [sv("L_P", 128, F32), sv("L_S", 128, F32)]
        Um = [sv("U_P", 128, F32), sv("U_S", 128, F32)]
        NEG2 = [sv("NEG2_P", 256, F32), sv("NEG2_S", 256, F32)]
        segm = sv("segm", 16, F32)
        hm = sv("hm", 2, F32)
        Wg = sbt("Wg", [32, 256], F32)
        gaug = sbt("gaug", [32, 128], F32)
        cw = sbt("cw", [128, 4, 12], F32)
        smallc = sv("smallc", 32, F32)
        cols = sbt("cols", [128, 256], F32)
        CO = lambda a, b: V(cols, "cols", a, b)
        X, PL = sv("X", 1024, F32), sv("PL", 256, F32)
        XT = sv("XT", 1024, BF16)
        PTTs = [sv("PTT", 256, BF16), sv("PTT2", 256, BF16)]
        XPAD, CARRY = sv("XPAD", 704, F32), sv("CARRY", 36, F32)
        SA, SAb = sv("SA", 256, F32), sv("SAb", 256, BF16)
        SB, SBb = sv("SB", 512, F32), sv("SBb", 512, BF16)
        Ft = sbt("F", [128, 4096], F32)
        Ht = sbt("H", [128, 8192], BF16)
        FV = lambda a, b: V(Ft, "F", a, b)
        HV = lambda a, b: V(Ht, "H", a, b)
        acc, rinv, qnT = FV(0, 512), FV(512, 1024), FV(1024, 1536)
        A4, B4, P4 = FV(1536, 2048), FV(2048, 2560), FV(2560, 3072)
        gbc, lbbc, E12, edbc = FV(3072, 3200), FV(3200, 3328), FV(3328, 3584), FV(3584, 3712)
        lsp, eb, enb, eend = FV(1536, 1792), FV(1792, 2048), FV(2048, 2304), FV(2304, 2560)
        oT, rstd, sga, sgb = FV(0, 512), FV(512, 1024), FV(1024, 1536), FV(1536, 2048)
        hres, yt = FV(2048, 3072), FV(3072, 4096)
        SS = [FV(0, 1024), FV(1024, 2048)]
        scv, tokq = FV(0, 1536), FV(2560, 4096)
        STG = [FV(0, 1024), FV(1024, 2048), FV(2048, 3072), FV(3072, 4096)]
        xs, pb = HV(0, 1024), HV(1792, 2048)
        qeT, keT, kend, va, attTm = HV(0, 512), HV(512, 768), HV(768, 1024), HV(1024, 1536), HV(1536, 2048)
        ebm = FV(2560, 3072)
        SSbd, kendmd = HV(0, 1024), HV(1024, 2048)
        sq, knT, qnTb, vTb = HV(2048, 2560), HV(2560, 3072), HV(3072, 3584), HV(3584, 4096)
        qkTm, TinvTb, kbd, vbeta = HV(4096, 4608), HV(4608, 5120), HV(5120, 5632), HV(5632, 6144)
        kendb, negwT, vnewb, wSTb = HV(6144, 6656), HV(6656, 7168), HV(7168, 7680), HV(7680, 8192)
        qeTb = HV(2048, 2560)
        SSbg, kendmg = HV(2048, 3072), HV(3072, 4096)
        osq, sz, gT = HV(2048, 2560), HV(2560, 3072), [HV(3072, 3584), HV(3584, 4096)]
        mg, mT = HV(4096, 5120), HV(5120, 6144)
        Pb = []
        for i in range(7):
            Pb.append(V(pst("P%d" % i, [128, 512], F32), "P%d" % i, 0, 512))
        PT = V(pst("PT", [128, 1024], BF16), "PT", 0, 1024)

        def mm(out, lhsT, rhs, start=True, stop=True, r=(), w=()):
            S.op('pe', lambda e: e.matmul(out, lhsT=lhsT, rhs=rhs, start=start, stop=stop), r, w)

        def tr(out, in_, ident, r=(), w=()):
            S.op('pe', lambda e: e.transpose(out=out, in_=in_, identity=ident), r, w)

        def act(out, in_, func, r=(), w=(), **kw):
            S.op('act', lambda e: e.activation(out=out, in_=in_, func=func, **kw), r, w)

        def tt(out, in0, in1, op, r=(), w=(), eng='dve'):
            S.op(eng, lambda e: e.tensor_tensor(out=out, in0=in0, in1=in1, op=op), r, w)

        def ts(out, in0, s1, op0, r=(), w=(), s2=None, op1=None, eng='dve'):
            if op1 is None:
                S.op(eng, lambda e: e.tensor_scalar(out=out, in0=in0, scalar1=s1, scalar2=None, op0=op0), r, w)
            else:
                S.op(eng, lambda e: e.tensor_scalar(out=out, in0=in0, scalar1=s1, scalar2=s2, op0=op0, op1=op1), r, w)

        def stt(out, in0, sc, in1, op0, op1, r=(), w=(), eng='dve'):
            S.op(eng, lambda e: e.scalar_tensor_tensor(out=out, in0=in0, scalar=sc, in1=in1, op0=op0, op1=op1), r, w)

        def cp(out, in_, r=(), w=(), eng='dve'):
            if eng == 'act':
                S.op('act', lambda e: e.copy(out=out, in_=in_), r, w)
            else:
                S.op(eng, lambda e: e.tensor_copy(out=out, in_=in_), r, w)

        def dma(out, in_, r=(), w=(), eng='sp', slow=False):
            if slow:
                S.dma(eng, lambda e: e.dma_start(out=out, in_=in_, allow_slow_non_contiguous=True), r, w)
            else:
                S.dma(eng, lambda e: e.dma_start(out=out, in_=in_), r, w)

        def dump(slot, v, n, bf=False, psum=False):
            if not DBG_DUMP:
                return
            if psum:
                cp(yt()[:, 0:n], v()[:, 0:n], r=[v], w=[yt])
                dma(dbgf[slot, :, 0:n], yt()[:, 0:n], r=[yt], eng='pool')
            elif bf:
                dma(dbgb[slot, :, 0:n], v()[:, 0:n], r=[v], eng='pool')
            else:
                dma(dbgf[slot, :, 0:n], v()[:, 0:n], r=[v], eng='pool')

        def zerofill(P):
            mm(P(), zB(), XT()[:, 0:512], True, False, r=[zB, XT], w=[P])

        def ms(v, val, eng='pool'):
            S.op(eng, lambda e: e.memset(v(), val), w=[v])

        def asel(v, pattern, base, cm, cmp, fill, view=None):
            ap = v() if view is None else view
            S.op('pool', lambda e: e.affine_select(out=ap, in_=ap, pattern=pattern, compare_op=cmp, fill=fill,
                                                   base=base, channel_multiplier=cm), r=[v], w=[v])
        t0_ = (list(range(NT + 1)) if DBG_TILES is None else list(DBG_TILES))[0]
        dma(X(), xsm[:, :] if t0_ == NT else xp[t0_ * 128:(t0_ + 1) * 128, :], w=[X])
        dma(PL(), psm[:, :] if t0_ == NT else pp[t0_ * 128:(t0_ + 1) * 128, :], w=[PL])
        PRELOADED = [t0_]
        ms(identF, 0.0)
        asel(identF, [[-1, 128]], 0, 1, ALU.not_equal, 1.0)
        ms(onesF, 1.0)
        cp(identB(), identF(), r=[identF], w=[identB])
        ms(onesB, 1.0)
        ms(zB, 0.0)
        ms(Lm[0], 1.0)
        asel(Lm[0], [[1, 128]], 0, -1, ALU.is_ge, 0.0)
        ms(Um[0], 1.0)
        asel(Um[0], [[-1, 128]], 0, 1, ALU.is_gt, 0.0)
        same = FV(0, 128)
        ms(same, 1.0)
        sview = same("p (a r) -> p a r", a=16)
        asel(same, [[-8, 16], [0, 8]], 0, 1, ALU.is_ge, 0.0, view=sview)
        asel(same, [[8, 16], [0, 8]], 7, -1, ALU.is_ge, 0.0, view=sview)
        ms(segm, 1.0)
        asel(segm, [[-8, 16]], 0, 1, ALU.is_ge, 0.0)
        asel(segm, [[8, 16]], 7, -1, ALU.is_ge, 0.0)
        ms(hm, 1.0)
        asel(hm, [[-64, 2]], 63, -1, ALU.is_ge, 0.0)
        hm1 = FV(256, 258)
        ms(hm1, 1.0)
        asel(hm1, [[64, 2]], -64, 1, ALU.is_ge, 0.0)
        S.op('dve', lambda e: e.tensor_copy(out=hm()[:, 1:2], in_=hm1()[:, 0:1]), r=[hm1, hm], w=[hm])
        tt(Lm[1](), Lm[0](), same(), ALU.mult, r=[Lm[0], same], w=[Lm[1]])
        tt(Um[1](), Um[0](), same(), ALU.mult, r=[Um[0], same], w=[Um[1]])
        ind = FV(128, 256)
        for v in range(2):
            ms(ind, 1.0)
            asel(ind, [[1, 128]], 0, -1, ALU.is_gt, 0.0)
            if v == 1:
                tt(ind(), ind(), same(), ALU.mult, r=[ind, same], w=[ind])
            ts(NEG2[v]()[:, 0:128], ind(), 1.0, ALU.subtract, s2=BIG, op1=ALU.mult, r=[ind], w=[NEG2[v].sub(0, 128)])
            ts(NEG2[v]()[:, 128:256], Lm[v](), 1.0, ALU.subtract, s2=BIG, op1=ALU.mult, r=[Lm[v]],
               w=[NEG2[v].sub(128, 256)])
        S.op('pool', lambda e: e.memset(gaug[:], 1.0), w=["gaug"])
        S.op('dve', lambda e: e.memset(Wg[:], 0.0), w=["Wg"])
        dma(Wg[0:16, :], wgg[:, :], r=["Wg"], w=["Wg"])
        dma(Wg[16:17, :], bgg[0:1, :], r=["Wg"], w=["Wg"])
        PR2, PRc = FV(1536, 1664), FV(0, 1536)
        dma(Ft[0:8, 1536:1664], normw[0].rearrange("(k p) -> k p", p=128), w=[PR2])
        dma(Ft[8:9, 1536:1664], glanw[0:1, :], r=[PR2], w=[PR2])
        dma(Ft[9:10, 1536:1664], gdnnw[0:1, :], r=[PR2], w=[PR2])
        dma(Ft[0:4, 0:1536], convw[:, :], w=[PRc])
        tr(Pb[0]()[:, 0:10], Ft[0:10, 1536:1664], identF.t[0:10, 0:10], r=[PR2, identF], w=[Pb[0]])
        cp(smallc()[:, 0:10], Pb[0]()[:, 0:10], r=[Pb[0]], w=[smallc.sub(0, 10)])
        for k in range(12):
            tr(Pb[1]()[:, 4 * k:4 * k + 4], Ft[0:4, k * 128:(k + 1) * 128], identF.t[0:4, 0:4], r=[PRc, identF], w=[Pb[1]])
        cp(cw[:].rearrange("p w k -> p k w"), Pb[1]()[:, 0:48].rearrange("p (k w) -> p k w", w=4), r=[Pb[1]], w=["cw"])
        dma(smallc()[:, 12:16], dtb[0:1, :].partition_broadcast(128), w=[smallc.sub(12, 16)])
        dma(smallc()[:, 16:20], alog[0:1, :].partition_broadcast(128), w=[smallc.sub(16, 20)])
        act(smallc()[:, 16:20], smallc()[:, 16:20], AF.Exp, r=[smallc.sub(16, 20)], w=[smallc.sub(16, 20)])
        ts(smallc()[:, 16:20], smallc()[:, 16:20], -1.0, ALU.mult, r=[smallc.sub(16, 20)], w=[smallc.sub(16, 20)])
        dma(fnwbc(), fnw[0:1, :].partition_broadcast(128), w=[fnwbc])
        ms(SA, 0.0), ms(SAb, 0.0), ms(SB, 0.0), ms(SBb, 0.0), ms(CARRY, 0.0)

        if not DBG_NOW:
            for (c0, c1) in ((0, 3608), (3608, INC)):
                for kc in range(8):
                    dma(Win[:, kc, c0:c1], win[kc * 128:(kc + 1) * 128, c0:c1],
                        w=[("Win", kc * INC + c0, kc * INC + c1)], eng='pool')
            dma(Wupa[:], wupa.rearrange("(h p) n -> p h n", p=128), w=["Wupa"], eng='pool')
            dma(Wupb[:], wupb.rearrange("(h p) n -> p h n", p=128), w=["Wupb"], eng='pool')
            for hk in range(2):
                dma(Wout[:, 4 * hk:4 * hk + 4, :], wout[512 * hk:512 * hk + 512, :].rearrange("(k p) n -> p k n", p=128),
                    w=[("Wout", 4 * hk, 4 * hk + 4)], eng='pool')
            for hk in range(2):
                dma(Wpg[:, 4 * hk:4 * hk + 4, :], wpg[512 * hk:512 * hk + 512, :].rearrange("(k p) n -> p k n", p=128),
                    w=[("Wpg", 4 * hk, 4 * hk + 4)], eng='pool')
            dma(Wple[:], wple.rearrange("(k p) n -> p k n", p=128), w=["Wple"], eng='pool')
        WIN, WUPA, WUPB, WOUT, WPG, WPLE = "Win", "Wupa", "Wupb", "Wout", "Wpg", "Wple"

        MARKS = []

        def phaseA(ti):
            smp = (ti == NT)
            xsrc = xsm[:, :] if smp else xp[ti * 128:(ti + 1) * 128, :]
            psrc = psm[:, :] if smp else pp[ti * 128:(ti + 1) * 128, :]
            PTT = PTTs[ti % 2]
            MARKS.append((ti, 'A', S.ncomp['dve']))
            if ti in PRELOADED:
                PRELOADED.remove(ti)
            else:
                dma(X(), xsrc, w=[X])
                dma(PL(), psrc, w=[PL])
            c0_, c1_, c2_ = CO(0, 1), CO(1, 2), CO(2, 3)
            act(xs(), X(), AF.Square, r=[X], w=[xs, c0_], accum_out=c0_())
            act(c1_(), c0_(), AF.Ln, r=[c0_], w=[c1_], scale=1.0 / 1024, bias=EPS)
            act(c2_(), c1_(), AF.Exp, r=[c1_], w=[c2_], scale=-0.5)
            ts(xs(), X(), c2_(), ALU.mult, r=[X, c2_], w=[xs])
            for k in range(8):
                tr(PT()[:, k * 128:(k + 1) * 128], xs()[:, k * 128:(k + 1) * 128], identB(), r=[xs, identB],
                   w=[PT.sub(k * 128, (k + 1) * 128)])
            tt(XT("p (k t) -> p k t", k=8), PT("p (k t) -> p k t", k=8),
               smallc()[:, 0:8].unsqueeze(2).broadcast_to([128, 8, 128]), ALU.mult, r=[PT, smallc], w=[XT])
            cp(pb(), PL(), r=[PL], w=[pb])
            for k in range(2):
                tr(PT()[:, k * 128:(k + 1) * 128], pb()[:, k * 128:(k + 1) * 128], identB(), r=[pb, identB],
                   w=[PT.sub(k * 128, (k + 1) * 128)])
            cp(PTT(), PT()[:, 0:256], r=[PT.sub(0, 256)], w=[PTT])

            if (not smp) and ti != NT - 1 and not DBG_NOILV:
                XT3a = XT("p (k t) -> p k t", k=8)
                for g in range(3):
                    for c in range(4):
                        c0 = 1552 + (4 * g + c) * 128
                        for kc in range(8):
                            mm(Pb[g]()[:, c * 128:(c + 1) * 128], Win[:, kc, c0:c0 + 128], XT3a[:, kc, :], kc == 0, kc == 7,
                               r=[XT, ("Win", kc * INC + c0, kc * INC + c0 + 128)], w=[Pb[g]])
                for (bk, off_, c0_, n_) in ((4, 0, 256, 512), (6, 0, 768, 256)):
                    for kc in range(8):
                        mm(Pb[bk]()[:, off_:off_ + n_], XT3a[:, kc, :], Win[:, kc, c0_:c0_ + n_], kc == 0, kc == 7,
                           r=[XT, ("Win", kc * INC + c0_, kc * INC + c0_ + n_)], w=[Pb[bk]])
                for kc in range(8):
                    mm(Pb[3]()[0:16, 0:128], Win[:, kc, 1024:1040], XT3a[:, kc, :], kc == 0, kc == 7,
                       r=[XT, ("Win", kc * INC + 1024, kc * INC + 1040)], w=[Pb[3]])
                for c in range(4):
                    for kc in range(8):
                        mm(Pb[5]()[:, c * 128:(c + 1) * 128], Win[:, kc, c * 128:(c + 1) * 128], XT3a[:, kc, :], kc == 0, kc == 7,
                           r=[XT, ("Win", kc * INC + c * 128, kc * INC + (c + 1) * 128)], w=[Pb[5]])

        def tile(ti, nextA=None):
            smp = (ti == NT)
            v = 1 if smp else 0
            L, U, NG = Lm[v], Um[v], NEG2[v]
            xsrc = xsm[:, :] if smp else xp[ti * 128:(ti + 1) * 128, :]
            psrc = psm[:, :] if smp else pp[ti * 128:(ti + 1) * 128, :]
            ydst = ysm[:, :] if smp else yp[ti * 128:(ti + 1) * 128, :]
            P0, P1, P2, P3, P4p, P5, P6 = Pb
            W = lambda c0, n: Win[:, :, c0:c0 + n]

            PTT = PTTs[ti % 2]
            XT3 = XT("p (k t) -> p k t", k=8)
            dump(0, XT, 1024, bf=True)

            def proj_fm(Pv, off, c0, M=128):
                for kc in range(8):
                    mm(Pv()[0:M, off:off + 128], Win[:, kc, c0:c0 + M], XT3[:, kc, :], kc == 0, kc == 7,
                       r=[XT, ("Win", kc * INC + c0, kc * INC + c0 + M)], w=[Pv.sub(off, off + 128)])

            def proj_tm(Pv, off, c0, n):
                for kc in range(8):
                    mm(Pv()[:, off:off + n], XT3[:, kc, :], Win[:, kc, c0:c0 + n], kc == 0, kc == 7,
                       r=[XT, ("Win", kc * INC + c0, kc * INC + c0 + n)], w=[Pv.sub(off, off + n)])

            MARKS.append((ti, 'B', S.ncomp['dve']))
            ilv = (not smp) and ti != NT - 1 and not DBG_NOILV
            G0, G1, G2 = P3, (P3 if ilv else P4p), P5
            if ilv:
                S.begin_lane()
            if not ilv:
                proj_fm(G0, 0, 1024, M=16)
            S.op('dve', lambda e: e.tensor_copy(out=gaug[0:16, :], in_=G0()[0:16, 0:128]), r=[G0.sub(0, 128)], w=["gaug"])
            mm(G0()[:, 128:384], gaug[0:17, :], Wg[0:17, :], r=["gaug", "Wg"], w=[G0.sub(128, 384)])
            act(lsp(), G0()[:, 128:384], AF.Exp, r=[G0.sub(128, 384)], w=[lsp], scale=-1.0)
            act(lsp(), lsp(), AF.Ln, r=[lsp], w=[lsp], bias=1.0)
            for dc in range(2):
                mm(G1()[:, dc * 128:(dc + 1) * 128], lsp()[:, dc * 128:(dc + 1) * 128], L(), r=[lsp, L],
                   w=[G1.sub(dc * 128, (dc + 1) * 128)])
            mm(G1()[:, 256:512], U(), lsp(), r=[lsp, U], w=[G1.sub(256, 512)])
            act(eb(), G1()[:, 0:256], AF.Exp, r=[G1.sub(0, 256)], w=[eb], scale=-1.0 / 16)
            act(enb(), G1()[:, 0:256], AF.Exp, r=[G1.sub(0, 256)], w=[enb], scale=1.0 / 16)
            act(eend(), G1()[:, 256:512], AF.Exp, r=[G1.sub(256, 512)], w=[eend], scale=-1.0 / 16)
            dump(1, lsp, 256); dump(2, eb, 256); dump(3, eend, 256)
            for c in range(4):
                if not ilv:
                    proj_fm(G2, c * 128, c * 128)
            dump(4, G2, 512, psum=True)
            tt(ebm("p (c h t) -> p c h t", c=2, h=2), eb("p (c t) -> p c t", c=2).unsqueeze(2).broadcast_to([128, 2, 2, 128]),
               hm().unsqueeze(1).unsqueeze(3).broadcast_to([128, 2, 2, 128]), ALU.mult, r=[eb, hm], w=[ebm])
            for hh in range(2):
                stt(qeT("p (c h t) -> p c h t", c=2, h=2)[:, :, hh, :],
                    G2()[:, 0:256].rearrange("p (c t) -> p c t", c=2), 0.125,
                    ebm("p (c h t) -> p c h t", c=2, h=2)[:, :, hh, :], ALU.mult, ALU.mult, r=[G2.sub(0, 256), ebm], w=[qeT])
            tt(keT(), G2()[:, 256:512], enb(), ALU.mult, r=[G2.sub(256, 512), enb], w=[keT])
            KV1, KV2 = (P4p, P6) if ilv else (P3, P4p)
            if not ilv:
                proj_tm(KV1, 0, 256, 512)
                proj_tm(KV2, 0, 768, 256)
            tt(kend(), KV1()[:, 0:256], eend(), ALU.mult, r=[KV1.sub(0, 256), eend], w=[kend])
            cp(va()[:, 0:256], KV1()[:, 256:512], r=[KV1.sub(256, 512)], w=[va.sub(0, 256)], eng='act')
            cp(va()[:, 256:512], KV2()[:, 0:256], r=[KV2.sub(0, 256)], w=[va.sub(256, 512)], eng='act')
            for h in range(4):
                dc, hp = h // 2, 64 * (h % 2)
                mm(P5()[:, h * 128:(h + 1) * 128], keT()[:, dc * 128:(dc + 1) * 128],
                   qeT()[:, h * 128:(h + 1) * 128], r=[keT, qeT], w=[P5.sub(h * 128, (h + 1) * 128)])
            tt(attTm("p (h c) -> p h c", h=4), P5("p (h c) -> p h c", h=4),
               L().unsqueeze(1).broadcast_to([128, 4, 128]), ALU.mult, r=[P5, L], w=[attTm])
            dump(5, qeT, 512, bf=True); dump(6, keT, 256, bf=True); dump(7, kend, 256, bf=True); dump(8, va, 512, bf=True); dump(9, attTm, 512, bf=True)
            zerofill(P6)
            if not smp:
                for h in range(4):
                    dc, hp = h // 2, 64 * (h % 2)
                    mm(P6()[:, h * 128:(h + 1) * 128], va()[:, h * 128:(h + 1) * 128], attTm()[:, h * 128:(h + 1) * 128],
                       False, False, r=[va, attTm], w=[P6])
                    mm(P6()[:, h * 128:(h + 1) * 128], SAb()[:, dc * 128:(dc + 1) * 128],
                       qeT()[:, h * 128:(h + 1) * 128], False, h == 3, r=[SAb, qeT], w=[P6])
                for h in range(4):
                    dc, hp = h // 2, 64 * (h % 2)
                    mm(P4p()[hp:hp + 64, 256 + dc * 128:256 + (dc + 1) * 128], kend()[:, h * 64:(h + 1) * 64],
                       va()[:, h * 128:(h + 1) * 128], r=[kend, va], w=[P4p.sub(256, 512)])
                for dc in range(2):
                    stt(SA()[:, dc * 128:(dc + 1) * 128], SA()[:, dc * 128:(dc + 1) * 128],
                        eb()[:, dc * 128 + 127:dc * 128 + 128], P4p()[:, 256 + dc * 128:256 + (dc + 1) * 128],
                        ALU.mult, ALU.add, r=[SA, eb, P4p.sub(256, 512)], w=[SA])
                cp(SAb(), SA(), r=[SA], w=[SAb], eng='act')
                if ti == NT - 1:
                    dma(glap.rearrange("(c h) d v -> (h d) c v", c=2), SA("p (c v) -> p c v", c=2), r=[SA], eng='pool')
            else:
                for h in range(4):
                    mm(P6()[:, h * 128:(h + 1) * 128], va()[:, h * 128:(h + 1) * 128], attTm()[:, h * 128:(h + 1) * 128],
                       False, False, r=[va, attTm], w=[P6])
                eb4 = eb("p (c s t) -> p c s t", c=2, s=16)
                SSg = [FV(0, 1024), FV(3072, 4096)]
                for dc in range(2):
                    for hf in range(2):
                        SSx = SSg[hf]
                        SS3 = SSx("p (s v) -> p s v", s=8)
                        dma(SS3, sgla[8 * hf:8 * hf + 8, 2 * dc:2 * dc + 2, :, :].rearrange("s h d v -> (h d) s v"),
                            w=[SSx])
                        SSbg_ = (SSbg, HV(4096, 5120))[(2 * dc + hf) % 2]
                        cp(SSbg_(), SSx(), r=[SSx], w=[SSbg_], eng=('act', 'dve')[(2 * dc + hf) % 2])
                        SSb3 = SSbg_("p (s v) -> p s v", s=8)
                        for hh in range(2):
                            h, hp = 2 * dc + hh, 64 * hh
                            for j in range(8):
                                s = 8 * hf + j
                                mm(P6()[:, h * 128 + 8 * s:h * 128 + 8 * s + 8], SSb3[:, j, :],
                                   qeT()[:, h * 128 + 8 * s:h * 128 + 8 * s + 8], False,
                                   (dc == 1 and hf == 1 and hh == 1 and j == 7), r=[SSbg_, qeT], w=[P6])
                        km3 = kendmg("p (s d) -> p s d", s=8)
                        tt(km3, kend()[:, dc * 128:(dc + 1) * 128].unsqueeze(1).broadcast_to([128, 8, 128]),
                           segm()[:, 8 * hf:8 * hf + 8].unsqueeze(2).broadcast_to([128, 8, 128]), ALU.mult,
                           r=[kend, segm], w=[kendmg])
                        for g in range(2):
                            Pq = (G0, G1)[g]
                            for jj in range(4):
                                j = 4 * g + jj
                                for hh in range(2):
                                    h, hp = 2 * dc + hh, 64 * hh
                                    mm(Pq()[hp:hp + 64, jj * 128:(jj + 1) * 128], km3[:, j, hp:hp + 64],
                                       va()[:, h * 128:(h + 1) * 128], r=[kendmg, va], w=[Pq])
                            sl = SS3[:, 4 * g:4 * g + 4, :]
                            s0 = 8 * hf + 4 * g
                            tt(sl, sl, eb4[:, dc, s0:s0 + 4, 7:8].broadcast_to([128, 4, 128]), ALU.mult,
                               r=[SSx, eb], w=[SSx])
                            tt(sl, sl, Pq("p (s v) -> p s v", s=4), ALU.add, r=[SSx, Pq], w=[SSx])
                        dma(glas[8 * hf:8 * hf + 8, 2 * dc:2 * dc + 2, :, :].rearrange("s h d v -> (h d) s v"), SS3,
                            r=[SSx], eng='pool')

            dump(10, P6, 512, psum=True)
            MARKS.append((ti, 'C', S.ncomp['dve']))
            laneB = S.end_lane() if ilv else None
            if ilv:
                S.begin_lane()
            if smp or ti == NT - 1:
                for c3 in range(3):
                    Pc = (P3, P4p, P3)[c3]
                    proj_tm(Pc, 0, 1552 + c3 * 512, 512)
                    cp(tokq()[:, c3 * 512:(c3 + 1) * 512], Pc(), r=[Pc], w=[tokq.sub(c3 * 512, (c3 + 1) * 512)], eng='act')
                if smp:
                    for s in range(16):
                        dma(convs[s], tokq.t[8 * s + 5:8 * s + 8, 2560:4096], r=[tokq], eng='pool')
                else:
                    dma(convp[:, :], tokq.t[125:128, 2560:4096], r=[tokq], eng='pool')
            if smp:
                dma(scv.t[0:48, 0:1536], sconv[:, :], w=[scv])
                for k in range(12):
                    Px, c = (P5, k) if k < 8 else (P4p, k - 8)
                    tr(Px()[:, c * 48:(c + 1) * 48], scv.t[0:48, k * 128:(k + 1) * 128], identF.t[0:48, 0:48],
                       r=[scv, identF], w=[Px])
            PGT = P0 if ilv else P2
            S.begin_lane()
            proj_tm(PGT, 0, 3088, 8)
            t1, gcol, l2, beta = CO(8, 12), CO(12, 16), CO(16, 20), CO(20, 24)
            negdec, bed, Ecol, gm = CO(24, 28), CO(28, 32), CO(32, 104), CO(104, 168)
            tt(t1(), PGT()[:, 0:4], smallc()[:, 12:16], ALU.add, r=[PGT.sub(0, 8), smallc], w=[t1])
            act(t1(), t1(), AF.Exp, r=[t1], w=[t1])
            act(t1(), t1(), AF.Ln, r=[t1], w=[t1], bias=1.0)
            tt(gcol(), t1(), smallc()[:, 16:20], ALU.mult, r=[t1, smallc], w=[gcol])
            act(l2(), PGT()[:, 4:8], AF.Exp, r=[PGT.sub(0, 8)], w=[l2], scale=-1.0)
            act(l2(), l2(), AF.Ln, r=[l2], w=[l2], bias=1.0)
            act(beta(), l2(), AF.Exp, r=[l2], w=[beta], scale=-1.0)
            mm(PGT()[:, 8:12], L(), gcol(), r=[L, gcol], w=[PGT.sub(8, 12)])
            mm(PGT()[:, 12:16], U(), gcol(), r=[U, gcol], w=[PGT.sub(12, 16)])
            if not smp:
                mm(PGT()[:, 16:20], onesF(), gcol(), r=[onesF, gcol], w=[PGT.sub(16, 20)])
                ne = 12
            else:
                tt(gm("p (s h) -> p s h", s=16), gcol().unsqueeze(1).broadcast_to([128, 16, 4]),
                   segm().unsqueeze(2).broadcast_to([128, 16, 4]), ALU.mult, r=[gcol, segm], w=[gm])
                mm(PGT()[:, 16:80], onesF(), gm(), r=[onesF, gm], w=[PGT.sub(16, 80)])
                ne = 72
            act(Ecol()[:, 0:ne], PGT()[:, 8:8 + ne], AF.Exp, r=[PGT.sub(8, 8 + ne)], w=[Ecol])
            edec, eend4, etot = Ecol()[:, 0:4], Ecol()[:, 4:8], Ecol()[:, 8:ne]
            ts(negdec(), PGT()[:, 8:12], -1.0, ALU.mult, r=[PGT.sub(8, 12)], w=[negdec])
            tt(bed(), beta(), edec, ALU.mult, r=[beta, Ecol], w=[bed])
            for g in range(3):
                Pg = (P0, P1, P2)[g]
                if ilv:
                    continue
                for c in range(4):
                    proj_fm(Pg, c * 128, 1552 + (4 * g + c) * 128)
            laneGate = S.end_lane()
            if not ilv:
                S.emit_list(laneGate)
            stages = {}
            accs, rinvs = [FV(0, 512), FV(3072, 3584)], [FV(512, 1024), FV(3584, 4096)]
            for g in range(3):
                Pg = (P0, P1, P2)[g]
                acc, rinv = accs[g % 2], rinvs[g % 2]
                S.begin_lane()
                if not smp:
                    xp3 = XPAD()[:, 0:524].rearrange("p (c t) -> p c t", c=4)
                    car = CARRY("p (k w) -> p k w", k=12)[:, 4 * g:4 * g + 4, :]
                    cp(xp3[:, :, 0:3], car, r=[CARRY], w=[XPAD])
                    cp(xp3[:, :, 3:131], Pg("p (c t) -> p c t", c=4), r=[Pg], w=[XPAD], eng='act')
                    cp(car, xp3[:, :, 128:131], r=[XPAD], w=[CARRY])
                else:
                    xp4 = XPAD("p (c s t) -> p c s t", c=4, s=16)
                    Px, po = ((P5, 0), (P5, 192), (P4p, 0))[g]
                    cp(xp4[:, :, :, 0:3], Px()[:, po:po + 192].rearrange("p (c s w) -> p c s w", c=4, s=16), r=[Px], w=[XPAD])
                    cp(xp4[:, :, :, 3:11], Pg("p (c s t) -> p c s t", c=4, s=16), r=[Pg], w=[XPAD], eng='act')
                stages[g, 0] = S.end_lane()
                S.begin_lane()
                tmp = rinv
                for w_ in range(4):
                    cwv = cw[:, w_, 4 * g:4 * g + 4]
                    if not smp:
                        a3 = acc("p (c t) -> p c t", c=4)
                        t3 = tmp("p (c t) -> p c t", c=4)
                        xw = xp3[:, :, w_:w_ + 128]
                        bcw = cwv.unsqueeze(2).broadcast_to([128, 4, 128])
                    else:
                        a3 = acc("p (c s t) -> p c s t", c=4, s=16)
                        t3 = tmp("p (c s t) -> p c s t", c=4, s=16)
                        xw = xp4[:, :, :, w_:w_ + 8]
                        bcw = cwv.unsqueeze(2).unsqueeze(3).broadcast_to([128, 4, 16, 8])
                    if w_ == 0:
                        tt(a3, xw, bcw, ALU.mult, r=[XPAD, "cw"], w=[acc])
                    else:
                        tt(t3, xw, bcw, ALU.mult, r=[XPAD, "cw"], w=[tmp])
                        tt(a3, a3, t3, ALU.add, r=[acc, tmp], w=[acc])
                stages[g, 1] = S.end_lane()
                S.begin_lane()
                if g == 2:
                    act(vTb(), acc(), AF.Silu, r=[acc], w=[vTb])
                    stages[g, 2] = S.end_lane()
                    S.begin_lane()
                else:
                    act(acc(), acc(), AF.Silu, r=[acc], w=[acc])
                    tt(sq(), acc(), acc(), ALU.mult, r=[acc], w=[sq])
                    mm(Pg(), onesB(), sq(), r=[onesB, sq], w=[Pg])
                    stages[g, 2] = S.end_lane()
                    S.begin_lane()
                    act(rinv(), Pg(), AF.Ln, r=[Pg], w=[rinv], bias=EPS)
                    act(rinv(), rinv(), AF.Exp, r=[rinv], w=[rinv], scale=-0.5)
                    if g == 0:
                        stt(qnT(), acc(), 128.0 ** -0.5, rinv(), ALU.mult, ALU.mult, r=[acc, rinv], w=[qnT])
                    else:
                        tt(knT(), acc(), rinv(), ALU.mult, r=[acc, rinv], w=[knT])
                stages[g, 3] = S.end_lane()
            pre_keys = [(0, 0), (0, 1)] if ilv else []
            for key in [(0, 0), (0, 1), (1, 0), (0, 2), (1, 1), (0, 3), (2, 0), (1, 2), (2, 1), (1, 3), (2, 2), (2, 3)]:
                if key == (2, 0) and ilv:
                    S.emit_list(laneGate)
                if key not in pre_keys:
                    S.emit_list(stages[key])
            acc, rinv = accs[0], rinvs[0]
            cp(qnTb(), qnT(), r=[qnT], w=[qnTb], eng='act')
            if ilv:
                laneC = S.end_lane()
                for key in pre_keys:
                    S.emit_list(stages[key])
                S.merge([laneB, laneC])
            dump(11, qnT, 512); dump(12, knT, 512, bf=True); dump(13, vTb, 512, bf=True)
            for h in range(4):
                tr(PT()[:, h * 128:(h + 1) * 128], knT()[:, h * 128:(h + 1) * 128], identB(), r=[knT, identB],
                   w=[PT.sub(h * 128, (h + 1) * 128)])
                tr(PT()[:, 512 + h * 128:512 + (h + 1) * 128], vTb()[:, h * 128:(h + 1) * 128], identB(),
                   r=[vTb, identB], w=[PT.sub(512 + h * 128, 512 + (h + 1) * 128)])
            MARKS.append((ti, 'C8', S.ncomp['dve']))
            gbcs, lbbcs = [FV(3072, 3200), FV(3712, 3840)], [FV(3200, 3328), FV(3840, 3968)]
            edbcs, E12s = [FV(3584, 3712), FV(3968, 4096)], [FV(512, 768), FV(768, 1024)]
            hst = {}
            for h in range(4):
                RB = (P0, P1)[h % 2]
                KQ = (P2, P4p)[h % 2]
                gbc, lbbc, edbc, E12 = gbcs[h % 2], lbbcs[h % 2], edbcs[h % 2], E12s[h % 2]
                hs = slice(h * 128, (h + 1) * 128)
                S.begin_lane()
                cp(gbc(), gcol()[:, h:h + 1].broadcast_to([128, 128]), r=[gcol], w=[gbc])
                ts(lbbc(), l2()[:, h:h + 1].broadcast_to([128, 128]), -1.0, ALU.mult, r=[l2], w=[lbbc])
                mm(RB()[:, 0:128], gbc(), L(), True, True, r=[gbc, L], w=[RB])
                mm(RB()[:, 128:256], lbbc(), identF(), True, True, r=[lbbc, identF], w=[RB])
                mm(KQ()[:, 0:128], knT()[:, hs], knT()[:, hs], r=[knT], w=[KQ])
                mm(KQ()[:, 128:256], knT()[:, hs], qnTb()[:, hs], r=[knT, qnTb], w=[KQ])
                hst[h, 0] = S.end_lane()
                S.begin_lane()
                tt(E12()[:, 128:256], RB()[:, 0:128], NG()[:, 128:256], ALU.add, r=[RB, NG], w=[E12])
                tt(E12()[:, 0:128], RB()[:, 128:256], NG()[:, 0:128], ALU.add, r=[RB, NG], w=[E12])
                tt(E12()[:, 0:128], E12()[:, 0:128], RB()[:, 0:128], ALU.add, r=[RB, E12], w=[E12])
                act(E12(), E12(), AF.Exp, r=[E12, negdec], w=[E12], bias=negdec()[:, h:h + 1])
                act(edbc(), RB()[:, 0:128], AF.Exp, r=[RB], w=[edbc])
                tt(qeTb()[:, hs], qnT()[:, hs], edbc(), ALU.mult, r=[qnT, edbc], w=[qeTb.sub(h * 128, (h + 1) * 128)])
                hst[h, 1] = S.end_lane()
                S.begin_lane()
                tt(B4()[:, hs], KQ()[:, 0:128], E12()[:, 0:128], ALU.mult, r=[KQ, E12],
                   w=[B4.sub(h * 128, (h + 1) * 128)])
                tt(qkTm()[:, hs], KQ()[:, 128:256], E12()[:, 128:256], ALU.mult,
                   r=[KQ, E12], w=[qkTm.sub(h * 128, (h + 1) * 128)])
                tr(P3()[:, hs], B4()[:, hs], identF(), r=[B4.sub(h * 128, (h + 1) * 128), identF],
                   w=[P3.sub(h * 128, (h + 1) * 128)])
                hst[h, 2] = S.end_lane()
            for key in [(0, 0), (1, 0), (0, 1), (0, 2), (2, 0), (1, 1), (1, 2), (3, 0), (2, 1), (2, 2), (3, 1), (3, 2)]:
                S.emit_list(hst[key])
            h3 = "p (h c) -> p h c"

            class VB(V):
                def __call__(self):
                    return self.t[:, self.lo:self.hi].bitcast(BF16)

            def FB(lo):
                return [VB(Ft, "F", lo + 128 * q, lo + 128 * (q + 1)) for q in range(2)]
            Ab_, Bb_, Pb_, MTh_, MTl_, Rb_, Y0T_ = FB(1536), FB(1792), FB(2560), FB(2816), FB(512), FB(768), FB(1024)
            IA = FV(0, 512)
            tt(IA(h3, h=4), identF().unsqueeze(1).broadcast_to([128, 4, 128]), P3(h3, h=4), ALU.add, r=[identF, P3], w=[IA])
            for q in range(2):
                qs = slice(q * 256, (q + 1) * 256)
                cp(Ab_[q](), P3()[:, qs], r=[P3], w=[Ab_[q]], eng='act')
                cp(Bb_[q](), B4()[:, qs], r=[B4.sub(q * 256, (q + 1) * 256)], w=[Bb_[q]], eng='act')
                cp(MTh_[q](), IA()[:, qs], r=[IA.sub(q * 256, (q + 1) * 256)], w=[MTh_[q]], eng='act')
                tt(MTl_[q](), IA()[:, qs], MTh_[q](), ALU.subtract, r=[IA.sub(q * 256, (q + 1) * 256), MTh_[q]], w=[MTl_[q]])
                tt(Pb_[q]().rearrange("p (h c) -> p h c", h=2), identF().unsqueeze(1).broadcast_to([128, 2, 128]),
                   B4()[:, qs].rearrange("p (h c) -> p h c", h=2), ALU.subtract,
                   r=[identF, B4.sub(q * 256, (q + 1) * 256)], w=[Pb_[q]])
            MARKS.append((ti, 'C9', S.ncomp['dve']))
            nst = 2 if smp else 6
            half_lanes = []
            for hf2 in range(2):
                QA, QB, QZ = ((P0, P1, P2), (P3, P4p, P5))[hf2]
                Ab, Bb, Yb, MTh, MTl, Rb, Y0T = (Ab_[hf2], Bb_[hf2], Pb_[hf2], MTh_[hf2], MTl_[hf2], Rb_[hf2], Y0T_[hf2])
                ea, eb_ = (('act', 'dve'), ('dve', 'act'))[hf2]
                S.begin_lane()
                for k in range(1, nst + 1):
                    last = (k == nst)
                    if not last:
                        for hh in range(2):
                            hs = slice(hh * 128, (hh + 1) * 128)
                            mm(QB()[:, hs], Ab()[:, hs], Bb()[:, hs], r=[Ab, Bb], w=[QB])
                    for hh in range(2):
                        hs = slice(hh * 128, (hh + 1) * 128)
                        mm(QA()[:, hs], Bb()[:, hs], Ab()[:, hs], r=[Ab, Bb], w=[QA])
                    cp(Ab(), QA()[:, 0:256], r=[QA], w=[Ab], eng=ea)
                    if not last:
                        cp(Bb(), QB()[:, 0:256], r=[QB], w=[Bb], eng=eb_)
                    for hh in range(2):
                        hs = slice(hh * 128, (hh + 1) * 128)
                        mm(QZ()[:, hs], Ab()[:, hs], Yb()[:, hs], r=[Ab, Yb], w=[QZ])
                    tt(Yb(), Yb(), QZ()[:, 0:256], ALU.add, r=[Yb, QZ], w=[Yb])
                for hh in range(2):
                    hs = slice(hh * 128, (hh + 1) * 128)
                    mm(QA()[:, hs], MTh()[:, hs], Yb()[:, hs], True, False, r=[MTh, Yb], w=[QA])
                    mm(QA()[:, hs], MTl()[:, hs], Yb()[:, hs], False, True, r=[MTl, Yb], w=[QA])
                    mm(QB()[:, hs], Yb()[:, hs], identB(), r=[Yb, identB], w=[QB])
                tt(Rb().rearrange("p (h c) -> p h c", h=2), identF().unsqueeze(1).broadcast_to([128, 2, 128]),
                   QA()[:, 0:256].rearrange("p (h c) -> p h c", h=2), ALU.subtract, r=[identF, QA], w=[Rb])
                cp(Y0T(), QB()[:, 0:256], r=[QB], w=[Y0T], eng=ea)
                for hh in range(2):
                    hs = slice(hh * 128, (hh + 1) * 128)
                    mm(QZ()[:, hs], Y0T()[:, hs], Rb()[:, hs], r=[Y0T, Rb], w=[QZ])
                tt(TinvTb()[:, hf2 * 256:(hf2 + 1) * 256], Yb(), QZ()[:, 0:256], ALU.add, r=[Yb, QZ],
                   w=[TinvTb.sub(hf2 * 256, (hf2 + 1) * 256)])
                half_lanes.append(S.end_lane())
            S.merge(half_lanes)
            dump(14, P4, 512); dump(15, V(cols, 'cols', 0, 256), 256); dump(16, qkTm, 512, bf=True)
            MARKS.append((ti, 'C10', S.ncomp['dve']))
            bc4 = lambda ap: ap.unsqueeze(2).broadcast_to([128, 4, 128])
            tt(kbd(h3, h=4), PT()[:, 0:512].rearrange(h3, h=4), bc4(bed()), ALU.mult, r=[PT.sub(0, 512), bed], w=[kbd])
            tt(vbeta(h3, h=4), PT()[:, 512:1024].rearrange(h3, h=4), bc4(beta()), ALU.mult, r=[PT.sub(512, 1024), beta],
               w=[vbeta])
            tt(kendb(h3, h=4), PT()[:, 0:512].rearrange(h3, h=4), bc4(eend4), ALU.mult, r=[PT.sub(0, 512), Ecol],
               w=[kendb])
            for h in range(4):
                hs = slice(h * 128, (h + 1) * 128)
                mm(P3()[:, hs], kbd()[:, hs], TinvTb()[:, hs], r=[kbd, TinvTb], w=[P3.sub(h * 128, (h + 1) * 128)])
            S.op('act', lambda e: e.mul(out=negwT(), in_=P3(), mul=-1.0), r=[P3], w=[negwT])
            if not smp:
                for h in range(4):
                    hs = slice(h * 128, (h + 1) * 128)
                    mm(P4p()[:, hs], TinvTb()[:, hs], vbeta()[:, hs], True, False, r=[TinvTb, vbeta], w=[P4p])
                    mm(P4p()[:, hs], negwT()[:, hs], SBb()[:, hs], False, True, r=[negwT, SBb], w=[P4p])
                cp(vnewb(), P4p(), r=[P4p], w=[vnewb], eng='act')
                for h in range(4):
                    hs = slice(h * 128, (h + 1) * 128)
                    mm(P5()[:, hs], vnewb()[:, hs], qkTm()[:, hs], True, False, r=[vnewb, qkTm], w=[P5])
                    mm(P5()[:, hs], SBb()[:, hs], qeTb()[:, hs], False, True, r=[SBb, qeTb], w=[P5])
                for h in range(4):
                    hs = slice(h * 128, (h + 1) * 128)
                    mm(P0()[:, hs], kendb()[:, hs], vnewb()[:, hs], r=[kendb, vnewb], w=[P0.sub(h * 128, (h + 1) * 128)])
                tt(SB(h3, h=4), SB(h3, h=4), bc4(etot), ALU.mult, r=[SB, Ecol], w=[SB])
                tt(SB(), SB(), P0(), ALU.add, r=[SB, P0], w=[SB])
                cp(SBb(), SB(), r=[SB], w=[SBb], eng='act')
                if ti == NT - 1:
                    dma(gdnp.rearrange("h d v -> d h v"), SB(h3, h=4), r=[SB], eng='pool')
            else:
                zerofill(P0)
                zerofill(P5)
                SS4 = [FV(0, 1024), FV(1024, 2048), FV(2048, 3072), FV(3072, 4096)]
                SSbds = [SSbd, HV(2560, 3584)]
                ssi = 0
                for h in range(4):
                    for hf in range(2):
                        SSx = SS4[ssi % 4]
                        SSbd_ = SSbds[ssi % 2]
                        SSb3 = SSbd_("p (s v) -> p s v", s=8)
                        SS3 = SSx("p (s v) -> p s v", s=8)
                        dma(SS3, sgdn[8 * hf:8 * hf + 8, h, :, :].rearrange("s d v -> d s v"), w=[SSx])
                        cp(SSbd_(), SSx(), r=[SSx], w=[SSbd_], eng=('act', 'dve')[ssi % 2])
                        ssi += 1
                        for j in range(8):
                            s = 8 * hf + j
                            cs_ = slice(h * 128 + 8 * s, h * 128 + 8 * s + 8)
                            mm(P0()[:, cs_], SSb3[:, j, :], negwT()[:, cs_], False, (h == 3 and hf == 1 and j == 7),
                               r=[SSbd_, negwT], w=[P0])
                            mm(P5()[:, cs_], SSb3[:, j, :], qeTb()[:, cs_], False, False, r=[SSbd_, qeTb], w=[P5])
                cp(wSTb(), P0(), r=[P0], w=[wSTb], eng='act')
                for h in range(4):
                    hs = slice(h * 128, (h + 1) * 128)
                    mm(P4p()[:, hs], TinvTb()[:, hs], vbeta()[:, hs], True, False, r=[TinvTb, vbeta], w=[P4p])
                    mm(P4p()[:, hs], wSTb()[:, hs], identB(), False, True, r=[wSTb, identB], w=[P4p])
                cp(vnewb(), P4p(), r=[P4p], w=[vnewb], eng='act')
                for h in range(4):
                    hs = slice(h * 128, (h + 1) * 128)
                    mm(P5()[:, hs], vnewb()[:, hs], qkTm()[:, hs], False, h == 3, r=[vnewb, qkTm], w=[P5])
                et3 = Ecol()[:, 8:72].rearrange("p (s h) -> p s h", s=16)
                km3 = kendmd("p (s d) -> p s d", s=8)
                for h in range(4):
                    hs = slice(h * 128, (h + 1) * 128)
                    for hf in range(2):
                        SSx = SS4[ssi % 4]
                        ssi += 1
                        SS3 = SSx("p (s v) -> p s v", s=8)
                        dma(SS3, sgdn[8 * hf:8 * hf + 8, h, :, :].rearrange("s d v -> d s v"), w=[SSx])
                        tt(km3, kendb()[:, hs].unsqueeze(1).broadcast_to([128, 8, 128]),
                           segm()[:, 8 * hf:8 * hf + 8].unsqueeze(2).broadcast_to([128, 8, 128]), ALU.mult,
                           r=[kendb, segm], w=[kendmd])
                        for g in range(2):
                            Pq = ((P0, P1), (P2, P3))[hf][g]
                            for jj in range(4):
                                mm(Pq()[:, jj * 128:(jj + 1) * 128], km3[:, 4 * g + jj, :], vnewb()[:, hs],
                                   r=[kendmd, vnewb], w=[Pq.sub(jj * 128, (jj + 1) * 128)])
                            sl = SS3[:, 4 * g:4 * g + 4, :]
                            s0 = 8 * hf + 4 * g
                            tt(sl, sl, et3[:, s0:s0 + 4, h:h + 1].broadcast_to([128, 4, 128]), ALU.mult,
                               r=[SSx, Ecol], w=[SSx])
                            tt(sl, sl, Pq("p (s v) -> p s v", s=4), ALU.add, r=[SSx, Pq], w=[SSx])
                        dma(gdns[8 * hf:8 * hf + 8, h, :, :].rearrange("s d v -> d s v"), SS3, r=[SSx], eng='pool')

            dump(17, vnewb, 512, bf=True); dump(18, P5, 512, psum=True)
            if DBG_PHASE < 4:
                return
            MARKS.append((ti, 'D', S.ncomp['dve']))
            oTs, rstds = [FV(0, 512), FV(1024, 1536)], [FV(512, 1024), FV(1536, 2048)]
            osqs, szs = [HV(2048, 2560), HV(6144, 6656)], [HV(2560, 3072), HV(6656, 7168)]
            Pos, Pss, Pzs, zc0s = (P6, P5), (P1, P2), (P3, P4p), (1040, 3096)
            for br in range(2):
                act(osqs[br](), Pos[br](), AF.Square, r=[Pos[br]], w=[osqs[br]])
            for br in range(2):
                mm(Pss[br](), onesB(), osqs[br](), r=[onesB, osqs[br]], w=[Pss[br]])
            for br in range(2):
                for c in range(4):
                    proj_fm(Pzs[br], c * 128, zc0s[br] + c * 128)
            for br in range(2):
                act(rstds[br](), Pss[br](), AF.Ln, r=[Pss[br]], w=[rstds[br]], scale=1.0 / 128, bias=EPS)
            for br in range(2):
                act(rstds[br](), rstds[br](), AF.Exp, r=[rstds[br]], w=[rstds[br]], scale=-0.5)
            for br in range(2):
                act(szs[br](), Pzs[br](), AF.Silu, r=[Pzs[br]], w=[szs[br]])
            for br in range(2):
                stt(oTs[br](), Pos[br](), smallc()[:, 8 + br:9 + br], rstds[br](), ALU.mult, ALU.mult,
                    r=[Pos[br], rstds[br], smallc], w=[oTs[br]])
                tt(gT[br](), oTs[br](), szs[br](), ALU.mult, r=[oTs[br], szs[br]], w=[gT[br]])
            dump(19, gT[0], 512, bf=True); dump(20, gT[1], 512, bf=True)
            if DBG_PHASE < 5:
                return
            MARKS.append((ti, 'E', S.ncomp['dve']))
            dma(hres(), xsrc, w=[hres])
            for n in range(2):
                ns = slice(n * 512, (n + 1) * 512)
                Qa, Qb, Qc, Qd = ((P0, P1, P2, P3), (P4p, P5, P6, P0))[n]
                proj_tm(Qc, 0, 3608 + n * 512, 512)
                proj_tm(Qd, 0, 4632 + n * 512, 512)
                for h in range(4):
                    mm(Qa(), gT[0]()[:, h * 128:(h + 1) * 128], Wupa[:, h, ns], h == 0, h == 3, r=[gT[0], WUPA], w=[Qa])
                for h in range(4):
                    mm(Qb(), gT[1]()[:, h * 128:(h + 1) * 128], Wupb[:, h, ns], h == 0, h == 3, r=[gT[1], WUPB], w=[Qb])
                act(sga(), Qc(), AF.Sigmoid, r=[Qc], w=[sga])
                act(sgb(), Qd(), AF.Sigmoid, r=[Qd], w=[sgb])
                tt(sga(), sga(), Qa(), ALU.mult, r=[sga, Qa], w=[sga])
                tt(sgb(), sgb(), Qb(), ALU.mult, r=[sgb, Qb], w=[sgb])
                tt(mg()[:, ns], sga(), sgb(), ALU.add, r=[sga, sgb], w=[mg.sub(n * 512, (n + 1) * 512)])
            if KEEPWARM:
                for _ in range(KEEPWARM):
                    mm(P3(), zB(), Wout[:, 0, 0:512], r=[zB, WOUT], w=[P3])
            for k in range(8):
                tr(PT()[:, k * 128:(k + 1) * 128], mg()[:, k * 128:(k + 1) * 128], identB(), r=[mg, identB],
                   w=[PT.sub(k * 128, (k + 1) * 128)])
            cp(mT(), PT(), r=[PT], w=[mT], eng='act')
            if KEEPWARM:
                for _ in range(KEEPWARM // 2):
                    mm(P3(), zB(), Wout[:, 0, 0:512], r=[zB, WOUT], w=[P3])
            mT3 = mT("p (k t) -> p k t", k=8)
            for n in range(2):
                ns = slice(n * 512, (n + 1) * 512)
                Pn = (P1, P2)[n]
                for kc in range(8):
                    mm(Pn(), mT3[:, kc, :], Wout[:, kc, ns], kc == 0, kc == 7, r=[mT, WOUT], w=[Pn])
                tt(hres()[:, ns], hres()[:, ns], Pn(), ALU.add, r=[hres.sub(n * 512, (n + 1) * 512), Pn],
                   w=[hres.sub(n * 512, (n + 1) * 512)])
            cp(mg(), hres(), r=[hres], w=[mg])
            for k in range(8):
                tr(PT()[:, k * 128:(k + 1) * 128], mg()[:, k * 128:(k + 1) * 128], identB(), r=[mg, identB],
                   w=[PT.sub(k * 128, (k + 1) * 128)])
            cp(mT(), PT(), r=[PT], w=[mT], eng='act')
            PT3 = PTT("p (k t) -> p k t", k=2)
            for n in range(2):
                ns = slice(n * 512, (n + 1) * 512)
                Pa, Pp = ((P0, P1), (P2, P3))[n]
                for kc in range(8):
                    mm(Pa(), mT3[:, kc, :], Wpg[:, kc, ns], kc == 0, kc == 7, r=[mT, WPG], w=[Pa])
                for k in range(2):
                    mm(Pp(), PT3[:, k, :], Wple[:, k, ns], k == 0, k == 1, r=[PTT, WPLE], w=[Pp])
                sp_ = (sga, sgb)[n]
                act(sp_(), Pa(), AF.Sigmoid, r=[Pa], w=[sp_])
                tt(sp_(), sp_(), Pp(), ALU.mult, r=[sp_, Pp], w=[sp_])
                tt(hres()[:, ns], hres()[:, ns], sp_(), ALU.add, r=[hres.sub(n * 512, (n + 1) * 512), sp_],
                   w=[hres.sub(n * 512, (n + 1) * 512)])
            dump(21, hres, 1024)
            if nextA is not None:
                nextA()
            f0, f1, f2 = CO(3, 4), CO(4, 5), CO(5, 6)
            act(mg(), hres(), AF.Square, r=[hres], w=[mg, f0], accum_out=f0())
            act(f1(), f0(), AF.Ln, r=[f0], w=[f1], scale=1.0 / 1024, bias=EPS)
            act(f2(), f1(), AF.Exp, r=[f1], w=[f2], scale=-0.5)
            stt(yt(), hres(), f2(), fnwbc(), ALU.mult, ALU.mult, r=[hres, f2, fnwbc], w=[yt])
            dma(ydst, yt(), r=[yt], eng='pool')

        S.budget = DBG_BUDGET
        tl = list(range(NT + 1) if DBG_TILES is None else DBG_TILES)
        phaseA(tl[0])
        for i_, ti in enumerate(tl):
            tile(ti, (lambda t2=tl[i_ + 1]: phaseA(t2)) if i_ + 1 < len(tl) else None)
        print('op counts', S.ncomp, S.dma_cnt)
        global LAST_MARKS
        LAST_MARKS = MARKS
        S.final_wait_all('sp')
        S.emit()
    return nc


_NC = None


def kernel(x_prompt, x_sample, state_gla, state_gdn, state_conv, p_prompt, p_sample, norm_w, w_in, w_gla_gate,
           b_gla_gate, gla_norm_w, conv_w, gdn_a_log, gdn_dt_bias, gdn_norm_w, w_up_gla, w_up_gdn, w_out,
           w_ple_gate, w_ple, final_norm_w):
    global _NC
    f = lambda a: np.ascontiguousarray(np.asarray(a, dtype=np.float32))
    if _NC is None:
        _NC = build()
    nc = _NC
    shared = {
        "normw": f(norm_w).reshape(1, 1024), "win": f(w_in[0]), "wgg": f(w_gla_gate[0]), "bgg": f(b_gla_gate).reshape(1, 256),
        "glanw": f(gla_norm_w).reshape(1, 128), "convw": f(conv_w[0]), "alog": f(gdn_a_log).reshape(1, 4),
        "dtb": f(gdn_dt_bias).reshape(1, 4), "gdnnw": f(gdn_norm_w).reshape(1, 128), "wupa": f(w_up_gla[0]),
        "wupb": f(w_up_gdn[0]), "wout": f(w_out[0]), "wpg": f(w_ple_gate[0]), "wple": f(w_ple[0]),
        "fnw": f(final_norm_w).reshape(1, 1024),
    }
    xs_ = f(x_sample).reshape(8, 128, 1024)
    ps_ = f(p_sample[0]).reshape(8, 128, 256)
    in_maps = []
    for c in range(8):
        m = dict(shared)
        m["xp"] = f(x_prompt[c])
        m["xsm"] = xs_[c]
        m["pp"] = f(p_prompt[0, c])
        m["psm"] = ps_[c]
        m["sgla"] = f(state_gla[0, 16 * c:16 * c + 16])
        m["sgdn"] = f(state_gdn[0, 16 * c:16 * c + 16])
        m["sconv"] = f(state_conv[0, 16 * c:16 * c + 16]).reshape(48, 1536)
        in_maps.append(m)
    res = run_bass_kernel_spmd(nc, in_maps, core_ids=list(range(8)))
    R = res.results
    y_prompt = np.stack([R[c]["yp"] for c in range(8)], 0)
    y_sample = np.concatenate([R[c]["ysm"].reshape(16, 8, 1024) for c in range(8)], 0)
    gla_p = np.stack([R[c]["glap"] for c in range(8)], 0)[None]
    gdn_p = np.stack([R[c]["gdnp"] for c in range(8)], 0)[None]
    conv_p = np.stack([R[c]["convp"] for c in range(8)], 0)[None]
    gla_s = np.concatenate([R[c]["glas"] for c in range(8)], 0)[None]
    gdn_s = np.concatenate([R[c]["gdns"] for c in range(8)], 0)[None]
    conv_s = np.concatenate([R[c]["convs"] for c in range(8)], 0)[None]
    return (y_prompt, y_sample, gla_p, gdn_p, conv_p, gla_s, gdn_s, conv_s)
```

```python
import numpy as np
from contextlib import ExitStack
import concourse.bass as bass
import concourse.mybir as mybir
from concourse.bass_utils import run_bass_kernel_spmd

F32 = mybir.dt.float32
BF16 = mybir.dt.bfloat16
AF = mybir.ActivationFunctionType
ALU = mybir.AluOpType
EPS = 1e-6
BIG = 30000.0
NT = 16
DBG_TILES = None
DBG_PHASE = 9
DBG_NOW = False
DBG_A = 99
DBG_BUDGET = None
DBG_DUMP = False
DBG_NOILV = False
KEEPWARM = 0
INC = 5656


class V:
    def __init__(self, t, name, lo, hi):
        self.t, self.name, self.lo, self.hi = t, name, lo, hi
        self.key = (name, lo, hi)

    def __call__(self, pat=None, **kw):
        ap = self.t[:, self.lo:self.hi]
        return ap.rearrange(pat, **kw) if pat else ap

    def sub(self, a, b):
        return V(self.t, self.name, self.lo + a, self.lo + b)


PSUM_NAMES = {"P0", "P1", "P2", "P3", "P4", "P5", "P6", "PT"}


def _key(k):
    if isinstance(k, V):
        k = k.key
    if isinstance(k, tuple):
        if k[0] in PSUM_NAMES:
            return (k[0], 0, 1 << 30)
        return k
    return (k, 0, 1 << 30)


class Sched:
    ENGS = ('pe', 'act', 'dve', 'pool', 'sp')

    def __init__(self, nc, es):
        self.nc = nc
        self.streams = {e: [] for e in self.ENGS}
        self.ncomp = {e: 0 for e in self.ENGS}
        self.sems = {e: es.enter_context(nc.semaphore("s_" + e)) for e in self.ENGS}
        self.qsems = {'sp': list(range(0, 8)), 'pool': list(range(8, 14)), 'act': list(range(14, 16))}
        self.dma_sems = [es.enter_context(nc.semaphore("s_dma%d" % i)) for i in range(16)]
        self.dma_cnt = [0] * 16
        self.rr = {'sp': 0, 'pool': 0, 'act': 0}
        self.acc = {}
        self.known = {e: {} for e in self.ENGS}
        self.budget = None
        self._lane = None
        self._stack = []

    def _collect(self, r, w):
        deps = {}

        def add(tok):
            if tok is None:
                return
            s, v = tok
            if deps.get(s, 0) < v:
                deps[s] = v
        for k in r:
            n, lo, hi = _key(k)
            for ent in self.acc.get(n, ()):
                if ent[0] < hi and lo < ent[1]:
                    add(ent[2])
        for k in w:
            n, lo, hi = _key(k)
            for ent in self.acc.get(n, ()):
                if ent[0] < hi and lo < ent[1]:
                    add(ent[2])
                    for t in ent[3]:
                        add(t)
        return deps

    def _finish(self, tok, r, w):
        for k in r:
            n, lo, hi = _key(k)
            hit = False
            for ent in self.acc.setdefault(n, []):
                if ent[0] < hi and lo < ent[1]:
                    ent[3].append(tok)
                    hit = True
            if not hit:
                self.acc[n].append([lo, hi, None, [tok]])
        for k in w:
            n, lo, hi = _key(k)
            lst = self.acc.setdefault(n, [])
            new = []
            for ent in lst:
                if ent[0] < hi and lo < ent[1]:
                    if ent[0] < lo:
                        new.append([ent[0], lo, ent[2], list(ent[3])])
                    if hi < ent[1]:
                        new.append([hi, ent[1], ent[2], list(ent[3])])
                else:
                    new.append(ent)
            new.append([lo, hi, tok, []])
            self.acc[n] = new

    def _waits(self, eng, deps):
        kn = self.known[eng]
        waits = []
        for s, v in deps.items():
            if kn.get(s, 0) < v:
                kn[s] = v
                waits.append((s, v))
        return waits

    @staticmethod
    def _excl(r, w):
        r2, w2 = [], list(w)
        for k in r:
            kk = _key(k)
            if kk[0] in PSUM_NAMES:
                w2.append(k)
            else:
                r2.append(k)
        return r2, w2

    def begin_lane(self):
        self._stack.append(self._lane)
        self._lane = []

    def end_lane(self):
        l = self._lane
        self._lane = self._stack.pop()
        return l

    def emit_list(self, lst):
        for kind, eng, fn, r, w in lst:
            (self.op if kind == 'c' else self.dma)(eng, fn, r, w)

    def merge(self, lanes):
        idx = [0] * len(lanes)
        tot = [max(1, len(l)) for l in lanes]
        while any(idx[i] < len(lanes[i]) for i in range(len(lanes))):
            best = min((i for i in range(len(lanes)) if idx[i] < len(lanes[i])), key=lambda i: idx[i] / tot[i])
            kind, eng, fn, r, w = lanes[best][idx[best]]
            idx[best] += 1
            (self.op if kind == 'c' else self.dma)(eng, fn, r, w)

    def op(self, eng, fn, r=(), w=()):
        if self._lane is not None:
            self._lane.append(('c', eng, fn, r, w))
            return
        r, w = self._excl(r, w)
        if self.budget is not None:
            if self.budget <= 0:
                return
            self.budget -= 1
        deps = self._collect(r, w)
        if eng == 'pe':
            deps.pop('pe', None)
        self.ncomp[eng] += 1
        self.streams[eng].append((fn, self._waits(eng, deps), 'c', None))
        self._finish((eng, self.ncomp[eng]), r, w)

    def dma(self, eng, fn, r=(), w=()):
        if self._lane is not None:
            self._lane.append(('d', eng, fn, r, w))
            return
        if self.budget is not None:
            if self.budget <= 0:
                return
            self.budget -= 1
        deps = self._collect(r, w)
        q = self.qsems[eng]
        i = q[self.rr[eng] % len(q)]
        self.rr[eng] += 1
        sname = 'dma%d' % i
        if self.dma_cnt[i] > 0:
            deps[sname] = max(deps.get(sname, 0), 16 * self.dma_cnt[i])
        self.dma_cnt[i] += 1
        self.streams[eng].append((fn, self._waits(eng, deps), 'd', i))
        self._finish((sname, 16 * self.dma_cnt[i]), r, w)

    def _sem(self, s):
        if s.startswith('dma'):
            return self.dma_sems[int(s[3:])]
        return self.sems[s]

    def final_wait_all(self, eng='sp'):
        waits = []
        for i, c in enumerate(self.dma_cnt):
            if c:
                waits.append(('dma%d' % i, 16 * c))
        for e in self.ENGS:
            if self.ncomp[e] and e != eng:
                waits.append((e, self.ncomp[e]))
        self.streams[eng].append((None, waits, 'w', None))

    def emit(self):
        nc = self.nc
        with nc.Block() as block:
            def mk(ename):
                def body(e):
                    for fn, waits, kind, di in self.streams[ename]:
                        for s, v in waits:
                            e.wait_ge(self._sem(s), v)
                        if fn is None:
                            continue
                        ins = fn(e)
                        if kind == 'c':
                            ins.then_inc(self.sems[ename], 1)
                        else:
                            ins.then_inc(self.dma_sems[di], 16)
                return body
            block.tensor(mk('pe'))
            block.scalar(mk('act'))
            block.vector(mk('dve'))
            block.gpsimd(mk('pool'))
            block.sync(mk('sp'))


def build():
    nc = bass.Bass("TRN2", target_bir_lowering=False)
    di = lambda n, s: nc.dram_tensor(n, s, F32, kind="ExternalInput").ap()
    do = lambda n, s: nc.dram_tensor(n, s, F32, kind="ExternalOutput").ap()
    xp, xsm = di("xp", [2048, 1024]), di("xsm", [128, 1024])
    pp, psm = di("pp", [2048, 256]), di("psm", [128, 256])
    sgla, sgdn, sconv = di("sgla", [16, 4, 64, 128]), di("sgdn", [16, 4, 128, 128]), di("sconv", [48, 1536])
    normw, win = di("normw", [1, 1024]), di("win", [1024, INC])
    wgg, bgg, glanw = di("wgg", [16, 256]), di("bgg", [1, 256]), di("glanw", [1, 128])
    convw, alog, dtb, gdnnw = di("convw", [4, 1536]), di("alog", [1, 4]), di("dtb", [1, 4]), di("gdnnw", [1, 128])
    wupa, wupb = di("wupa", [512, 1024]), di("wupb", [512, 1024])
    wout, wpg, wple = di("wout", [1024, 1024]), di("wpg", [1024, 1024]), di("wple", [256, 1024])
    fnw = di("fnw", [1, 1024])
    yp, ysm = do("yp", [2048, 1024]), do("ysm", [128, 1024])
    glap, gdnp, convp = do("glap", [4, 64, 128]), do("gdnp", [4, 128, 128]), do("convp", [3, 1536])
    glas, gdns, convs = do("glas", [16, 4, 64, 128]), do("gdns", [16, 4, 128, 128]), do("convs", [16, 3, 1536])

    if DBG_DUMP:
        dbgf = nc.dram_tensor("dbgf", [24, 128, 1024], F32, kind="ExternalOutput").ap()
        dbgb = nc.dram_tensor("dbgb", [24, 128, 1024], BF16, kind="ExternalOutput").ap()
    with ExitStack() as es:
        S = Sched(nc, es)
        sbt = lambda n, s, d: es.enter_context(nc.sbuf_tensor(n, s, d))
        pst = lambda n, s, d: es.enter_context(nc.psum_tensor(n, s, d))

        def sv(n, w, d):
            return V(sbt(n, [128, w], d), n, 0, w)

        Win = sbt("Win", [128, 8, INC], BF16)
        Wupa, Wupb = sbt("Wupa", [128, 4, 1024], BF16), sbt("Wupb", [128, 4, 1024], BF16)
        Wout, Wpg = sbt("Wout", [128, 8, 1024], BF16), sbt("Wpg", [128, 8, 1024], BF16)
        Wple = sbt("Wple", [128, 2, 1024], BF16)
        fnwbc = sv("fnwbc", 1024, F32)
        identF, onesF = sv("identF", 128, F32), sv("onesF", 128, F32)
        identB, onesB, zB = sv("identB", 128, BF16), sv("onesB", 128, BF16), sv("zB", 128, BF16)
        Lm = [sv("L_P", 128, F32), sv("L_S", 128, F32)]
        Um = [sv("U_P", 128, F32), sv("U_S", 128, F32)]
        NEG2 = [sv("NEG2_P", 256, F32), sv("NEG2_S", 256, F32)]
        segm = sv("segm", 16, F32)
        hm = sv("hm", 2, F32)
        Wg = sbt("Wg", [32, 256], F32)
        gaug = sbt("gaug", [32, 128], F32)
        cw = sbt("cw", [128, 4, 12], F32)
        smallc = sv("smallc", 32, F32)
        cols = sbt("cols", [128, 256], F32)
        CO = lambda a, b: V(cols, "cols", a, b)
        X, PL = sv("X", 1024, F32), sv("PL", 256, F32)
        XT = sv("XT", 1024, BF16)
        PTTs = [sv("PTT", 256, BF16), sv("PTT2", 256, BF16)]
        XPAD, CARRY = sv("XPAD", 704, F32), sv("CARRY", 36, F32)
        SA, SAb = sv("SA", 256, F32), sv("SAb", 256, BF16)
        SB, SBb = sv("SB", 512, F32), sv("SBb", 512, BF16)
        Ft = sbt("F", [128, 4096], F32)
        Ht = sbt("H", [128, 8192], BF16)
        FV = lambda a, b: V(Ft, "F", a, b)
        HV = lambda a, b: V(Ht, "H", a, b)
        acc, rinv, qnT = FV(0, 512), FV(512, 1024), FV(1024, 1536)
        A4, B4, P4 = FV(1536, 2048), FV(2048, 2560), FV(2560, 3072)
        gbc, lbbc, E12, edbc = FV(3072, 3200), FV(3200, 3328), FV(3328, 3584), FV(3584, 3712)
        lsp, eb, enb, eend = FV(1536, 1792), FV(1792, 2048), FV(2048, 2304), FV(2304, 2560)
        oT, rstd, sga, sgb = FV(0, 512), FV(512, 1024), FV(1024, 1536), FV(1536, 2048)
        hres, yt = FV(2048, 3072), FV(3072, 4096)
        SS = [FV(0, 1024), FV(1024, 2048)]
        scv, tokq = FV(0, 1536), FV(2560, 4096)
        STG = [FV(0, 1024), FV(1024, 2048), FV(2048, 3072), FV(3072, 4096)]
        xs, pb = HV(0, 1024), HV(1792, 2048)
        qeT, keT, kend, va, attTm = HV(0, 512), HV(512, 768), HV(768, 1024), HV(1024, 1536), HV(1536, 2048)
        ebm = FV(2560, 3072)
        SSbd, kendmd = HV(0, 1024), HV(1024, 2048)
        sq, knT, qnTb, vTb = HV(2048, 2560), HV(2560, 3072), HV(3072, 3584), HV(3584, 4096)
        qkTm, TinvTb, kbd, vbeta = HV(4096, 4608), HV(4608, 5120), HV(5120, 5632), HV(5632, 6144)
        kendb, negwT, vnewb, wSTb = HV(6144, 6656), HV(6656, 7168), HV(7168, 7680), HV(7680, 8192)
        qeTb = HV(2048, 2560)
        SSbg, kendmg = HV(2048, 3072), HV(3072, 4096)
        osq, sz, gT = HV(2048, 2560), HV(2560, 3072), [HV(3072, 3584), HV(3584, 4096)]
        mg, mT = HV(4096, 5120), HV(5120, 6144)
        Pb = []
        for i in range(7):
            Pb.append(V(pst("P%d" % i, [128, 512], F32), "P%d" % i, 0, 512))
        PT = V(pst("PT", [128, 1024], BF16), "PT", 0, 1024)

        def mm(out, lhsT, rhs, start=True, stop=True, r=(), w=()):
            S.op('pe', lambda e: e.matmul(out, lhsT=lhsT, rhs=rhs, start=start, stop=stop), r, w)

        def tr(out, in_, ident, r=(), w=()):
            S.op('pe', lambda e: e.transpose(out=out, in_=in_, identity=ident), r, w)

        def act(out, in_, func, r=(), w=(), **kw):
            S.op('act', lambda e: e.activation(out=out, in_=in_, func=func, **kw), r, w)

        def tt(out, in0, in1, op, r=(), w=(), eng='dve'):
            S.op(eng, lambda e: e.tensor_tensor(out=out, in0=in0, in1=in1, op=op), r, w)

        def ts(out, in0, s1, op0, r=(), w=(), s2=None, op1=None, eng='dve'):
            if op1 is None:
                S.op(eng, lambda e: e.tensor_scalar(out=out, in0=in0, scalar1=s1, scalar2=None, op0=op0), r, w)
            else:
                S.op(eng, lambda e: e.tensor_scalar(out=out, in0=in0, scalar1=s1, scalar2=s2, op0=op0, op1=op1), r, w)

        def stt(out, in0, sc, in1, op0, op1, r=(), w=(), eng='dve'):
            S.op(eng, lambda e: e.scalar_tensor_tensor(out=out, in0=in0, scalar=sc, in1=in1, op0=op0, op1=op1), r, w)

        def cp(out, in_, r=(), w=(), eng='dve'):
            if eng == 'act':
                S.op('act', lambda e: e.copy(out=out, in_=in_), r, w)
            else:
                S.op(eng, lambda e: e.tensor_copy(out=out, in_=in_), r, w)

        def dma(out, in_, r=(), w=(), eng='sp', slow=False):
            if slow:
                S.dma(eng, lambda e: e.dma_start(out=out, in_=in_, allow_slow_non_contiguous=True), r, w)
            else:
                S.dma(eng, lambda e: e.dma_start(out=out, in_=in_), r, w)

        def dump(slot, v, n, bf=False, psum=False):
            if not DBG_DUMP:
                return
            if psum:
                cp(yt()[:, 0:n], v()[:, 0:n], r=[v], w=[yt])
                dma(dbgf[slot, :, 0:n], yt()[:, 0:n], r=[yt], eng='pool')
            elif bf:
                dma(dbgb[slot, :, 0:n], v()[:, 0:n], r=[v], eng='pool')
            else:
                dma(dbgf[slot, :, 0:n], v()[:, 0:n], r=[v], eng='pool')

        def zerofill(P):
            mm(P(), zB(), XT()[:, 0:512], True, False, r=[zB, XT], w=[P])

        def ms(v, val, eng='pool'):
            S.op(eng, lambda e: e.memset(v(), val), w=[v])

        def asel(v, pattern, base, cm, cmp, fill, view=None):
            ap = v() if view is None else view
            S.op('pool', lambda e: e.affine_select(out=ap, in_=ap, pattern=pattern, compare_op=cmp, fill=fill,
                                                   base=base, channel_multiplier=cm), r=[v], w=[v])
        t0_ = (list(range(NT + 1)) if DBG_TILES is None else list(DBG_TILES))[0]
        dma(X(), xsm[:, :] if t0_ == NT else xp[t0_ * 128:(t0_ + 1) * 128, :], w=[X])
        dma(PL(), psm[:, :] if t0_ == NT else pp[t0_ * 128:(t0_ + 1) * 128, :], w=[PL])
        PRELOADED = [t0_]
        ms(identF, 0.0)
        asel(identF, [[-1, 128]], 0, 1, ALU.not_equal, 1.0)
        ms(onesF, 1.0)
        cp(identB(), identF(), r=[identF], w=[identB])
        ms(onesB, 1.0)
        ms(zB, 0.0)
        ms(Lm[0], 1.0)
        asel(Lm[0], [[1, 128]], 0, -1, ALU.is_ge, 0.0)
        ms(Um[0], 1.0)
        asel(Um[0], [[-1, 128]], 0, 1, ALU.is_gt, 0.0)
        same = FV(0, 128)
        ms(same, 1.0)
        sview = same("p (a r) -> p a r", a=16)
        asel(same, [[-8, 16], [0, 8]], 0, 1, ALU.is_ge, 0.0, view=sview)
        asel(same, [[8, 16], [0, 8]], 7, -1, ALU.is_ge, 0.0, view=sview)
        ms(segm, 1.0)
        asel(segm, [[-8, 16]], 0, 1, ALU.is_ge, 0.0)
        asel(segm, [[8, 16]], 7, -1, ALU.is_ge, 0.0)
        ms(hm, 1.0)
        asel(hm, [[-64, 2]], 63, -1, ALU.is_ge, 0.0)
        hm1 = FV(256, 258)
        ms(hm1, 1.0)
        asel(hm1, [[64, 2]], -64, 1, ALU.is_ge, 0.0)
        S.op('dve', lambda e: e.tensor_copy(out=hm()[:, 1:2], in_=hm1()[:, 0:1]), r=[hm1, hm], w=[hm])
        tt(Lm[1](), Lm[0](), same(), ALU.mult, r=[Lm[0], same], w=[Lm[1]])
        tt(Um[1](), Um[0](), same(), ALU.mult, r=[Um[0], same], w=[Um[1]])
        ind = FV(128, 256)
        for v in range(2):
            ms(ind, 1.0)
            asel(ind, [[1, 128]], 0, -1, ALU.is_gt, 0.0)
            if v == 1:
                tt(ind(), ind(), same(), ALU.mult, r=[ind, same], w=[ind])
            ts(NEG2[v]()[:, 0:128], ind(), 1.0, ALU.subtract, s2=BIG, op1=ALU.mult, r=[ind], w=[NEG2[v].sub(0, 128)])
            ts(NEG2[v]()[:, 128:256], Lm[v](), 1.0, ALU.subtract, s2=BIG, op1=ALU.mult, r=[Lm[v]],
               w=[NEG2[v].sub(128, 256)])
        S.op('pool', lambda e: e.memset(gaug[:], 1.0), w=["gaug"])
        S.op('dve', lambda e: e.memset(Wg[:], 0.0), w=["Wg"])
        dma(Wg[0:16, :], wgg[:, :], r=["Wg"], w=["Wg"])
        dma(Wg[16:17, :], bgg[0:1, :], r=["Wg"], w=["Wg"])
        PR2, PRc = FV(1536, 1664), FV(0, 1536)
        dma(Ft[0:8, 1536:1664], normw[0].rearrange("(k p) -> k p", p=128), w=[PR2])
        dma(Ft[8:9, 1536:1664], glanw[0:1, :], r=[PR2], w=[PR2])
        dma(Ft[9:10, 1536:1664], gdnnw[0:1, :], r=[PR2], w=[PR2])
        dma(Ft[0:4, 0:1536], convw[:, :], w=[PRc])
        tr(Pb[0]()[:, 0:10], Ft[0:10, 1536:1664], identF.t[0:10, 0:10], r=[PR2, identF], w=[Pb[0]])
        cp(smallc()[:, 0:10], Pb[0]()[:, 0:10], r=[Pb[0]], w=[smallc.sub(0, 10)])
        for k in range(12):
            tr(Pb[1]()[:, 4 * k:4 * k + 4], Ft[0:4, k * 128:(k + 1) * 128], identF.t[0:4, 0:4], r=[PRc, identF], w=[Pb[1]])
        cp(cw[:].rearrange("p w k -> p k w"), Pb[1]()[:, 0:48].rearrange("p (k w) -> p k w", w=4), r=[Pb[1]], w=["cw"])
        dma(smallc()[:, 12:16], dtb[0:1, :].partition_broadcast(128), w=[smallc.sub(12, 16)])
        dma(smallc()[:, 16:20], alog[0:1, :].partition_broadcast(128), w=[smallc.sub(16, 20)])
        act(smallc()[:, 16:20], smallc()[:, 16:20], AF.Exp, r=[smallc.sub(16, 20)], w=[smallc.sub(16, 20)])
        ts(smallc()[:, 16:20], smallc()[:, 16:20], -1.0, ALU.mult, r=[smallc.sub(16, 20)], w=[smallc.sub(16, 20)])
        dma(fnwbc(), fnw[0:1, :].partition_broadcast(128), w=[fnwbc])
        ms(SA, 0.0), ms(SAb, 0.0), ms(SB, 0.0), ms(SBb, 0.0), ms(CARRY, 0.0)

        if not DBG_NOW:
            for (c0, c1) in ((0, 3608), (3608, INC)):
                for kc in range(8):
                    dma(Win[:, kc, c0:c1], win[kc * 128:(kc + 1) * 128, c0:c1],
                        w=[("Win", kc * INC + c0, kc * INC + c1)], eng='pool')
            dma(Wupa[:], wupa.rearrange("(h p) n -> p h n", p=128), w=["Wupa"], eng='pool')
            dma(Wupb[:], wupb.rearrange("(h p) n -> p h n", p=128), w=["Wupb"], eng='pool')
            for hk in range(2):
                dma(Wout[:, 4 * hk:4 * hk + 4, :], wout[512 * hk:512 * hk + 512, :].rearrange("(k p) n -> p k n", p=128),
                    w=[("Wout", 4 * hk, 4 * hk + 4)], eng='pool')
            for hk in range(2):
                dma(Wpg[:, 4 * hk:4 * hk + 4, :], wpg[512 * hk:512 * hk + 512, :].rearrange("(k p) n -> p k n", p=128),
                    w=[("Wpg", 4 * hk, 4 * hk + 4)], eng='pool')
            dma(Wple[:], wple.rearrange("(k p) n -> p k n", p=128), w=["Wple"], eng='pool')
        WIN, WUPA, WUPB, WOUT, WPG, WPLE = "Win", "Wupa", "Wupb", "Wout", "Wpg", "Wple"

        MARKS = []

        def phaseA(ti):
            smp = (ti == NT)
            xsrc = xsm[:, :] if smp else xp[ti * 128:(ti + 1) * 128, :]
            psrc = psm[:, :] if smp else pp[ti * 128:(ti + 1) * 128, :]
            PTT = PTTs[ti % 2]
            MARKS.append((ti, 'A', S.ncomp['dve']))
            if ti in PRELOADED:
                PRELOADED.remove(ti)
            else:
                dma(X(), xsrc, w=[X])
                dma(PL(), psrc, w=[PL])
            c0_, c1_, c2_ = CO(0, 1), CO(1, 2), CO(2, 3)
            act(xs(), X(), AF.Square, r=[X], w=[xs, c0_], accum_out=c0_())
            act(c1_(), c0_(), AF.Ln, r=[c0_], w=[c1_], scale=1.0 / 1024, bias=EPS)
            act(c2_(), c1_(), AF.Exp, r=[c1_], w=[c2_], scale=-0.5)
            ts(xs(), X(), c2_(), ALU.mult, r=[X, c2_], w=[xs])
            for k in range(8):
                tr(PT()[:, k * 128:(k + 1) * 128], xs()[:, k * 128:(k + 1) * 128], identB(), r=[xs, identB],
                   w=[PT.sub(k * 128, (k + 1) * 128)])
            tt(XT("p (k t) -> p k t", k=8), PT("p (k t) -> p k t", k=8),
               smallc()[:, 0:8].unsqueeze(2).broadcast_to([128, 8, 128]), ALU.mult, r=[PT, smallc], w=[XT])
            cp(pb(), PL(), r=[PL], w=[pb])
            for k in range(2):
                tr(PT()[:, k * 128:(k + 1) * 128], pb()[:, k * 128:(k + 1) * 128], identB(), r=[pb, identB],
                   w=[PT.sub(k * 128, (k + 1) * 128)])
            cp(PTT(), PT()[:, 0:256], r=[PT.sub(0, 256)], w=[PTT])

            if (not smp) and ti != NT - 1 and not DBG_NOILV:
                XT3a = XT("p (k t) -> p k t", k=8)

                def hz_conv(g):
                    for c in range(4):
                        c0 = 1552 + (4 * g + c) * 128
                        for kc in range(8):
                            mm(Pb[g]()[:, c * 128:(c + 1) * 128], Win[:, kc, c0:c0 + 128], XT3a[:, kc, :], kc == 0, kc == 7,
                               r=[XT, ("Win", kc * INC + c0, kc * INC + c0 + 128)], w=[Pb[g]])
                for kc in range(8):
                    mm(Pb[3]()[0:16, 0:128], Win[:, kc, 1024:1040], XT3a[:, kc, :], kc == 0, kc == 7,
                       r=[XT, ("Win", kc * INC + 1024, kc * INC + 1040)], w=[Pb[3]])
                hz_conv(0)
                for c in range(4):
                    for kc in range(8):
                        mm(Pb[5]()[:, c * 128:(c + 1) * 128], Win[:, kc, c * 128:(c + 1) * 128], XT3a[:, kc, :], kc == 0, kc == 7,
                           r=[XT, ("Win", kc * INC + c * 128, kc * INC + (c + 1) * 128)], w=[Pb[5]])
                hz_conv(1)
                hz_conv(2)

        def tile(ti, nextA=None):
            smp = (ti == NT)
            v = 1 if smp else 0
            L, U, NG = Lm[v], Um[v], NEG2[v]
            xsrc = xsm[:, :] if smp else xp[ti * 128:(ti + 1) * 128, :]
            psrc = psm[:, :] if smp else pp[ti * 128:(ti + 1) * 128, :]
            ydst = ysm[:, :] if smp else yp[ti * 128:(ti + 1) * 128, :]
            P0, P1, P2, P3, P4p, P5, P6 = Pb
            W = lambda c0, n: Win[:, :, c0:c0 + n]

            PTT = PTTs[ti % 2]
            XT3 = XT("p (k t) -> p k t", k=8)
            dump(0, XT, 1024, bf=True)

            def proj_fm(Pv, off, c0, M=128):
                for kc in range(8):
                    mm(Pv()[0:M, off:off + 128], Win[:, kc, c0:c0 + M], XT3[:, kc, :], kc == 0, kc == 7,
                       r=[XT, ("Win", kc * INC + c0, kc * INC + c0 + M)], w=[Pv.sub(off, off + 128)])

            def proj_tm(Pv, off, c0, n):
                for kc in range(8):
                    mm(Pv()[:, off:off + n], XT3[:, kc, :], Win[:, kc, c0:c0 + n], kc == 0, kc == 7,
                       r=[XT, ("Win", kc * INC + c0, kc * INC + c0 + n)], w=[Pv.sub(off, off + n)])

            MARKS.append((ti, 'B', S.ncomp['dve']))
            G0, G1, G2 = P3, P4p, P5
            ilv = (not smp) and ti != NT - 1 and not DBG_NOILV
            if ilv:
                S.begin_lane()
            if not ilv:
                proj_fm(G0, 0, 1024, M=16)
            S.op('dve', lambda e: e.tensor_copy(out=gaug[0:16, :], in_=G0()[0:16, 0:128]), r=[G0.sub(0, 128)], w=["gaug"])
            mm(G0()[:, 128:384], gaug[0:17, :], Wg[0:17, :], r=["gaug", "Wg"], w=[G0.sub(128, 384)])
            act(lsp(), G0()[:, 128:384], AF.Exp, r=[G0.sub(128, 384)], w=[lsp], scale=-1.0)
            act(lsp(), lsp(), AF.Ln, r=[lsp], w=[lsp], bias=1.0)
            for dc in range(2):
                mm(G1()[:, dc * 128:(dc + 1) * 128], lsp()[:, dc * 128:(dc + 1) * 128], L(), r=[lsp, L],
                   w=[G1.sub(dc * 128, (dc + 1) * 128)])
            mm(G1()[:, 256:512], U(), lsp(), r=[lsp, U], w=[G1.sub(256, 512)])
            act(eb(), G1()[:, 0:256], AF.Exp, r=[G1.sub(0, 256)], w=[eb], scale=-1.0 / 16)
            act(enb(), G1()[:, 0:256], AF.Exp, r=[G1.sub(0, 256)], w=[enb], scale=1.0 / 16)
            act(eend(), G1()[:, 256:512], AF.Exp, r=[G1.sub(256, 512)], w=[eend], scale=-1.0 / 16)
            dump(1, lsp, 256); dump(2, eb, 256); dump(3, eend, 256)
            for c in range(4):
                if not ilv:
                    proj_fm(G2, c * 128, c * 128)
            dump(4, G2, 512, psum=True)
            tt(ebm("p (c h t) -> p c h t", c=2, h=2), eb("p (c t) -> p c t", c=2).unsqueeze(2).broadcast_to([128, 2, 2, 128]),
               hm().unsqueeze(1).unsqueeze(3).broadcast_to([128, 2, 2, 128]), ALU.mult, r=[eb, hm], w=[ebm])
            for hh in range(2):
                stt(qeT("p (c h t) -> p c h t", c=2, h=2)[:, :, hh, :],
                    G2()[:, 0:256].rearrange("p (c t) -> p c t", c=2), 0.125,
                    ebm("p (c h t) -> p c h t", c=2, h=2)[:, :, hh, :], ALU.mult, ALU.mult, r=[G2.sub(0, 256), ebm], w=[qeT])
            tt(keT(), G2()[:, 256:512], enb(), ALU.mult, r=[G2.sub(256, 512), enb], w=[keT])
            proj_tm(P3, 0, 256, 512)
            proj_tm(P4p, 0, 768, 256)
            tt(kend(), P3()[:, 0:256], eend(), ALU.mult, r=[P3.sub(0, 256), eend], w=[kend])
            cp(va()[:, 0:256], P3()[:, 256:512], r=[P3.sub(256, 512)], w=[va.sub(0, 256)], eng='act')
            cp(va()[:, 256:512], P4p()[:, 0:256], r=[P4p.sub(0, 256)], w=[va.sub(256, 512)], eng='act')
            for h in range(4):
                dc, hp = h // 2, 64 * (h % 2)
                mm(P5()[:, h * 128:(h + 1) * 128], keT()[:, dc * 128:(dc + 1) * 128],
                   qeT()[:, h * 128:(h + 1) * 128], r=[keT, qeT], w=[P5.sub(h * 128, (h + 1) * 128)])
            tt(attTm("p (h c) -> p h c", h=4), P5("p (h c) -> p h c", h=4),
               L().unsqueeze(1).broadcast_to([128, 4, 128]), ALU.mult, r=[P5, L], w=[attTm])
            dump(5, qeT, 512, bf=True); dump(6, keT, 256, bf=True); dump(7, kend, 256, bf=True); dump(8, va, 512, bf=True); dump(9, attTm, 512, bf=True)
            zerofill(P6)
            if not smp:
                for h in range(4):
                    dc, hp = h // 2, 64 * (h % 2)
                    mm(P6()[:, h * 128:(h + 1) * 128], va()[:, h * 128:(h + 1) * 128], attTm()[:, h * 128:(h + 1) * 128],
                       False, False, r=[va, attTm], w=[P6])
                    mm(P6()[:, h * 128:(h + 1) * 128], SAb()[:, dc * 128:(dc + 1) * 128],
                       qeT()[:, h * 128:(h + 1) * 128], False, h == 3, r=[SAb, qeT], w=[P6])
                for h in range(4):
                    dc, hp = h // 2, 64 * (h % 2)
                    mm(P4p()[hp:hp + 64, 256 + dc * 128:256 + (dc + 1) * 128], kend()[:, h * 64:(h + 1) * 64],
                       va()[:, h * 128:(h + 1) * 128], r=[kend, va], w=[P4p.sub(256, 512)])
                for dc in range(2):
                    stt(SA()[:, dc * 128:(dc + 1) * 128], SA()[:, dc * 128:(dc + 1) * 128],
                        eb()[:, dc * 128 + 127:dc * 128 + 128], P4p()[:, 256 + dc * 128:256 + (dc + 1) * 128],
                        ALU.mult, ALU.add, r=[SA, eb, P4p.sub(256, 512)], w=[SA])
                cp(SAb(), SA(), r=[SA], w=[SAb], eng='act')
                if ti == NT - 1:
                    dma(glap.rearrange("(c h) d v -> (h d) c v", c=2), SA("p (c v) -> p c v", c=2), r=[SA], eng='pool')
            else:
                for h in range(4):
                    mm(P6()[:, h * 128:(h + 1) * 128], va()[:, h * 128:(h + 1) * 128], attTm()[:, h * 128:(h + 1) * 128],
                       False, False, r=[va, attTm], w=[P6])
                eb4 = eb("p (c s t) -> p c s t", c=2, s=16)
                SSg = [FV(0, 1024), FV(3072, 4096)]
                for dc in range(2):
                    for hf in range(2):
                        SSx = SSg[hf]
                        SS3 = SSx("p (s v) -> p s v", s=8)
                        dma(SS3, sgla[8 * hf:8 * hf + 8, 2 * dc:2 * dc + 2, :, :].rearrange("s h d v -> (h d) s v"),
                            w=[SSx])
                        SSbg_ = (SSbg, HV(4096, 5120))[(2 * dc + hf) % 2]
                        cp(SSbg_(), SSx(), r=[SSx], w=[SSbg_], eng=('act', 'dve')[(2 * dc + hf) % 2])
                        SSb3 = SSbg_("p (s v) -> p s v", s=8)
                        for hh in range(2):
                            h, hp = 2 * dc + hh, 64 * hh
                            for j in range(8):
                                s = 8 * hf + j
                                mm(P6()[:, h * 128 + 8 * s:h * 128 + 8 * s + 8], SSb3[:, j, :],
                                   qeT()[:, h * 128 + 8 * s:h * 128 + 8 * s + 8], False,
                                   (dc == 1 and hf == 1 and hh == 1 and j == 7), r=[SSbg_, qeT], w=[P6])
                        km3 = kendmg("p (s d) -> p s d", s=8)
                        tt(km3, kend()[:, dc * 128:(dc + 1) * 128].unsqueeze(1).broadcast_to([128, 8, 128]),
                           segm()[:, 8 * hf:8 * hf + 8].unsqueeze(2).broadcast_to([128, 8, 128]), ALU.mult,
                           r=[kend, segm], w=[kendmg])
                        for g in range(2):
                            Pq = (G0, G1)[g]
                            for jj in range(4):
                                j = 4 * g + jj
                                for hh in range(2):
                                    h, hp = 2 * dc + hh, 64 * hh
                                    mm(Pq()[hp:hp + 64, jj * 128:(jj + 1) * 128], km3[:, j, hp:hp + 64],
                                       va()[:, h * 128:(h + 1) * 128], r=[kendmg, va], w=[Pq])
                            sl = SS3[:, 4 * g:4 * g + 4, :]
                            s0 = 8 * hf + 4 * g
                            tt(sl, sl, eb4[:, dc, s0:s0 + 4, 7:8].broadcast_to([128, 4, 128]), ALU.mult,
                               r=[SSx, eb], w=[SSx])
                            tt(sl, sl, Pq("p (s v) -> p s v", s=4), ALU.add, r=[SSx, Pq], w=[SSx])
                        dma(glas[8 * hf:8 * hf + 8, 2 * dc:2 * dc + 2, :, :].rearrange("s h d v -> (h d) s v"), SS3,
                            r=[SSx], eng='pool')

            dump(10, P6, 512, psum=True)
            MARKS.append((ti, 'C', S.ncomp['dve']))
            laneB = S.end_lane() if ilv else None
            if ilv:
                S.begin_lane()
            if smp or ti == NT - 1:
                for c3 in range(3):
                    Pc = (P3, P4p, P3)[c3]
                    proj_tm(Pc, 0, 1552 + c3 * 512, 512)
                    cp(tokq()[:, c3 * 512:(c3 + 1) * 512], Pc(), r=[Pc], w=[tokq.sub(c3 * 512, (c3 + 1) * 512)], eng='act')
                if smp:
                    for s in range(16):
                        dma(convs[s], tokq.t[8 * s + 5:8 * s + 8, 2560:4096], r=[tokq], eng='pool')
                else:
                    dma(convp[:, :], tokq.t[125:128, 2560:4096], r=[tokq], eng='pool')
            if smp:
                dma(scv.t[0:48, 0:1536], sconv[:, :], w=[scv])
                for k in range(12):
                    Px, c = (P5, k) if k < 8 else (P4p, k - 8)
                    tr(Px()[:, c * 48:(c + 1) * 48], scv.t[0:48, k * 128:(k + 1) * 128], identF.t[0:48, 0:48],
                       r=[scv, identF], w=[Px])
            PGT = P0 if ilv else P2
            S.begin_lane()
            proj_tm(PGT, 0, 3088, 8)
            t1, gcol, l2, beta = CO(8, 12), CO(12, 16), CO(16, 20), CO(20, 24)
            negdec, bed, Ecol, gm = CO(24, 28), CO(28, 32), CO(32, 104), CO(104, 168)
            tt(t1(), PGT()[:, 0:4], smallc()[:, 12:16], ALU.add, r=[PGT.sub(0, 8), smallc], w=[t1])
            act(t1(), t1(), AF.Exp, r=[t1], w=[t1])
            act(t1(), t1(), AF.Ln, r=[t1], w=[t1], bias=1.0)
            tt(gcol(), t1(), smallc()[:, 16:20], ALU.mult, r=[t1, smallc], w=[gcol])
            act(l2(), PGT()[:, 4:8], AF.Exp, r=[PGT.sub(0, 8)], w=[l2], scale=-1.0)
            act(l2(), l2(), AF.Ln, r=[l2], w=[l2], bias=1.0)
            act(beta(), l2(), AF.Exp, r=[l2], w=[beta], scale=-1.0)
            mm(PGT()[:, 8:12], L(), gcol(), r=[L, gcol], w=[PGT.sub(8, 12)])
            mm(PGT()[:, 12:16], U(), gcol(), r=[U, gcol], w=[PGT.sub(12, 16)])
            if not smp:
                mm(PGT()[:, 16:20], onesF(), gcol(), r=[onesF, gcol], w=[PGT.sub(16, 20)])
                ne = 12
            else:
                tt(gm("p (s h) -> p s h", s=16), gcol().unsqueeze(1).broadcast_to([128, 16, 4]),
                   segm().unsqueeze(2).broadcast_to([128, 16, 4]), ALU.mult, r=[gcol, segm], w=[gm])
                mm(PGT()[:, 16:80], onesF(), gm(), r=[onesF, gm], w=[PGT.sub(16, 80)])
                ne = 72
            act(Ecol()[:, 0:ne], PGT()[:, 8:8 + ne], AF.Exp, r=[PGT.sub(8, 8 + ne)], w=[Ecol])
            edec, eend4, etot = Ecol()[:, 0:4], Ecol()[:, 4:8], Ecol()[:, 8:ne]
            ts(negdec(), PGT()[:, 8:12], -1.0, ALU.mult, r=[PGT.sub(8, 12)], w=[negdec])
            tt(bed(), beta(), edec, ALU.mult, r=[beta, Ecol], w=[bed])
            for g in range(3):
                Pg = (P0, P1, P2)[g]
                if ilv:
                    continue
                for c in range(4):
                    proj_fm(Pg, c * 128, 1552 + (4 * g + c) * 128)
            laneGate = S.end_lane()
            if not ilv:
                S.emit_list(laneGate)
            stages = {}
            accs, rinvs = [FV(0, 512), FV(3072, 3584)], [FV(512, 1024), FV(3584, 4096)]
            for g in range(3):
                Pg = (P0, P1, P2)[g]
                acc, rinv = accs[g % 2], rinvs[g % 2]
                S.begin_lane()
                if not smp:
                    xp3 = XPAD()[:, 0:524].rearrange("p (c t) -> p c t", c=4)
                    car = CARRY("p (k w) -> p k w", k=12)[:, 4 * g:4 * g + 4, :]
                    cp(xp3[:, :, 0:3], car, r=[CARRY], w=[XPAD])
                    cp(xp3[:, :, 3:131], Pg("p (c t) -> p c t", c=4), r=[Pg], w=[XPAD], eng='act')
                    cp(car, xp3[:, :, 128:131], r=[XPAD], w=[CARRY])
                else:
                    xp4 = XPAD("p (c s t) -> p c s t", c=4, s=16)
                    Px, po = ((P5, 0), (P5, 192), (P4p, 0))[g]
                    cp(xp4[:, :, :, 0:3], Px()[:, po:po + 192].rearrange("p (c s w) -> p c s w", c=4, s=16), r=[Px], w=[XPAD])
                    cp(xp4[:, :, :, 3:11], Pg("p (c s t) -> p c s t", c=4, s=16), r=[Pg], w=[XPAD], eng='act')
                stages[g, 0] = S.end_lane()
                S.begin_lane()
                tmp = rinv
                for w_ in range(4):
                    cwv = cw[:, w_, 4 * g:4 * g + 4]
                    if not smp:
                        a3 = acc("p (c t) -> p c t", c=4)
                        t3 = tmp("p (c t) -> p c t", c=4)
                        xw = xp3[:, :, w_:w_ + 128]
                        bcw = cwv.unsqueeze(2).broadcast_to([128, 4, 128])
                    else:
                        a3 = acc("p (c s t) -> p c s t", c=4, s=16)
                        t3 = tmp("p (c s t) -> p c s t", c=4, s=16)
                        xw = xp4[:, :, :, w_:w_ + 8]
                        bcw = cwv.unsqueeze(2).unsqueeze(3).broadcast_to([128, 4, 16, 8])
                    if w_ == 0:
                        tt(a3, xw, bcw, ALU.mult, r=[XPAD, "cw"], w=[acc])
                    else:
                        tt(t3, xw, bcw, ALU.mult, r=[XPAD, "cw"], w=[tmp])
                        tt(a3, a3, t3, ALU.add, r=[acc, tmp], w=[acc])
                stages[g, 1] = S.end_lane()
                S.begin_lane()
                if g == 2:
                    act(vTb(), acc(), AF.Silu, r=[acc], w=[vTb])
                    stages[g, 2] = S.end_lane()
                    S.begin_lane()
                else:
                    act(acc(), acc(), AF.Silu, r=[acc], w=[acc])
                    tt(sq(), acc(), acc(), ALU.mult, r=[acc], w=[sq])
                    mm(Pg(), onesB(), sq(), r=[onesB, sq], w=[Pg])
                    stages[g, 2] = S.end_lane()
                    S.begin_lane()
                    act(rinv(), Pg(), AF.Ln, r=[Pg], w=[rinv], bias=EPS)
                    act(rinv(), rinv(), AF.Exp, r=[rinv], w=[rinv], scale=-0.5)
                    if g == 0:
                        stt(qnT(), acc(), 128.0 ** -0.5, rinv(), ALU.mult, ALU.mult, r=[acc, rinv], w=[qnT])
                    else:
                        tt(knT(), acc(), rinv(), ALU.mult, r=[acc, rinv], w=[knT])
                stages[g, 3] = S.end_lane()
            pre_keys = [(0, 0), (0, 1)] if ilv else []
            for key in [(0, 0), (0, 1), (1, 0), (0, 2), (1, 1), (0, 3), (2, 0), (1, 2), (2, 1), (1, 3), (2, 2), (2, 3)]:
                if key == (2, 0) and ilv:
                    S.emit_list(laneGate)
                if key not in pre_keys:
                    S.emit_list(stages[key])
            acc, rinv = accs[0], rinvs[0]
            cp(qnTb(), qnT(), r=[qnT], w=[qnTb], eng='act')
            if ilv:
                laneC = S.end_lane()
                for key in pre_keys:
                    S.emit_list(stages[key])
                S.merge([laneB, laneC])
            dump(11, qnT, 512); dump(12, knT, 512, bf=True); dump(13, vTb, 512, bf=True)
            for h in range(4):
                tr(PT()[:, h * 128:(h + 1) * 128], knT()[:, h * 128:(h + 1) * 128], identB(), r=[knT, identB],
                   w=[PT.sub(h * 128, (h + 1) * 128)])
                tr(PT()[:, 512 + h * 128:512 + (h + 1) * 128], vTb()[:, h * 128:(h + 1) * 128], identB(),
                   r=[vTb, identB], w=[PT.sub(512 + h * 128, 512 + (h + 1) * 128)])
            MARKS.append((ti, 'C8', S.ncomp['dve']))
            gbcs, lbbcs = [FV(3072, 3200), FV(3712, 3840)], [FV(3200, 3328), FV(3840, 3968)]
            edbcs, E12s = [FV(3584, 3712), FV(3968, 4096)], [FV(512, 768), FV(768, 1024)]
            hst = {}
            for h in range(4):
                RB = (P0, P1)[h % 2]
                KQ = (P2, P4p)[h % 2]
                gbc, lbbc, edbc, E12 = gbcs[h % 2], lbbcs[h % 2], edbcs[h % 2], E12s[h % 2]
                hs = slice(h * 128, (h + 1) * 128)
                S.begin_lane()
                cp(gbc(), gcol()[:, h:h + 1].broadcast_to([128, 128]), r=[gcol], w=[gbc])
                ts(lbbc(), l2()[:, h:h + 1].broadcast_to([128, 128]), -1.0, ALU.mult, r=[l2], w=[lbbc])
                mm(RB()[:, 0:128], gbc(), L(), True, True, r=[gbc, L], w=[RB])
                mm(RB()[:, 128:256], lbbc(), identF(), True, True, r=[lbbc, identF], w=[RB])
                mm(KQ()[:, 0:128], knT()[:, hs], knT()[:, hs], r=[knT], w=[KQ])
                mm(KQ()[:, 128:256], knT()[:, hs], qnTb()[:, hs], r=[knT, qnTb], w=[KQ])
                hst[h, 0] = S.end_lane()
                S.begin_lane()
                tt(E12()[:, 128:256], RB()[:, 0:128], NG()[:, 128:256], ALU.add, r=[RB, NG], w=[E12])
                tt(E12()[:, 0:128], RB()[:, 128:256], NG()[:, 0:128], ALU.add, r=[RB, NG], w=[E12])
                tt(E12()[:, 0:128], E12()[:, 0:128], RB()[:, 0:128], ALU.add, r=[RB, E12], w=[E12])
                act(E12(), E12(), AF.Exp, r=[E12, negdec], w=[E12], bias=negdec()[:, h:h + 1])
                act(edbc(), RB()[:, 0:128], AF.Exp, r=[RB], w=[edbc])
                tt(qeTb()[:, hs], qnT()[:, hs], edbc(), ALU.mult, r=[qnT, edbc], w=[qeTb.sub(h * 128, (h + 1) * 128)])
                hst[h, 1] = S.end_lane()
                S.begin_lane()
                tt(B4()[:, hs], KQ()[:, 0:128], E12()[:, 0:128], ALU.mult, r=[KQ, E12],
                   w=[B4.sub(h * 128, (h + 1) * 128)])
                tt(qkTm()[:, hs], KQ()[:, 128:256], E12()[:, 128:256], ALU.mult,
                   r=[KQ, E12], w=[qkTm.sub(h * 128, (h + 1) * 128)])
                tr(P3()[:, hs], B4()[:, hs], identF(), r=[B4.sub(h * 128, (h + 1) * 128), identF],
                   w=[P3.sub(h * 128, (h + 1) * 128)])
                hst[h, 2] = S.end_lane()
            for key in [(0, 0), (1, 0), (0, 1), (0, 2), (2, 0), (1, 1), (1, 2), (3, 0), (2, 1), (2, 2), (3, 1), (3, 2)]:
                S.emit_list(hst[key])
            h3 = "p (h c) -> p h c"

            class VB(V):
                def __call__(self):
                    return self.t[:, self.lo:self.hi].bitcast(BF16)

            def FB(lo):
                return [VB(Ft, "F", lo + 128 * q, lo + 128 * (q + 1)) for q in range(2)]
            Ab_, Bb_, Pb_, MTh_, MTl_, Rb_, Y0T_ = FB(1536), FB(1792), FB(2560), FB(2816), FB(512), FB(768), FB(1024)
            IA = FV(0, 512)
            tt(IA(h3, h=4), identF().unsqueeze(1).broadcast_to([128, 4, 128]), P3(h3, h=4), ALU.add, r=[identF, P3], w=[IA])
            for q in range(2):
                qs = slice(q * 256, (q + 1) * 256)
                cp(Ab_[q](), P3()[:, qs], r=[P3], w=[Ab_[q]], eng='act')
                cp(Bb_[q](), B4()[:, qs], r=[B4.sub(q * 256, (q + 1) * 256)], w=[Bb_[q]], eng='act')
                cp(MTh_[q](), IA()[:, qs], r=[IA.sub(q * 256, (q + 1) * 256)], w=[MTh_[q]], eng='act')
                tt(MTl_[q](), IA()[:, qs], MTh_[q](), ALU.subtract, r=[IA.sub(q * 256, (q + 1) * 256), MTh_[q]], w=[MTl_[q]])
                tt(Pb_[q]().rearrange("p (h c) -> p h c", h=2), identF().unsqueeze(1).broadcast_to([128, 2, 128]),
                   B4()[:, qs].rearrange("p (h c) -> p h c", h=2), ALU.subtract,
                   r=[identF, B4.sub(q * 256, (q + 1) * 256)], w=[Pb_[q]])
            MARKS.append((ti, 'C9', S.ncomp['dve']))
            nst = 2 if smp else 6
            half_lanes = []
            for hf2 in range(2):
                QA, QB, QZ = ((P0, P1, P2), (P3, P4p, P5))[hf2]
                Ab, Bb, Yb, MTh, MTl, Rb, Y0T = (Ab_[hf2], Bb_[hf2], Pb_[hf2], MTh_[hf2], MTl_[hf2], Rb_[hf2], Y0T_[hf2])
                ea, eb_ = (('act', 'dve'), ('dve', 'act'))[hf2]
                S.begin_lane()
                for k in range(1, nst + 1):
                    last = (k == nst)
                    if not last:
                        for hh in range(2):
                            hs = slice(hh * 128, (hh + 1) * 128)
                            mm(QB()[:, hs], Ab()[:, hs], Bb()[:, hs], r=[Ab, Bb], w=[QB])
                    for hh in range(2):
                        hs = slice(hh * 128, (hh + 1) * 128)
                        mm(QA()[:, hs], Bb()[:, hs], Ab()[:, hs], r=[Ab, Bb], w=[QA])
                    cp(Ab(), QA()[:, 0:256], r=[QA], w=[Ab], eng=ea)
                    if not last:
                        cp(Bb(), QB()[:, 0:256], r=[QB], w=[Bb], eng=eb_)
                    for hh in range(2):
                        hs = slice(hh * 128, (hh + 1) * 128)
                        mm(QZ()[:, hs], Ab()[:, hs], Yb()[:, hs], r=[Ab, Yb], w=[QZ])
                    tt(Yb(), Yb(), QZ()[:, 0:256], ALU.add, r=[Yb, QZ], w=[Yb])
                for hh in range(2):
                    hs = slice(hh * 128, (hh + 1) * 128)
                    mm(QA()[:, hs], MTh()[:, hs], Yb()[:, hs], True, False, r=[MTh, Yb], w=[QA])
                    mm(QA()[:, hs], MTl()[:, hs], Yb()[:, hs], False, True, r=[MTl, Yb], w=[QA])
                    mm(QB()[:, hs], Yb()[:, hs], identB(), r=[Yb, identB], w=[QB])
                tt(Rb().rearrange("p (h c) -> p h c", h=2), identF().unsqueeze(1).broadcast_to([128, 2, 128]),
                   QA()[:, 0:256].rearrange("p (h c) -> p h c", h=2), ALU.subtract, r=[identF, QA], w=[Rb])
                cp(Y0T(), QB()[:, 0:256], r=[QB], w=[Y0T], eng=ea)
                for hh in range(2):
                    hs = slice(hh * 128, (hh + 1) * 128)
                    mm(QZ()[:, hs], Y0T()[:, hs], Rb()[:, hs], r=[Y0T, Rb], w=[QZ])
                tt(TinvTb()[:, hf2 * 256:(hf2 + 1) * 256], Yb(), QZ()[:, 0:256], ALU.add, r=[Yb, QZ],
                   w=[TinvTb.sub(hf2 * 256, (hf2 + 1) * 256)])
                half_lanes.append(S.end_lane())
            S.merge(half_lanes)
            dump(14, P4, 512); dump(15, V(cols, 'cols', 0, 256), 256); dump(16, qkTm, 512, bf=True)
            MARKS.append((ti, 'C10', S.ncomp['dve']))
            bc4 = lambda ap: ap.unsqueeze(2).broadcast_to([128, 4, 128])
            tt(kbd(h3, h=4), PT()[:, 0:512].rearrange(h3, h=4), bc4(bed()), ALU.mult, r=[PT.sub(0, 512), bed], w=[kbd])
            tt(vbeta(h3, h=4), PT()[:, 512:1024].rearrange(h3, h=4), bc4(beta()), ALU.mult, r=[PT.sub(512, 1024), beta],
               w=[vbeta])
            tt(kendb(h3, h=4), PT()[:, 0:512].rearrange(h3, h=4), bc4(eend4), ALU.mult, r=[PT.sub(0, 512), Ecol],
               w=[kendb])
            for h in range(4):
                hs = slice(h * 128, (h + 1) * 128)
                mm(P3()[:, hs], kbd()[:, hs], TinvTb()[:, hs], r=[kbd, TinvTb], w=[P3.sub(h * 128, (h + 1) * 128)])
            S.op('act', lambda e: e.mul(out=negwT(), in_=P3(), mul=-1.0), r=[P3], w=[negwT])
            if not smp:
                for h in range(4):
                    hs = slice(h * 128, (h + 1) * 128)
                    mm(P4p()[:, hs], TinvTb()[:, hs], vbeta()[:, hs], True, False, r=[TinvTb, vbeta], w=[P4p])
                    mm(P4p()[:, hs], negwT()[:, hs], SBb()[:, hs], False, True, r=[negwT, SBb], w=[P4p])
                cp(vnewb(), P4p(), r=[P4p], w=[vnewb], eng='act')
                for h in range(4):
                    hs = slice(h * 128, (h + 1) * 128)
                    mm(P5()[:, hs], vnewb()[:, hs], qkTm()[:, hs], True, False, r=[vnewb, qkTm], w=[P5])
                    mm(P5()[:, hs], SBb()[:, hs], qeTb()[:, hs], False, True, r=[SBb, qeTb], w=[P5])
                for h in range(4):
                    hs = slice(h * 128, (h + 1) * 128)
                    mm(P0()[:, hs], kendb()[:, hs], vnewb()[:, hs], r=[kendb, vnewb], w=[P0.sub(h * 128, (h + 1) * 128)])
                tt(SB(h3, h=4), SB(h3, h=4), bc4(etot), ALU.mult, r=[SB, Ecol], w=[SB])
                tt(SB(), SB(), P0(), ALU.add, r=[SB, P0], w=[SB])
                cp(SBb(), SB(), r=[SB], w=[SBb], eng='act')
                if ti == NT - 1:
                    dma(gdnp.rearrange("h d v -> d h v"), SB(h3, h=4), r=[SB], eng='pool')
            else:
                zerofill(P0)
                zerofill(P5)
                SS4 = [FV(0, 1024), FV(1024, 2048), FV(2048, 3072), FV(3072, 4096)]
                SSbds = [SSbd, HV(2560, 3584)]
                ssi = 0
                for h in range(4):
                    for hf in range(2):
                        SSx = SS4[ssi % 4]
                        SSbd_ = SSbds[ssi % 2]
                        SSb3 = SSbd_("p (s v) -> p s v", s=8)
                        SS3 = SSx("p (s v) -> p s v", s=8)
                        dma(SS3, sgdn[8 * hf:8 * hf + 8, h, :, :].rearrange("s d v -> d s v"), w=[SSx])
                        cp(SSbd_(), SSx(), r=[SSx], w=[SSbd_], eng=('act', 'dve')[ssi % 2])
                        ssi += 1
                        for j in range(8):
                            s = 8 * hf + j
                            cs_ = slice(h * 128 + 8 * s, h * 128 + 8 * s + 8)
                            mm(P0()[:, cs_], SSb3[:, j, :], negwT()[:, cs_], False, (h == 3 and hf == 1 and j == 7),
                               r=[SSbd_, negwT], w=[P0])
                            mm(P5()[:, cs_], SSb3[:, j, :], qeTb()[:, cs_], False, False, r=[SSbd_, qeTb], w=[P5])
                cp(wSTb(), P0(), r=[P0], w=[wSTb], eng='act')
                for h in range(4):
                    hs = slice(h * 128, (h + 1) * 128)
                    mm(P4p()[:, hs], TinvTb()[:, hs], vbeta()[:, hs], True, False, r=[TinvTb, vbeta], w=[P4p])
                    mm(P4p()[:, hs], wSTb()[:, hs], identB(), False, True, r=[wSTb, identB], w=[P4p])
                cp(vnewb(), P4p(), r=[P4p], w=[vnewb], eng='act')
                for h in range(4):
                    hs = slice(h * 128, (h + 1) * 128)
                    mm(P5()[:, hs], vnewb()[:, hs], qkTm()[:, hs], False, h == 3, r=[vnewb, qkTm], w=[P5])
                et3 = Ecol()[:, 8:72].rearrange("p (s h) -> p s h", s=16)
                km3 = kendmd("p (s d) -> p s d", s=8)
                for h in range(4):
                    hs = slice(h * 128, (h + 1) * 128)
                    for hf in range(2):
                        SSx = SS4[ssi % 4]
                        ssi += 1
                        SS3 = SSx("p (s v) -> p s v", s=8)
                        dma(SS3, sgdn[8 * hf:8 * hf + 8, h, :, :].rearrange("s d v -> d s v"), w=[SSx])
                        tt(km3, kendb()[:, hs].unsqueeze(1).broadcast_to([128, 8, 128]),
                           segm()[:, 8 * hf:8 * hf + 8].unsqueeze(2).broadcast_to([128, 8, 128]), ALU.mult,
                           r=[kendb, segm], w=[kendmd])
                        for g in range(2):
                            Pq = ((P0, P1), (P2, P3))[hf][g]
                            for jj in range(4):
                                mm(Pq()[:, jj * 128:(jj + 1) * 128], km3[:, 4 * g + jj, :], vnewb()[:, hs],
                                   r=[kendmd, vnewb], w=[Pq.sub(jj * 128, (jj + 1) * 128)])
                            sl = SS3[:, 4 * g:4 * g + 4, :]
                            s0 = 8 * hf + 4 * g
                            tt(sl, sl, et3[:, s0:s0 + 4, h:h + 1].broadcast_to([128, 4, 128]), ALU.mult,
                               r=[SSx, Ecol], w=[SSx])
                            tt(sl, sl, Pq("p (s v) -> p s v", s=4), ALU.add, r=[SSx, Pq], w=[SSx])
                        dma(gdns[8 * hf:8 * hf + 8, h, :, :].rearrange("s d v -> d s v"), SS3, r=[SSx], eng='pool')

            dump(17, vnewb, 512, bf=True); dump(18, P5, 512, psum=True)
            if DBG_PHASE < 4:
                return
            MARKS.append((ti, 'D', S.ncomp['dve']))
            oTs, rstds = [FV(0, 512), FV(1024, 1536)], [FV(512, 1024), FV(1536, 2048)]
            osqs, szs = [HV(2048, 2560), HV(6144, 6656)], [HV(2560, 3072), HV(6656, 7168)]
            Pos, Pss, Pzs, zc0s = (P6, P5), (P1, P2), (P3, P4p), (1040, 3096)
            for br in range(2):
                act(osqs[br](), Pos[br](), AF.Square, r=[Pos[br]], w=[osqs[br]])
            for br in range(2):
                mm(Pss[br](), onesB(), osqs[br](), r=[onesB, osqs[br]], w=[Pss[br]])
            for br in range(2):
                for c in range(4):
                    proj_fm(Pzs[br], c * 128, zc0s[br] + c * 128)
            for br in range(2):
                act(rstds[br](), Pss[br](), AF.Ln, r=[Pss[br]], w=[rstds[br]], scale=1.0 / 128, bias=EPS)
            for br in range(2):
                act(rstds[br](), rstds[br](), AF.Exp, r=[rstds[br]], w=[rstds[br]], scale=-0.5)
            for br in range(2):
                act(szs[br](), Pzs[br](), AF.Silu, r=[Pzs[br]], w=[szs[br]])
            for br in range(2):
                stt(oTs[br](), Pos[br](), smallc()[:, 8 + br:9 + br], rstds[br](), ALU.mult, ALU.mult,
                    r=[Pos[br], rstds[br], smallc], w=[oTs[br]])
                tt(gT[br](), oTs[br](), szs[br](), ALU.mult, r=[oTs[br], szs[br]], w=[gT[br]])
            dump(19, gT[0], 512, bf=True); dump(20, gT[1], 512, bf=True)
            if DBG_PHASE < 5:
                return
            MARKS.append((ti, 'E', S.ncomp['dve']))
            dma(hres(), xsrc, w=[hres])
            for n in range(2):
                ns = slice(n * 512, (n + 1) * 512)
                Qa, Qb, Qc, Qd = ((P0, P1, P2, P3), (P4p, P5, P6, P0))[n]
                proj_tm(Qc, 0, 3608 + n * 512, 512)
                proj_tm(Qd, 0, 4632 + n * 512, 512)
                for h in range(4):
                    mm(Qa(), gT[0]()[:, h * 128:(h + 1) * 128], Wupa[:, h, ns], h == 0, h == 3, r=[gT[0], WUPA], w=[Qa])
                for h in range(4):
                    mm(Qb(), gT[1]()[:, h * 128:(h + 1) * 128], Wupb[:, h, ns], h == 0, h == 3, r=[gT[1], WUPB], w=[Qb])
                act(sga(), Qc(), AF.Sigmoid, r=[Qc], w=[sga])
                act(sgb(), Qd(), AF.Sigmoid, r=[Qd], w=[sgb])
                tt(sga(), sga(), Qa(), ALU.mult, r=[sga, Qa], w=[sga])
                tt(sgb(), sgb(), Qb(), ALU.mult, r=[sgb, Qb], w=[sgb])
                tt(mg()[:, ns], sga(), sgb(), ALU.add, r=[sga, sgb], w=[mg.sub(n * 512, (n + 1) * 512)])
            if KEEPWARM:
                for _ in range(KEEPWARM):
                    mm(P3(), zB(), Wout[:, 0, 0:512], r=[zB, WOUT], w=[P3])
            for k in range(8):
                tr(PT()[:, k * 128:(k + 1) * 128], mg()[:, k * 128:(k + 1) * 128], identB(), r=[mg, identB],
                   w=[PT.sub(k * 128, (k + 1) * 128)])
            cp(mT(), PT(), r=[PT], w=[mT], eng='act')
            if KEEPWARM:
                for _ in range(KEEPWARM // 2):
                    mm(P3(), zB(), Wout[:, 0, 0:512], r=[zB, WOUT], w=[P3])
            mT3 = mT("p (k t) -> p k t", k=8)
            for n in range(2):
                ns = slice(n * 512, (n + 1) * 512)
                Pn = (P1, P2)[n]
                for kc in range(8):
                    mm(Pn(), mT3[:, kc, :], Wout[:, kc, ns], kc == 0, kc == 7, r=[mT, WOUT], w=[Pn])
                tt(hres()[:, ns], hres()[:, ns], Pn(), ALU.add, r=[hres.sub(n * 512, (n + 1) * 512), Pn],
                   w=[hres.sub(n * 512, (n + 1) * 512)])
            cp(mg(), hres(), r=[hres], w=[mg])
            for k in range(8):
                tr(PT()[:, k * 128:(k + 1) * 128], mg()[:, k * 128:(k + 1) * 128], identB(), r=[mg, identB],
                   w=[PT.sub(k * 128, (k + 1) * 128)])
            cp(mT(), PT(), r=[PT], w=[mT], eng='act')
            PT3 = PTT("p (k t) -> p k t", k=2)
            for n in range(2):
                ns = slice(n * 512, (n + 1) * 512)
                Pa, Pp = ((P0, P1), (P2, P3))[n]
                for kc in range(8):
                    mm(Pa(), mT3[:, kc, :], Wpg[:, kc, ns], kc == 0, kc == 7, r=[mT, WPG], w=[Pa])
                for k in range(2):
                    mm(Pp(), PT3[:, k, :], Wple[:, k, ns], k == 0, k == 1, r=[PTT, WPLE], w=[Pp])
                sp_ = (sga, sgb)[n]
                act(sp_(), Pa(), AF.Sigmoid, r=[Pa], w=[sp_])
                tt(sp_(), sp_(), Pp(), ALU.mult, r=[sp_, Pp], w=[sp_])
                tt(hres()[:, ns], hres()[:, ns], sp_(), ALU.add, r=[hres.sub(n * 512, (n + 1) * 512), sp_],
                   w=[hres.sub(n * 512, (n + 1) * 512)])
            dump(21, hres, 1024)
            if nextA is not None:
                nextA()
            f0, f1, f2 = CO(3, 4), CO(4, 5), CO(5, 6)
            act(mg(), hres(), AF.Square, r=[hres], w=[mg, f0], accum_out=f0())
            act(f1(), f0(), AF.Ln, r=[f0], w=[f1], scale=1.0 / 1024, bias=EPS)
            act(f2(), f1(), AF.Exp, r=[f1], w=[f2], scale=-0.5)
            stt(yt(), hres(), f2(), fnwbc(), ALU.mult, ALU.mult, r=[hres, f2, fnwbc], w=[yt])
            dma(ydst, yt(), r=[yt], eng='pool')

        S.budget = DBG_BUDGET
        tl = list(range(NT + 1) if DBG_TILES is None else DBG_TILES)
        phaseA(tl[0])
        for i_, ti in enumerate(tl):
            tile(ti, (lambda t2=tl[i_ + 1]: phaseA(t2)) if i_ + 1 < len(tl) else None)
        print('op counts', S.ncomp, S.dma_cnt)
        global LAST_MARKS
        LAST_MARKS = MARKS
        S.final_wait_all('sp')
        S.emit()
    return nc


_NC = None


def kernel(x_prompt, x_sample, state_gla, state_gdn, state_conv, p_prompt, p_sample, norm_w, w_in, w_gla_gate,
           b_gla_gate, gla_norm_w, conv_w, gdn_a_log, gdn_dt_bias, gdn_norm_w, w_up_gla, w_up_gdn, w_out,
           w_ple_gate, w_ple, final_norm_w):
    global _NC
    f = lambda a: np.ascontiguousarray(np.asarray(a, dtype=np.float32))
    if _NC is None:
        _NC = build()
    nc = _NC
    shared = {
        "normw": f(norm_w).reshape(1, 1024), "win": f(w_in[0]), "wgg": f(w_gla_gate[0]), "bgg": f(b_gla_gate).reshape(1, 256),
        "glanw": f(gla_norm_w).reshape(1, 128), "convw": f(conv_w[0]), "alog": f(gdn_a_log).reshape(1, 4),
        "dtb": f(gdn_dt_bias).reshape(1, 4), "gdnnw": f(gdn_norm_w).reshape(1, 128), "wupa": f(w_up_gla[0]),
        "wupb": f(w_up_gdn[0]), "wout": f(w_out[0]), "wpg": f(w_ple_gate[0]), "wple": f(w_ple[0]),
        "fnw": f(final_norm_w).reshape(1, 1024),
    }
    xs_ = f(x_sample).reshape(8, 128, 1024)
    ps_ = f(p_sample[0]).reshape(8, 128, 256)
    in_maps = []
    for c in range(8):
        m = dict(shared)
        m["xp"] = f(x_prompt[c])
        m["xsm"] = xs_[c]
        m["pp"] = f(p_prompt[0, c])
        m["psm"] = ps_[c]
        m["sgla"] = f(state_gla[0, 16 * c:16 * c + 16])
        m["sgdn"] = f(state_gdn[0, 16 * c:16 * c + 16])
        m["sconv"] = f(state_conv[0, 16 * c:16 * c + 16]).reshape(48, 1536)
        in_maps.append(m)
    res = run_bass_kernel_spmd(nc, in_maps, core_ids=list(range(8)))
    R = res.results
    y_prompt = np.stack([R[c]["yp"] for c in range(8)], 0)
    y_sample = np.concatenate([R[c]["ysm"].reshape(16, 8, 1024) for c in range(8)], 0)
    gla_p = np.stack([R[c]["glap"] for c in range(8)], 0)[None]
    gdn_p = np.stack([R[c]["gdnp"] for c in range(8)], 0)[None]
    conv_p = np.stack([R[c]["convp"] for c in range(8)], 0)[None]
    gla_s = np.concatenate([R[c]["glas"] for c in range(8)], 0)[None]
    gdn_s = np.concatenate([R[c]["gdns"] for c in range(8)], 0)[None]
    conv_s = np.concatenate([R[c]["convs"] for c in range(8)], 0)[None]
    return (y_prompt, y_sample, gla_p, gdn_p, conv_p, gla_s, gdn_s, conv_s)
```

```python
import numpy as np
from contextlib import ExitStack
import concourse.bass as bass
import concourse.mybir as mybir
from concourse.bass_utils import run_bass_kernel_spmd

F32 = mybir.dt.float32
BF16 = mybir.dt.bfloat16
AF = mybir.ActivationFunctionType
ALU = mybir.AluOpType
EPS = 1e-6
BIG = 30000.0
NT = 16
DBG_TILES = None
DBG_PHASE = 9
DBG_NOW = False
DBG_A = 99
DBG_BUDGET = None
DBG_DUMP = False
DBG_NOILV = False
KEEPWARM = 0
INC = 5656


class V:
    def __init__(self, t, name, lo, hi):
        self.t, self.name, self.lo, self.hi = t, name, lo, hi
        self.key = (name, lo, hi)

    def __call__(self, pat=None, **kw):
        ap = self.t[:, self.lo:self.hi]
        return ap.rearrange(pat, **kw) if pat else ap

    def sub(self, a, b):
        return V(self.t, self.name, self.lo + a, self.lo + b)


PSUM_NAMES = {"P0", "P1", "P2", "P3", "P4", "P5", "P6", "PT"}


def _key(k):
    if isinstance(k, V):
        k = k.key
    if isinstance(k, tuple):
        if k[0] in PSUM_NAMES:
            return (k[0], 0, 1 << 30)
        return k
    return (k, 0, 1 << 30)


class Sched:
    ENGS = ('pe', 'act', 'dve', 'pool', 'sp')

    def __init__(self, nc, es):
        self.nc = nc
        self.streams = {e: [] for e in self.ENGS}
        self.ncomp = {e: 0 for e in self.ENGS}
        self.sems = {e: es.enter_context(nc.semaphore("s_" + e)) for e in self.ENGS}
        self.qsems = {'sp': list(range(0, 8)), 'pool': list(range(8, 14)), 'act': list(range(14, 16))}
        self.dma_sems = [es.enter_context(nc.semaphore("s_dma%d" % i)) for i in range(16)]
        self.dma_cnt = [0] * 16
        self.rr = {'sp': 0, 'pool': 0, 'act': 0}
        self.acc = {}
        self.known = {e: {} for e in self.ENGS}
        self.budget = None
        self._lane = None
        self._stack = []

    def _collect(self, r, w):
        deps = {}

        def add(tok):
            if tok is None:
                return
            s, v = tok
            if deps.get(s, 0) < v:
                deps[s] = v
        for k in r:
            n, lo, hi = _key(k)
            for ent in self.acc.get(n, ()):
                if ent[0] < hi and lo < ent[1]:
                    add(ent[2])
        for k in w:
            n, lo, hi = _key(k)
            for ent in self.acc.get(n, ()):
                if ent[0] < hi and lo < ent[1]:
                    add(ent[2])
                    for t in ent[3]:
                        add(t)
        return deps

    def _finish(self, tok, r, w):
        for k in r:
            n, lo, hi = _key(k)
            hit = False
            for ent in self.acc.setdefault(n, []):
                if ent[0] < hi and lo < ent[1]:
                    ent[3].append(tok)
                    hit = True
            if not hit:
                self.acc[n].append([lo, hi, None, [tok]])
        for k in w:
            n, lo, hi = _key(k)
            lst = self.acc.setdefault(n, [])
            new = []
            for ent in lst:
                if ent[0] < hi and lo < ent[1]:
                    if ent[0] < lo:
                        new.append([ent[0], lo, ent[2], list(ent[3])])
                    if hi < ent[1]:
                        new.append([hi, ent[1], ent[2], list(ent[3])])
                else:
                    new.append(ent)
            new.append([lo, hi, tok, []])
            self.acc[n] = new

    def _waits(self, eng, deps):
        kn = self.known[eng]
        waits = []
        for s, v in deps.items():
            if kn.get(s, 0) < v:
                kn[s] = v
                waits.append((s, v))
        return waits

    @staticmethod
    def _excl(r, w):
        r2, w2 = [], list(w)
        for k in r:
            kk = _key(k)
            if kk[0] in PSUM_NAMES:
                w2.append(k)
            else:
                r2.append(k)
        return r2, w2

    def begin_lane(self):
        self._stack.append(self._lane)
        self._lane = []

    def end_lane(self):
        l = self._lane
        self._lane = self._stack.pop()
        return l

    def emit_list(self, lst):
        for kind, eng, fn, r, w in lst:
            (self.op if kind == 'c' else self.dma)(eng, fn, r, w)

    def merge(self, lanes):
        idx = [0] * len(lanes)
        tot = [max(1, len(l)) for l in lanes]
        while any(idx[i] < len(lanes[i]) for i in range(len(lanes))):
            best = min((i for i in range(len(lanes)) if idx[i] < len(lanes[i])), key=lambda i: idx[i] / tot[i])
            kind, eng, fn, r, w = lanes[best][idx[best]]
            idx[best] += 1
            (self.op if kind == 'c' else self.dma)(eng, fn, r, w)

    def op(self, eng, fn, r=(), w=()):
        if self._lane is not None:
            self._lane.append(('c', eng, fn, r, w))
            return
        r, w = self._excl(r, w)
        if self.budget is not None:
            if self.budget <= 0:
                return
            self.budget -= 1
        deps = self._collect(r, w)
        if eng == 'pe':
            deps.pop('pe', None)
        self.ncomp[eng] += 1
        self.streams[eng].append((fn, self._waits(eng, deps), 'c', None))
        self._finish((eng, self.ncomp[eng]), r, w)

    def dma(self, eng, fn, r=(), w=()):
        if self._lane is not None:
            self._lane.append(('d', eng, fn, r, w))
            return
        if self.budget is not None:
            if self.budget <= 0:
                return
            self.budget -= 1
        deps = self._collect(r, w)
        q = self.qsems[eng]
        i = q[self.rr[eng] % len(q)]
        self.rr[eng] += 1
        sname = 'dma%d' % i
        if self.dma_cnt[i] > 0:
            deps[sname] = max(deps.get(sname, 0), 16 * self.dma_cnt[i])
        self.dma_cnt[i] += 1
        self.streams[eng].append((fn, self._waits(eng, deps), 'd', i))
        self._finish((sname, 16 * self.dma_cnt[i]), r, w)

    def _sem(self, s):
        if s.startswith('dma'):
            return self.dma_sems[int(s[3:])]
        return self.sems[s]

    def final_wait_all(self, eng='sp'):
        waits = []
        for i, c in enumerate(self.dma_cnt):
            if c:
                waits.append(('dma%d' % i, 16 * c))
        for e in self.ENGS:
            if self.ncomp[e] and e != eng:
                waits.append((e, self.ncomp[e]))
        self.streams[eng].append((None, waits, 'w', None))

    def emit(self):
        nc = self.nc
        with nc.Block() as block:
            def mk(ename):
                def body(e):
                    for fn, waits, kind, di in self.streams[ename]:
                        for s, v in waits:
                            e.wait_ge(self._sem(s), v)
                        if fn is None:
                            continue
                        ins = fn(e)
                        if kind == 'c':
                            ins.then_inc(self.sems[ename], 1)
                        else:
                            ins.then_inc(self.dma_sems[di], 16)
                return body
            block.tensor(mk('pe'))
            block.scalar(mk('act'))
            block.vector(mk('dve'))
            block.gpsimd(mk('pool'))
            block.sync(mk('sp'))


def build():
    nc = bass.Bass("TRN2", target_bir_lowering=False)
    di = lambda n, s: nc.dram_tensor(n, s, F32, kind="ExternalInput").ap()
    do = lambda n, s: nc.dram_tensor(n, s, F32, kind="ExternalOutput").ap()
    xp, xsm = di("xp", [2048, 1024]), di("xsm", [128, 1024])
    pp, psm = di("pp", [2048, 256]), di("psm", [128, 256])
    sgla, sgdn, sconv = di("sgla", [16, 4, 64, 128]), di("sgdn", [16, 4, 128, 128]), di("sconv", [48, 1536])
    normw, win = di("normw", [1, 1024]), di("win", [1024, INC])
    wgg, bgg, glanw = di("wgg", [16, 256]), di("bgg", [1, 256]), di("glanw", [1, 128])
    convw, alog, dtb, gdnnw = di("convw", [4, 1536]), di("alog", [1, 4]), di("dtb", [1, 4]), di("gdnnw", [1, 128])
    wupa, wupb = di("wupa", [512, 1024]), di("wupb", [512, 1024])
    wout, wpg, wple = di("wout", [1024, 1024]), di("wpg", [1024, 1024]), di("wple", [256, 1024])
    fnw = di("fnw", [1, 1024])
    yp, ysm = do("yp", [2048, 1024]), do("ysm", [128, 1024])
    glap, gdnp, convp = do("glap", [4, 64, 128]), do("gdnp", [4, 128, 128]), do("convp", [3, 1536])
    glas, gdns, convs = do("glas", [16, 4, 64, 128]), do("gdns", [16, 4, 128, 128]), do("convs", [16, 3, 1536])

    if DBG_DUMP:
        dbgf = nc.dram_tensor("dbgf", [24, 128, 1024], F32, kind="ExternalOutput").ap()
        dbgb = nc.dram_tensor("dbgb", [24, 128, 1024], BF16, kind="ExternalOutput").ap()
    with ExitStack() as es:
        S = Sched(nc, es)
        sbt = lambda n, s, d: es.enter_context(nc.sbuf_tensor(n, s, d))
        pst = lambda n, s, d: es.enter_context(nc.psum_tensor(n, s, d))

        def sv(n, w, d):
            return V(sbt(n, [128, w], d), n, 0, w)

        Win = sbt("Win", [128, 8, INC], BF16)
        Wupa, Wupb = sbt("Wupa", [128, 4, 1024], BF16), sbt("Wupb", [128, 4, 1024], BF16)
        Wout, Wpg = sbt("Wout", [128, 8, 1024], BF16), sbt("Wpg", [128, 8, 1024], BF16)
        Wple = sbt("Wple", [128, 2, 1024], BF16)
        fnwbc = sv("fnwbc", 1024, F32)
        identF, onesF = sv("identF", 128, F32), sv("onesF", 128, F32)
        identB, onesB, zB = sv("identB", 128, BF16), sv("onesB", 128, BF16), sv("zB", 128, BF16)
        Lm = [sv("L_P", 128, F32), sv("L_S", 128, F32)]
        Um = [sv("U_P", 128, F32), sv("U_S", 128, F32)]
        NEG2 = [sv("NEG2_P", 256, F32), sv("NEG2_S", 256, F32)]
        segm = sv("segm", 16, F32)
        hm = sv("hm", 2, F32)
        Wg = sbt("Wg", [32, 256], F32)
        gaug = sbt("gaug", [32, 128], F32)
        cw = sbt("cw", [128, 4, 12], F32)
        smallc = sv("smallc", 32, F32)
        cols = sbt("cols", [128, 256], F32)
        CO = lambda a, b: V(cols, "cols", a, b)
        X, PL = sv("X", 1024, F32), sv("PL", 256, F32)
        XT = sv("XT", 1024, BF16)
        PTTs = [sv("PTT", 256, BF16), sv("PTT2", 256, BF16)]
        XPAD, CARRY = sv("XPAD", 704, F32), sv("CARRY", 36, F32)
        SA, SAb = sv("SA", 256, F32), sv("SAb", 256, BF16)
        SB, SBb = sv("SB", 512, F32), sv("SBb", 512, BF16)
        Ft = sbt("F", [128, 4096], F32)
        Ht = sbt("H", [128, 8192], BF16)
        FV = lambda a, b: V(Ft, "F", a, b)
        HV = lambda a, b: V(Ht, "H", a, b)
        acc, rinv, qnT = FV(0, 512), FV(512, 1024), FV(1024, 1536)
        A4, B4, P4 = FV(1536, 2048), FV(2048, 2560), FV(2560, 3072)
        gbc, lbbc, E12, edbc = FV(3072, 3200), FV(3200, 3328), FV(3328, 3584), FV(3584, 3712)
        lsp, eb, enb, eend = FV(1536, 1792), FV(1792, 2048), FV(2048, 2304), FV(2304, 2560)
        oT, rstd, sga, sgb = FV(0, 512), FV(512, 1024), FV(1024, 1536), FV(1536, 2048)
        hres, yt = FV(2048, 3072), FV(3072, 4096)
        SS = [FV(0, 1024), FV(1024, 2048)]
        scv, tokq = FV(0, 1536), FV(2560, 4096)
        STG = [FV(0, 1024), FV(1024, 2048), FV(2048, 3072), FV(3072, 4096)]
        xs, pb = HV(0, 1024), HV(1792, 2048)
        qeT, keT, kend, va, attTm = HV(0, 512), HV(512, 768), HV(768, 1024), HV(1024, 1536), HV(1536, 2048)
        ebm = FV(2560, 3072)
        SSbd, kendmd = HV(0, 1024), HV(1024, 2048)
        sq, knT, qnTb, vTb = HV(2048, 2560), HV(2560, 3072), HV(3072, 3584), HV(3584, 4096)
        qkTm, TinvTb, kbd, vbeta = HV(4096, 4608), HV(4608, 5120), HV(5120, 5632), HV(5632, 6144)
        kendb, negwT, vnewb, wSTb = HV(6144, 6656), HV(6656, 7168), HV(7168, 7680), HV(7680, 8192)
        qeTb = HV(2048, 2560)
        SSbg, kendmg = HV(2048, 3072), HV(3072, 4096)
        osq, sz, gT = HV(2048, 2560), HV(2560, 3072), [HV(3072, 3584), HV(3584, 4096)]
        mg, mT = HV(4096, 5120), HV(5120, 6144)
        Pb = []
        for i in range(7):
            Pb.append(V(pst("P%d" % i, [128, 512], F32), "P%d" % i, 0, 512))
        PT = V(pst("PT", [128, 1024], BF16), "PT", 0, 1024)

        def mm(out, lhsT, rhs, start=True, stop=True, r=(), w=()):
            S.op('pe', lambda e: e.matmul(out, lhsT=lhsT, rhs=rhs, start=start, stop=stop), r, w)

        def tr(out, in_, ident, r=(), w=()):
            S.op('pe', lambda e: e.transpose(out=out, in_=in_, identity=ident), r, w)

        def act(out, in_, func, r=(), w=(), **kw):
            S.op('act', lambda e: e.activation(out=out, in_=in_, func=func, **kw), r, w)

        def tt(out, in0, in1, op, r=(), w=(), eng='dve'):
            S.op(eng, lambda e: e.tensor_tensor(out=out, in0=in0, in1=in1, op=op), r, w)

        def ts(out, in0, s1, op0, r=(), w=(), s2=None, op1=None, eng='dve'):
            if op1 is None:
                S.op(eng, lambda e: e.tensor_scalar(out=out, in0=in0, scalar1=s1, scalar2=None, op0=op0), r, w)
            else:
                S.op(eng, lambda e: e.tensor_scalar(out=out, in0=in0, scalar1=s1, scalar2=s2, op0=op0, op1=op1), r, w)

        def stt(out, in0, sc, in1, op0, op1, r=(), w=(), eng='dve'):
            S.op(eng, lambda e: e.scalar_tensor_tensor(out=out, in0=in0, scalar=sc, in1=in1, op0=op0, op1=op1), r, w)

        def cp(out, in_, r=(), w=(), eng='dve'):
            if eng == 'act':
                S.op('act', lambda e: e.copy(out=out, in_=in_), r, w)
            else:
                S.op(eng, lambda e: e.tensor_copy(out=out, in_=in_), r, w)

        def dma(out, in_, r=(), w=(), eng='sp', slow=False):
            if slow:
                S.dma(eng, lambda e: e.dma_start(out=out, in_=in_, allow_slow_non_contiguous=True), r, w)
            else:
                S.dma(eng, lambda e: e.dma_start(out=out, in_=in_), r, w)

        def dump(slot, v, n, bf=False, psum=False):
            if not DBG_DUMP:
                return
            if psum:
                cp(yt()[:, 0:n], v()[:, 0:n], r=[v], w=[yt])
                dma(dbgf[slot, :, 0:n], yt()[:, 0:n], r=[yt], eng='pool')
            elif bf:
                dma(dbgb[slot, :, 0:n], v()[:, 0:n], r=[v], eng='pool')
            else:
                dma(dbgf[slot, :, 0:n], v()[:, 0:n], r=[v], eng='pool')

        def zerofill(P):
            mm(P(), zB(), XT()[:, 0:512], True, False, r=[zB, XT], w=[P])

        def ms(v, val, eng='pool'):
            S.op(eng, lambda e: e.memset(v(), val), w=[v])

        def asel(v, pattern, base, cm, cmp, fill, view=None):
            ap = v() if view is None else view
            S.op('pool', lambda e: e.affine_select(out=ap, in_=ap, pattern=pattern, compare_op=cmp, fill=fill,
                                                   base=base, channel_multiplier=cm), r=[v], w=[v])
        t0_ = (list(range(NT + 1)) if DBG_TILES is None else list(DBG_TILES))[0]
        dma(X(), xsm[:, :] if t0_ == NT else xp[t0_ * 128:(t0_ + 1) * 128, :], w=[X])
        dma(PL(), psm[:, :] if t0_ == NT else pp[t0_ * 128:(t0_ + 1) * 128, :], w=[PL])
        PRELOADED = [t0_]
        ms(identF, 0.0)
        asel(identF, [[-1, 128]], 0, 1, ALU.not_equal, 1.0)
        ms(onesF, 1.0)
        cp(identB(), identF(), r=[identF], w=[identB])
        ms(onesB, 1.0)
        ms(zB, 0.0)
        ms(Lm[0], 1.0)
        asel(Lm[0], [[1, 128]], 0, -1, ALU.is_ge, 0.0)
        ms(Um[0], 1.0)
        asel(Um[0], [[-1, 128]], 0, 1, ALU.is_gt, 0.0)
        same = FV(0, 128)
        ms(same, 1.0)
        sview = same("p (a r) -> p a r", a=16)
        asel(same, [[-8, 16], [0, 8]], 0, 1, ALU.is_ge, 0.0, view=sview)
        asel(same, [[8, 16], [0, 8]], 7, -1, ALU.is_ge, 0.0, view=sview)
        ms(segm, 1.0)
        asel(segm, [[-8, 16]], 0, 1, ALU.is_ge, 0.0)
        asel(segm, [[8, 16]], 7, -1, ALU.is_ge, 0.0)
        ms(hm, 1.0)
        asel(hm, [[-64, 2]], 63, -1, ALU.is_ge, 0.0)
        hm1 = FV(256, 258)
        ms(hm1, 1.0)
        asel(hm1, [[64, 2]], -64, 1, ALU.is_ge, 0.0)
        S.op('dve', lambda e: e.tensor_copy(out=hm()[:, 1:2], in_=hm1()[:, 0:1]), r=[hm1, hm], w=[hm])
        tt(Lm[1](), Lm[0](), same(), ALU.mult, r=[Lm[0], same], w=[Lm[1]])
        tt(Um[1](), Um[0](), same(), ALU.mult, r=[Um[0], same], w=[Um[1]])
        ind = FV(128, 256)
        for v in range(2):
            ms(ind, 1.0)
            asel(ind, [[1, 128]], 0, -1, ALU.is_gt, 0.0)
            if v == 1:
                tt(ind(), ind(), same(), ALU.mult, r=[ind, same], w=[ind])
            ts(NEG2[v]()[:, 0:128], ind(), 1.0, ALU.subtract, s2=BIG, op1=ALU.mult, r=[ind], w=[NEG2[v].sub(0, 128)])
            ts(NEG2[v]()[:, 128:256], Lm[v](), 1.0, ALU.subtract, s2=BIG, op1=ALU.mult, r=[Lm[v]],
               w=[NEG2[v].sub(128, 256)])
        S.op('pool', lambda e: e.memset(gaug[:], 1.0), w=["gaug"])
        S.op('dve', lambda e: e.memset(Wg[:], 0.0), w=["Wg"])
        dma(Wg[0:16, :], wgg[:, :], r=["Wg"], w=["Wg"])
        dma(Wg[16:17, :], bgg[0:1, :], r=["Wg"], w=["Wg"])
        PR2, PRc = FV(1536, 1664), FV(0, 1536)
        dma(Ft[0:8, 1536:1664], normw[0].rearrange("(k p) -> k p", p=128), w=[PR2])
        dma(Ft[8:9, 1536:1664], glanw[0:1, :], r=[PR2], w=[PR2])
        dma(Ft[9:10, 1536:1664], gdnnw[0:1, :], r=[PR2], w=[PR2])
        dma(Ft[0:4, 0:1536], convw[:, :], w=[PRc])
        tr(Pb[0]()[:, 0:10], Ft[0:10, 1536:1664], identF.t[0:10, 0:10], r=[PR2, identF], w=[Pb[0]])
        cp(smallc()[:, 0:10], Pb[0]()[:, 0:10], r=[Pb[0]], w=[smallc.sub(0, 10)])
        for k in range(12):
            tr(Pb[1]()[:, 4 * k:4 * k + 4], Ft[0:4, k * 128:(k + 1) * 128], identF.t[0:4, 0:4], r=[PRc, identF], w=[Pb[1]])
        cp(cw[:].rearrange("p w k -> p k w"), Pb[1]()[:, 0:48].rearrange("p (k w) -> p k w", w=4), r=[Pb[1]], w=["cw"])
        dma(smallc()[:, 12:16], dtb[0:1, :].partition_broadcast(128), w=[smallc.sub(12, 16)])
        dma(smallc()[:, 16:20], alog[0:1, :].partition_broadcast(128), w=[smallc.sub(16, 20)])
        act(smallc()[:, 16:20], smallc()[:, 16:20], AF.Exp, r=[smallc.sub(16, 20)], w=[smallc.sub(16, 20)])
        ts(smallc()[:, 16:20], smallc()[:, 16:20], -1.0, ALU.mult, r=[smallc.sub(16, 20)], w=[smallc.sub(16, 20)])
        dma(fnwbc(), fnw[0:1, :].partition_broadcast(128), w=[fnwbc])
        ms(SA, 0.0), ms(SAb, 0.0), ms(SB, 0.0), ms(SBb, 0.0), ms(CARRY, 0.0)

        if not DBG_NOW:
            for (c0, c1) in ((0, 1552), (1552, 3608), (3608, INC)):
                for kc in range(8):
                    dma(Win[:, kc, c0:c1], win[kc * 128:(kc + 1) * 128, c0:c1],
                        w=[("Win", kc * INC + c0, kc * INC + c1)], eng='pool')
            dma(Wupa[:], wupa.rearrange("(h p) n -> p h n", p=128), w=["Wupa"], eng='pool')
            dma(Wupb[:], wupb.rearrange("(h p) n -> p h n", p=128), w=["Wupb"], eng='pool')
            for hk in range(2):
                dma(Wout[:, 4 * hk:4 * hk + 4, :], wout[512 * hk:512 * hk + 512, :].rearrange("(k p) n -> p k n", p=128),
                    w=[("Wout", 4 * hk, 4 * hk + 4)], eng='pool')
            for hk in range(2):
                dma(Wpg[:, 4 * hk:4 * hk + 4, :], wpg[512 * hk:512 * hk + 512, :].rearrange("(k p) n -> p k n", p=128),
                    w=[("Wpg", 4 * hk, 4 * hk + 4)], eng='pool')
            dma(Wple[:], wple.rearrange("(k p) n -> p k n", p=128), w=["Wple"], eng='pool')
        WIN, WUPA, WUPB, WOUT, WPG, WPLE = "Win", "Wupa", "Wupb", "Wout", "Wpg", "Wple"

        MARKS = []

        def phaseA(ti):
            smp = (ti == NT)
            xsrc = xsm[:, :] if smp else xp[ti * 128:(ti + 1) * 128, :]
            psrc = psm[:, :] if smp else pp[ti * 128:(ti + 1) * 128, :]
            PTT = PTTs[ti % 2]
            MARKS.append((ti, 'A', S.ncomp['dve']))
            if ti in PRELOADED:
                PRELOADED.remove(ti)
            else:
                dma(X(), xsrc, w=[X])
                dma(PL(), psrc, w=[PL])
            c0_, c1_, c2_ = CO(0, 1), CO(1, 2), CO(2, 3)
            act(xs(), X(), AF.Square, r=[X], w=[xs, c0_], accum_out=c0_())
            act(c1_(), c0_(), AF.Ln, r=[c0_], w=[c1_], scale=1.0 / 1024, bias=EPS)
            act(c2_(), c1_(), AF.Exp, r=[c1_], w=[c2_], scale=-0.5)
            ts(xs(), X(), c2_(), ALU.mult, r=[X, c2_], w=[xs])
            for k in range(8):
                tr(PT()[:, k * 128:(k + 1) * 128], xs()[:, k * 128:(k + 1) * 128], identB(), r=[xs, identB],
                   w=[PT.sub(k * 128, (k + 1) * 128)])
            tt(XT("p (k t) -> p k t", k=8), PT("p (k t) -> p k t", k=8),
               smallc()[:, 0:8].unsqueeze(2).broadcast_to([128, 8, 128]), ALU.mult, r=[PT, smallc], w=[XT])
            cp(pb(), PL(), r=[PL], w=[pb])
            for k in range(2):
                tr(PT()[:, k * 128:(k + 1) * 128], pb()[:, k * 128:(k + 1) * 128], identB(), r=[pb, identB],
                   w=[PT.sub(k * 128, (k + 1) * 128)])
            cp(PTT(), PT()[:, 0:256], r=[PT.sub(0, 256)], w=[PTT])

            if (not smp) and ti != NT - 1 and not DBG_NOILV:
                XT3a = XT("p (k t) -> p k t", k=8)
                for g in range(3):
                    for c in range(4):
                        c0 = 1552 + (4 * g + c) * 128
                        for kc in range(8):
                            mm(Pb[g]()[:, c * 128:(c + 1) * 128], Win[:, kc, c0:c0 + 128], XT3a[:, kc, :], kc == 0, kc == 7,
                               r=[XT, ("Win", kc * INC + c0, kc * INC + c0 + 128)], w=[Pb[g]])
                for kc in range(8):
                    mm(Pb[3]()[0:16, 0:128], Win[:, kc, 1024:1040], XT3a[:, kc, :], kc == 0, kc == 7,
                       r=[XT, ("Win", kc * INC + 1024, kc * INC + 1040)], w=[Pb[3]])
                for c in range(4):
                    for kc in range(8):
                        mm(Pb[5]()[:, c * 128:(c + 1) * 128], Win[:, kc, c * 128:(c + 1) * 128], XT3a[:, kc, :], kc == 0, kc == 7,
                           r=[XT, ("Win", kc * INC + c * 128, kc * INC + (c + 1) * 128)], w=[Pb[5]])

        def tile(ti, nextA=None):
            smp = (ti == NT)
            v = 1 if smp else 0
            L, U, NG = Lm[v], Um[v], NEG2[v]
            xsrc = xsm[:, :] if smp else xp[ti * 128:(ti + 1) * 128, :]
            psrc = psm[:, :] if smp else pp[ti * 128:(ti + 1) * 128, :]
            ydst = ysm[:, :] if smp else yp[ti * 128:(ti + 1) * 128, :]
            P0, P1, P2, P3, P4p, P5, P6 = Pb
            W = lambda c0, n: Win[:, :, c0:c0 + n]

            PTT = PTTs[ti % 2]
            XT3 = XT("p (k t) -> p k t", k=8)
            dump(0, XT, 1024, bf=True)

            def proj_fm(Pv, off, c0, M=128):
                for kc in range(8):
                    mm(Pv()[0:M, off:off + 128], Win[:, kc, c0:c0 + M], XT3[:, kc, :], kc == 0, kc == 7,
                       r=[XT, ("Win", kc * INC + c0, kc * INC + c0 + M)], w=[Pv.sub(off, off + 128)])

            def proj_tm(Pv, off, c0, n):
                for kc in range(8):
                    mm(Pv()[:, off:off + n], XT3[:, kc, :], Win[:, kc, c0:c0 + n], kc == 0, kc == 7,
                       r=[XT, ("Win", kc * INC + c0, kc * INC + c0 + n)], w=[Pv.sub(off, off + n)])

            MARKS.append((ti, 'B', S.ncomp['dve']))
            G0, G1, G2 = P3, P4p, P5
            ilv = (not smp) and ti != NT - 1 and not DBG_NOILV
            if ilv:
                S.begin_lane()
            if not ilv:
                proj_fm(G0, 0, 1024, M=16)
            S.op('dve', lambda e: e.tensor_copy(out=gaug[0:16, :], in_=G0()[0:16, 0:128]), r=[G0.sub(0, 128)], w=["gaug"])
            mm(G0()[:, 128:384], gaug[0:17, :], Wg[0:17, :], r=["gaug", "Wg"], w=[G0.sub(128, 384)])
            act(lsp(), G0()[:, 128:384], AF.Exp, r=[G0.sub(128, 384)], w=[lsp], scale=-1.0)
            act(lsp(), lsp(), AF.Ln, r=[lsp], w=[lsp], bias=1.0)
            for dc in range(2):
                mm(G1()[:, dc * 128:(dc + 1) * 128], lsp()[:, dc * 128:(dc + 1) * 128], L(), r=[lsp, L],
                   w=[G1.sub(dc * 128, (dc + 1) * 128)])
            mm(G1()[:, 256:512], U(), lsp(), r=[lsp, U], w=[G1.sub(256, 512)])
            act(eb(), G1()[:, 0:256], AF.Exp, r=[G1.sub(0, 256)], w=[eb], scale=-1.0 / 16)
            act(enb(), G1()[:, 0:256], AF.Exp, r=[G1.sub(0, 256)], w=[enb], scale=1.0 / 16)
            act(eend(), G1()[:, 256:512], AF.Exp, r=[G1.sub(256, 512)], w=[eend], scale=-1.0 / 16)
            dump(1, lsp, 256); dump(2, eb, 256); dump(3, eend, 256)
            for c in range(4):
                if not ilv:
                    proj_fm(G2, c * 128, c * 128)
            dump(4, G2, 512, psum=True)
            tt(ebm("p (c h t) -> p c h t", c=2, h=2), eb("p (c t) -> p c t", c=2).unsqueeze(2).broadcast_to([128, 2, 2, 128]),
               hm().unsqueeze(1).unsqueeze(3).broadcast_to([128, 2, 2, 128]), ALU.mult, r=[eb, hm], w=[ebm])
            for hh in range(2):
                stt(qeT("p (c h t) -> p c h t", c=2, h=2)[:, :, hh, :],
                    G2()[:, 0:256].rearrange("p (c t) -> p c t", c=2), 0.125,
                    ebm("p (c h t) -> p c h t", c=2, h=2)[:, :, hh, :], ALU.mult, ALU.mult, r=[G2.sub(0, 256), ebm], w=[qeT])
            tt(keT(), G2()[:, 256:512], enb(), ALU.mult, r=[G2.sub(256, 512), enb], w=[keT])
            proj_tm(P3, 0, 256, 512)
            proj_tm(P4p, 0, 768, 256)
            tt(kend(), P3()[:, 0:256], eend(), ALU.mult, r=[P3.sub(0, 256), eend], w=[kend])
            cp(va()[:, 0:256], P3()[:, 256:512], r=[P3.sub(256, 512)], w=[va.sub(0, 256)], eng='act')
            cp(va()[:, 256:512], P4p()[:, 0:256], r=[P4p.sub(0, 256)], w=[va.sub(256, 512)], eng='act')
            for h in range(4):
                dc, hp = h // 2, 64 * (h % 2)
                mm(P5()[:, h * 128:(h + 1) * 128], keT()[:, dc * 128:(dc + 1) * 128],
                   qeT()[:, h * 128:(h + 1) * 128], r=[keT, qeT], w=[P5.sub(h * 128, (h + 1) * 128)])
            tt(attTm("p (h c) -> p h c", h=4), P5("p (h c) -> p h c", h=4),
               L().unsqueeze(1).broadcast_to([128, 4, 128]), ALU.mult, r=[P5, L], w=[attTm])
            dump(5, qeT, 512, bf=True); dump(6, keT, 256, bf=True); dump(7, kend, 256, bf=True); dump(8, va, 512, bf=True); dump(9, attTm, 512, bf=True)
            zerofill(P6)
            if not smp:
                for h in range(4):
                    dc, hp = h // 2, 64 * (h % 2)
                    mm(P6()[:, h * 128:(h + 1) * 128], va()[:, h * 128:(h + 1) * 128], attTm()[:, h * 128:(h + 1) * 128],
                       False, False, r=[va, attTm], w=[P6])
                    mm(P6()[:, h * 128:(h + 1) * 128], SAb()[:, dc * 128:(dc + 1) * 128],
                       qeT()[:, h * 128:(h + 1) * 128], False, h == 3, r=[SAb, qeT], w=[P6])
                for h in range(4):
                    dc, hp = h // 2, 64 * (h % 2)
                    mm(P4p()[hp:hp + 64, 256 + dc * 128:256 + (dc + 1) * 128], kend()[:, h * 64:(h + 1) * 64],
                       va()[:, h * 128:(h + 1) * 128], r=[kend, va], w=[P4p.sub(256, 512)])
                for dc in range(2):
                    stt(SA()[:, dc * 128:(dc + 1) * 128], SA()[:, dc * 128:(dc + 1) * 128],
                        eb()[:, dc * 128 + 127:dc * 128 + 128], P4p()[:, 256 + dc * 128:256 + (dc + 1) * 128],
                        ALU.mult, ALU.add, r=[SA, eb, P4p.sub(256, 512)], w=[SA])
                cp(SAb(), SA(), r=[SA], w=[SAb], eng='act')
                if ti == NT - 1:
                    dma(glap.rearrange("(c h) d v -> (h d) c v", c=2), SA("p (c v) -> p c v", c=2), r=[SA], eng='pool')
            else:
                for h in range(4):
                    mm(P6()[:, h * 128:(h + 1) * 128], va()[:, h * 128:(h + 1) * 128], attTm()[:, h * 128:(h + 1) * 128],
                       False, False, r=[va, attTm], w=[P6])
                eb4 = eb("p (c s t) -> p c s t", c=2, s=16)
                SSg = [FV(0, 1024), FV(3072, 4096)]
                for dc in range(2):
                    for hf in range(2):
                        SSx = SSg[hf]
                        SS3 = SSx("p (s v) -> p s v", s=8)
                        dma(SS3, sgla[8 * hf:8 * hf + 8, 2 * dc:2 * dc + 2, :, :].rearrange("s h d v -> (h d) s v"),
                            w=[SSx])
                        SSbg_ = (SSbg, HV(4096, 5120))[(2 * dc + hf) % 2]
                        cp(SSbg_(), SSx(), r=[SSx], w=[SSbg_], eng=('act', 'dve')[(2 * dc + hf) % 2])
                        SSb3 = SSbg_("p (s v) -> p s v", s=8)
                        for hh in range(2):
                            h, hp = 2 * dc + hh, 64 * hh
                            for j in range(8):
                                s = 8 * hf + j
                                mm(P6()[:, h * 128 + 8 * s:h * 128 + 8 * s + 8], SSb3[:, j, :],
                                   qeT()[:, h * 128 + 8 * s:h * 128 + 8 * s + 8], False,
                                   (dc == 1 and hf == 1 and hh == 1 and j == 7), r=[SSbg_, qeT], w=[P6])
                        km3 = kendmg("p (s d) -> p s d", s=8)
                        tt(km3, kend()[:, dc * 128:(dc + 1) * 128].unsqueeze(1).broadcast_to([128, 8, 128]),
                           segm()[:, 8 * hf:8 * hf + 8].unsqueeze(2).broadcast_to([128, 8, 128]), ALU.mult,
                           r=[kend, segm], w=[kendmg])
                        for g in range(2):
                            Pq = (G0, G1)[g]
                            for jj in range(4):
                                j = 4 * g + jj
                                for hh in range(2):
                                    h, hp = 2 * dc + hh, 64 * hh
                                    mm(Pq()[hp:hp + 64, jj * 128:(jj + 1) * 128], km3[:, j, hp:hp + 64],
                                       va()[:, h * 128:(h + 1) * 128], r=[kendmg, va], w=[Pq])
                            sl = SS3[:, 4 * g:4 * g + 4, :]
                            s0 = 8 * hf + 4 * g
                            tt(sl, sl, eb4[:, dc, s0:s0 + 4, 7:8].broadcast_to([128, 4, 128]), ALU.mult,
                               r=[SSx, eb], w=[SSx])
                            tt(sl, sl, Pq("p (s v) -> p s v", s=4), ALU.add, r=[SSx, Pq], w=[SSx])
                        dma(glas[8 * hf:8 * hf + 8, 2 * dc:2 * dc + 2, :, :].rearrange("s h d v -> (h d) s v"), SS3,
                            r=[SSx], eng='pool')

            dump(10, P6, 512, psum=True)
            MARKS.append((ti, 'C', S.ncomp['dve']))
            laneB = S.end_lane() if ilv else None
            if ilv:
                S.begin_lane()
            if smp or ti == NT - 1:
                for c3 in range(3):
                    Pc = (P3, P4p, P3)[c3]
                    proj_tm(Pc, 0, 1552 + c3 * 512, 512)
                    cp(tokq()[:, c3 * 512:(c3 + 1) * 512], Pc(), r=[Pc], w=[tokq.sub(c3 * 512, (c3 + 1) * 512)], eng='act')
                if smp:
                    for s in range(16):
                        dma(convs[s], tokq.t[8 * s + 5:8 * s + 8, 2560:4096], r=[tokq], eng='pool')
                else:
                    dma(convp[:, :], tokq.t[125:128, 2560:4096], r=[tokq], eng='pool')
            if smp:
                dma(scv.t[0:48, 0:1536], sconv[:, :], w=[scv])
                for k in range(12):
                    Px, c = (P5, k) if k < 8 else (P4p, k - 8)
                    tr(Px()[:, c * 48:(c + 1) * 48], scv.t[0:48, k * 128:(k + 1) * 128], identF.t[0:48, 0:48],
                       r=[scv, identF], w=[Px])
            PGT = P0 if ilv else P2
            S.begin_lane()
            proj_tm(PGT, 0, 3088, 8)
            t1, gcol, l2, beta = CO(8, 12), CO(12, 16), CO(16, 20), CO(20, 24)
            negdec, bed, Ecol, gm = CO(24, 28), CO(28, 32), CO(32, 104), CO(104, 168)
            tt(t1(), PGT()[:, 0:4], smallc()[:, 12:16], ALU.add, r=[PGT.sub(0, 8), smallc], w=[t1])
            act(t1(), t1(), AF.Exp, r=[t1], w=[t1])
            act(t1(), t1(), AF.Ln, r=[t1], w=[t1], bias=1.0)
            tt(gcol(), t1(), smallc()[:, 16:20], ALU.mult, r=[t1, smallc], w=[gcol])
            act(l2(), PGT()[:, 4:8], AF.Exp, r=[PGT.sub(0, 8)], w=[l2], scale=-1.0)
            act(l2(), l2(), AF.Ln, r=[l2], w=[l2], bias=1.0)
            act(beta(), l2(), AF.Exp, r=[l2], w=[beta], scale=-1.0)
            mm(PGT()[:, 8:12], L(), gcol(), r=[L, gcol], w=[PGT.sub(8, 12)])
            mm(PGT()[:, 12:16], U(), gcol(), r=[U, gcol], w=[PGT.sub(12, 16)])
            if not smp:
                mm(PGT()[:, 16:20], onesF(), gcol(), r=[onesF, gcol], w=[PGT.sub(16, 20)])
                ne = 12
            else:
                tt(gm("p (s h) -> p s h", s=16), gcol().unsqueeze(1).broadcast_to([128, 16, 4]),
                   segm().unsqueeze(2).broadcast_to([128, 16, 4]), ALU.mult, r=[gcol, segm], w=[gm])
                mm(PGT()[:, 16:80], onesF(), gm(), r=[onesF, gm], w=[PGT.sub(16, 80)])
                ne = 72
            act(Ecol()[:, 0:ne], PGT()[:, 8:8 + ne], AF.Exp, r=[PGT.sub(8, 8 + ne)], w=[Ecol])
            edec, eend4, etot = Ecol()[:, 0:4], Ecol()[:, 4:8], Ecol()[:, 8:ne]
            ts(negdec(), PGT()[:, 8:12], -1.0, ALU.mult, r=[PGT.sub(8, 12)], w=[negdec])
            tt(bed(), beta(), edec, ALU.mult, r=[beta, Ecol], w=[bed])
            for g in range(3):
                Pg = (P0, P1, P2)[g]
                if ilv:
                    continue
                for c in range(4):
                    proj_fm(Pg, c * 128, 1552 + (4 * g + c) * 128)
            laneGate = S.end_lane()
            if not ilv:
                S.emit_list(laneGate)
            stages = {}
            accs, rinvs = [FV(0, 512), FV(3072, 3584)], [FV(512, 1024), FV(3584, 4096)]
            for g in range(3):
                Pg = (P0, P1, P2)[g]
                acc, rinv = accs[g % 2], rinvs[g % 2]
                S.begin_lane()
                if not smp:
                    xp3 = XPAD()[:, 0:524].rearrange("p (c t) -> p c t", c=4)
                    car = CARRY("p (k w) -> p k w", k=12)[:, 4 * g:4 * g + 4, :]
                    cp(xp3[:, :, 0:3], car, r=[CARRY], w=[XPAD])
                    cp(xp3[:, :, 3:131], Pg("p (c t) -> p c t", c=4), r=[Pg], w=[XPAD], eng='act')
                    cp(car, xp3[:, :, 128:131], r=[XPAD], w=[CARRY])
                else:
                    xp4 = XPAD("p (c s t) -> p c s t", c=4, s=16)
                    Px, po = ((P5, 0), (P5, 192), (P4p, 0))[g]
                    cp(xp4[:, :, :, 0:3], Px()[:, po:po + 192].rearrange("p (c s w) -> p c s w", c=4, s=16), r=[Px], w=[XPAD])
                    cp(xp4[:, :, :, 3:11], Pg("p (c s t) -> p c s t", c=4, s=16), r=[Pg], w=[XPAD], eng='act')
                stages[g, 0] = S.end_lane()
                S.begin_lane()
                tmp = rinv
                for w_ in range(4):
                    cwv = cw[:, w_, 4 * g:4 * g + 4]
                    if not smp:
                        a3 = acc("p (c t) -> p c t", c=4)
                        t3 = tmp("p (c t) -> p c t", c=4)
                        xw = xp3[:, :, w_:w_ + 128]
                        bcw = cwv.unsqueeze(2).broadcast_to([128, 4, 128])
                    else:
                        a3 = acc("p (c s t) -> p c s t", c=4, s=16)
                        t3 = tmp("p (c s t) -> p c s t", c=4, s=16)
                        xw = xp4[:, :, :, w_:w_ + 8]
                        bcw = cwv.unsqueeze(2).unsqueeze(3).broadcast_to([128, 4, 16, 8])
                    if w_ == 0:
                        tt(a3, xw, bcw, ALU.mult, r=[XPAD, "cw"], w=[acc])
                    else:
                        tt(t3, xw, bcw, ALU.mult, r=[XPAD, "cw"], w=[tmp])
                        tt(a3, a3, t3, ALU.add, r=[acc, tmp], w=[acc])
                stages[g, 1] = S.end_lane()
                S.begin_lane()
                if g == 2:
                    act(vTb(), acc(), AF.Silu, r=[acc], w=[vTb])
                    stages[g, 2] = S.end_lane()
                    S.begin_lane()
                else:
                    act(acc(), acc(), AF.Silu, r=[acc], w=[acc])
                    tt(sq(), acc(), acc(), ALU.mult, r=[acc], w=[sq])
                    mm(Pg(), onesB(), sq(), r=[onesB, sq], w=[Pg])
                    stages[g, 2] = S.end_lane()
                    S.begin_lane()
                    act(rinv(), Pg(), AF.Ln, r=[Pg], w=[rinv], bias=EPS)
                    act(rinv(), rinv(), AF.Exp, r=[rinv], w=[rinv], scale=-0.5)
                    if g == 0:
                        stt(qnT(), acc(), 128.0 ** -0.5, rinv(), ALU.mult, ALU.mult, r=[acc, rinv], w=[qnT])
                    else:
                        tt(knT(), acc(), rinv(), ALU.mult, r=[acc, rinv], w=[knT])
                stages[g, 3] = S.end_lane()
            pre_keys = [(0, 0), (0, 1)] if ilv else []
            for key in [(0, 0), (0, 1), (1, 0), (0, 2), (1, 1), (0, 3), (2, 0), (1, 2), (2, 1), (1, 3), (2, 2), (2, 3)]:
                if key == (2, 0) and ilv:
                    S.emit_list(laneGate)
                if key not in pre_keys:
                    S.emit_list(stages[key])
            acc, rinv = accs[0], rinvs[0]
            cp(qnTb(), qnT(), r=[qnT], w=[qnTb], eng='act')
            if ilv:
                laneC = S.end_lane()
                for key in pre_keys:
                    S.emit_list(stages[key])
                S.merge([laneB, laneC])
            dump(11, qnT, 512); dump(12, knT, 512, bf=True); dump(13, vTb, 512, bf=True)
            for h in range(4):
                tr(PT()[:, h * 128:(h + 1) * 128], knT()[:, h * 128:(h + 1) * 128], identB(), r=[knT, identB],
                   w=[PT.sub(h * 128, (h + 1) * 128)])
                tr(PT()[:, 512 + h * 128:512 + (h + 1) * 128], vTb()[:, h * 128:(h + 1) * 128], identB(),
                   r=[vTb, identB], w=[PT.sub(512 + h * 128, 512 + (h + 1) * 128)])
            MARKS.append((ti, 'C8', S.ncomp['dve']))
            gbcs, lbbcs = [FV(3072, 3200), FV(3712, 3840)], [FV(3200, 3328), FV(3840, 3968)]
            edbcs, E12s = [FV(3584, 3712), FV(3968, 4096)], [FV(512, 768), FV(768, 1024)]
            hst = {}
            for h in range(4):
                RB = (P0, P1)[h % 2]
                KQ = (P2, P4p)[h % 2]
                gbc, lbbc, edbc, E12 = gbcs[h % 2], lbbcs[h % 2], edbcs[h % 2], E12s[h % 2]
                hs = slice(h * 128, (h + 1) * 128)
                S.begin_lane()
                cp(gbc(), gcol()[:, h:h + 1].broadcast_to([128, 128]), r=[gcol], w=[gbc])
                ts(lbbc(), l2()[:, h:h + 1].broadcast_to([128, 128]), -1.0, ALU.mult, r=[l2], w=[lbbc])
                mm(RB()[:, 0:128], gbc(), L(), True, True, r=[gbc, L], w=[RB])
                mm(RB()[:, 128:256], lbbc(), identF(), True, True, r=[lbbc, identF], w=[RB])
                mm(KQ()[:, 0:128], knT()[:, hs], knT()[:, hs], r=[knT], w=[KQ])
                mm(KQ()[:, 128:256], knT()[:, hs], qnTb()[:, hs], r=[knT, qnTb], w=[KQ])
                hst[h, 0] = S.end_lane()
                S.begin_lane()
                tt(E12()[:, 128:256], RB()[:, 0:128], NG()[:, 128:256], ALU.add, r=[RB, NG], w=[E12])
                tt(E12()[:, 0:128], RB()[:, 128:256], NG()[:, 0:128], ALU.add, r=[RB, NG], w=[E12])
                tt(E12()[:, 0:128], E12()[:, 0:128], RB()[:, 0:128], ALU.add, r=[RB, E12], w=[E12])
                act(E12(), E12(), AF.Exp, r=[E12, negdec], w=[E12], bias=negdec()[:, h:h + 1])
                act(edbc(), RB()[:, 0:128], AF.Exp, r=[RB], w=[edbc])
                tt(qeTb()[:, hs], qnT()[:, hs], edbc(), ALU.mult, r=[qnT, edbc], w=[qeTb.sub(h * 128, (h + 1) * 128)])
                hst[h, 1] = S.end_lane()
                S.begin_lane()
                tt(B4()[:, hs], KQ()[:, 0:128], E12()[:, 0:128], ALU.mult, r=[KQ, E12],
                   w=[B4.sub(h * 128, (h + 1) * 128)])
                tt(qkTm()[:, hs], KQ()[:, 128:256], E12()[:, 128:256], ALU.mult,
                   r=[KQ, E12], w=[qkTm.sub(h * 128, (h + 1) * 128)])
                tr(P3()[:, hs], B4()[:, hs], identF(), r=[B4.sub(h * 128, (h + 1) * 128), identF],
                   w=[P3.sub(h * 128, (h + 1) * 128)])
                hst[h, 2] = S.end_lane()
            for key in [(0, 0), (1, 0), (0, 1), (0, 2), (2, 0), (1, 1), (1, 2), (3, 0), (2, 1), (2, 2), (3, 1), (3, 2)]:
                S.emit_list(hst[key])
            h3 = "p (h c) -> p h c"

            class VB(V):
                def __call__(self):
                    return self.t[:, self.lo:self.hi].bitcast(BF16)

            def FB(lo):
                return [VB(Ft, "F", lo + 128 * q, lo + 128 * (q + 1)) for q in range(2)]
            Ab_, Bb_, Pb_, MTh_, MTl_, Rb_, Y0T_ = FB(1536), FB(1792), FB(2560), FB(2816), FB(512), FB(768), FB(1024)
            IA = FV(0, 512)
            tt(IA(h3, h=4), identF().unsqueeze(1).broadcast_to([128, 4, 128]), P3(h3, h=4), ALU.add, r=[identF, P3], w=[IA])
            for q in range(2):
                qs = slice(q * 256, (q + 1) * 256)
                cp(Ab_[q](), P3()[:, qs], r=[P3], w=[Ab_[q]], eng='act')
                cp(Bb_[q](), B4()[:, qs], r=[B4.sub(q * 256, (q + 1) * 256)], w=[Bb_[q]], eng='act')
                cp(MTh_[q](), IA()[:, qs], r=[IA.sub(q * 256, (q + 1) * 256)], w=[MTh_[q]], eng='act')
                tt(MTl_[q](), IA()[:, qs], MTh_[q](), ALU.subtract, r=[IA.sub(q * 256, (q + 1) * 256), MTh_[q]], w=[MTl_[q]])
                tt(Pb_[q]().rearrange("p (h c) -> p h c", h=2), identF().unsqueeze(1).broadcast_to([128, 2, 128]),
                   B4()[:, qs].rearrange("p (h c) -> p h c", h=2), ALU.subtract,
                   r=[identF, B4.sub(q * 256, (q + 1) * 256)], w=[Pb_[q]])
            MARKS.append((ti, 'C9', S.ncomp['dve']))
            nst = 2 if smp else 6
            half_lanes = []
            for hf2 in range(2):
                QA, QB, QZ = ((P0, P1, P2), (P3, P4p, P5))[hf2]
                Ab, Bb, Yb, MTh, MTl, Rb, Y0T = (Ab_[hf2], Bb_[hf2], Pb_[hf2], MTh_[hf2], MTl_[hf2], Rb_[hf2], Y0T_[hf2])
                ea, eb_ = (('act', 'dve'), ('dve', 'act'))[hf2]
                S.begin_lane()
                for k in range(1, nst + 1):
                    last = (k == nst)
                    if not last:
                        for hh in range(2):
                            hs = slice(hh * 128, (hh + 1) * 128)
                            mm(QB()[:, hs], Ab()[:, hs], Bb()[:, hs], r=[Ab, Bb], w=[QB])
                    for hh in range(2):
                        hs = slice(hh * 128, (hh + 1) * 128)
                        mm(QA()[:, hs], Bb()[:, hs], Ab()[:, hs], r=[Ab, Bb], w=[QA])
                    cp(Ab(), QA()[:, 0:256], r=[QA], w=[Ab], eng=ea)
                    if not last:
                        cp(Bb(), QB()[:, 0:256], r=[QB], w=[Bb], eng=eb_)
                    for hh in range(2):
                        hs = slice(hh * 128, (hh + 1) * 128)
                        mm(QZ()[:, hs], Ab()[:, hs], Yb()[:, hs], r=[Ab, Yb], w=[QZ])
                    tt(Yb(), Yb(), QZ()[:, 0:256], ALU.add, r=[Yb, QZ], w=[Yb])
                for hh in range(2):
                    hs = slice(hh * 128, (hh + 1) * 128)
                    mm(QA()[:, hs], MTh()[:, hs], Yb()[:, hs], True, False, r=[MTh, Yb], w=[QA])
                    mm(QA()[:, hs], MTl()[:, hs], Yb()[:, hs], False, True, r=[MTl, Yb], w=[QA])
                    mm(QB()[:, hs], Yb()[:, hs], identB(), r=[Yb, identB], w=[QB])
                tt(Rb().rearrange("p (h c) -> p h c", h=2), identF().unsqueeze(1).broadcast_to([128, 2, 128]),
                   QA()[:, 0:256].rearrange("p (h c) -> p h c", h=2), ALU.subtract, r=[identF, QA], w=[Rb])
                cp(Y0T(), QB()[:, 0:256], r=[QB], w=[Y0T], eng=ea)
                for hh in range(2):
                    hs = slice(hh * 128, (hh + 1) * 128)
                    mm(QZ()[:, hs], Y0T()[:, hs], Rb()[:, hs], r=[Y0T, Rb], w=[QZ])
                tt(TinvTb()[:, hf2 * 256:(hf2 + 1) * 256], Yb(), QZ()[:, 0:256], ALU.add, r=[Yb, QZ],
                   w=[TinvTb.sub(hf2 * 256, (hf2 + 1) * 256)])
                half_lanes.append(S.end_lane())
            S.merge(half_lanes)
            dump(14, P4, 512); dump(15, V(cols, 'cols', 0, 256), 256); dump(16, qkTm, 512, bf=True)
            MARKS.append((ti, 'C10', S.ncomp['dve']))
            bc4 = lambda ap: ap.unsqueeze(2).broadcast_to([128, 4, 128])
            tt(kbd(h3, h=4), PT()[:, 0:512].rearrange(h3, h=4), bc4(bed()), ALU.mult, r=[PT.sub(0, 512), bed], w=[kbd])
            tt(vbeta(h3, h=4), PT()[:, 512:1024].rearrange(h3, h=4), bc4(beta()), ALU.mult, r=[PT.sub(512, 1024), beta],
               w=[vbeta])
            tt(kendb(h3, h=4), PT()[:, 0:512].rearrange(h3, h=4), bc4(eend4), ALU.mult, r=[PT.sub(0, 512), Ecol],
               w=[kendb])
            for h in range(4):
                hs = slice(h * 128, (h + 1) * 128)
                mm(P3()[:, hs], kbd()[:, hs], TinvTb()[:, hs], r=[kbd, TinvTb], w=[P3.sub(h * 128, (h + 1) * 128)])
            S.op('act', lambda e: e.mul(out=negwT(), in_=P3(), mul=-1.0), r=[P3], w=[negwT])
            if not smp:
                for h in range(4):
                    hs = slice(h * 128, (h + 1) * 128)
                    mm(P4p()[:, hs], TinvTb()[:, hs], vbeta()[:, hs], True, False, r=[TinvTb, vbeta], w=[P4p])
                    mm(P4p()[:, hs], negwT()[:, hs], SBb()[:, hs], False, True, r=[negwT, SBb], w=[P4p])
                cp(vnewb(), P4p(), r=[P4p], w=[vnewb], eng='act')
                for h in range(4):
                    hs = slice(h * 128, (h + 1) * 128)
                    mm(P5()[:, hs], vnewb()[:, hs], qkTm()[:, hs], True, False, r=[vnewb, qkTm], w=[P5])
                    mm(P5()[:, hs], SBb()[:, hs], qeTb()[:, hs], False, True, r=[SBb, qeTb], w=[P5])
                for h in range(4):
                    hs = slice(h * 128, (h + 1) * 128)
                    mm(P0()[:, hs], kendb()[:, hs], vnewb()[:, hs], r=[kendb, vnewb], w=[P0.sub(h * 128, (h + 1) * 128)])
                tt(SB(h3, h=4), SB(h3, h=4), bc4(etot), ALU.mult, r=[SB, Ecol], w=[SB])
                tt(SB(), SB(), P0(), ALU.add, r=[SB, P0], w=[SB])
                cp(SBb(), SB(), r=[SB], w=[SBb], eng='act')
                if ti == NT - 1:
                    dma(gdnp.rearrange("h d v -> d h v"), SB(h3, h=4), r=[SB], eng='pool')
            else:
                zerofill(P0)
                zerofill(P5)
                SS4 = [FV(0, 1024), FV(1024, 2048), FV(2048, 3072), FV(3072, 4096)]
                SSbds = [SSbd, HV(2560, 3584)]
                ssi = 0
                for h in range(4):
                    for hf in range(2):
                        SSx = SS4[ssi % 4]
                        SSbd_ = SSbds[ssi % 2]
                        SSb3 = SSbd_("p (s v) -> p s v", s=8)
                        SS3 = SSx("p (s v) -> p s v", s=8)
                        dma(SS3, sgdn[8 * hf:8 * hf + 8, h, :, :].rearrange("s d v -> d s v"), w=[SSx])
                        cp(SSbd_(), SSx(), r=[SSx], w=[SSbd_], eng=('act', 'dve')[ssi % 2])
                        ssi += 1
                        for j in range(8):
                            s = 8 * hf + j
                            cs_ = slice(h * 128 + 8 * s, h * 128 + 8 * s + 8)
                            mm(P0()[:, cs_], SSb3[:, j, :], negwT()[:, cs_], False, (h == 3 and hf == 1 and j == 7),
                               r=[SSbd_, negwT], w=[P0])
                            mm(P5()[:, cs_], SSb3[:, j, :], qeTb()[:, cs_], False, False, r=[SSbd_, qeTb], w=[P5])
                cp(wSTb(), P0(), r=[P0], w=[wSTb], eng='act')
                for h in range(4):
                    hs = slice(h * 128, (h + 1) * 128)
                    mm(P4p()[:, hs], TinvTb()[:, hs], vbeta()[:, hs], True, False, r=[TinvTb, vbeta], w=[P4p])
                    mm(P4p()[:, hs], wSTb()[:, hs], identB(), False, True, r=[wSTb, identB], w=[P4p])
                cp(vnewb(), P4p(), r=[P4p], w=[vnewb], eng='act')
                for h in range(4):
                    hs = slice(h * 128, (h + 1) * 128)
                    mm(P5()[:, hs], vnewb()[:, hs], qkTm()[:, hs], False, h == 3, r=[vnewb, qkTm], w=[P5])
                et3 = Ecol()[:, 8:72].rearrange("p (s h) -> p s h", s=16)
                km3 = kendmd("p (s d) -> p s d", s=8)
                for h in range(4):
                    hs = slice(h * 128, (h + 1) * 128)
                    for hf in range(2):
                        SSx = SS4[ssi % 4]
                        ssi += 1
                        SS3 = SSx("p (s v) -> p s v", s=8)
                        dma(SS3, sgdn[8 * hf:8 * hf + 8, h, :, :].rearrange("s d v -> d s v"), w=[SSx])
                        tt(km3, kendb()[:, hs].unsqueeze(1).broadcast_to([128, 8, 128]),
                           segm()[:, 8 * hf:8 * hf + 8].unsqueeze(2).broadcast_to([128, 8, 128]), ALU.mult,
                           r=[kendb, segm], w=[kendmd])
                        for g in range(2):
                            Pq = ((P0, P1), (P2, P3))[hf][g]
                            for jj in range(4):
                                mm(Pq()[:, jj * 128:(jj + 1) * 128], km3[:, 4 * g + jj, :], vnewb()[:, hs],
                                   r=[kendmd, vnewb], w=[Pq.sub(jj * 128, (jj + 1) * 128)])
                            sl = SS3[:, 4 * g:4 * g + 4, :]
                            s0 = 8 * hf + 4 * g
                            tt(sl, sl, et3[:, s0:s0 + 4, h:h + 1].broadcast_to([128, 4, 128]), ALU.mult,
                               r=[SSx, Ecol], w=[SSx])
                            tt(sl, sl, Pq("p (s v) -> p s v", s=4), ALU.add, r=[SSx, Pq], w=[SSx])
                        dma(gdns[8 * hf:8 * hf + 8, h, :, :].rearrange("s d v -> d s v"), SS3, r=[SSx], eng='pool')

            dump(17, vnewb, 512, bf=True); dump(18, P5, 512, psum=True)
            if DBG_PHASE < 4:
                return
            MARKS.append((ti, 'D', S.ncomp['dve']))
            oTs, rstds = [FV(0, 512), FV(1024, 1536)], [FV(512, 1024), FV(1536, 2048)]
            osqs, szs = [HV(2048, 2560), HV(6144, 6656)], [HV(2560, 3072), HV(6656, 7168)]
            Pos, Pss, Pzs, zc0s = (P6, P5), (P1, P2), (P3, P4p), (1040, 3096)
            for br in range(2):
                act(osqs[br](), Pos[br](), AF.Square, r=[Pos[br]], w=[osqs[br]])
            for br in range(2):
                mm(Pss[br](), onesB(), osqs[br](), r=[onesB, osqs[br]], w=[Pss[br]])
            for br in range(2):
                for c in range(4):
                    proj_fm(Pzs[br], c * 128, zc0s[br] + c * 128)
            for br in range(2):
                act(rstds[br](), Pss[br](), AF.Ln, r=[Pss[br]], w=[rstds[br]], scale=1.0 / 128, bias=EPS)
            for br in range(2):
                act(rstds[br](), rstds[br](), AF.Exp, r=[rstds[br]], w=[rstds[br]], scale=-0.5)
            for br in range(2):
                act(szs[br](), Pzs[br](), AF.Silu, r=[Pzs[br]], w=[szs[br]])
            for br in range(2):
                stt(oTs[br](), Pos[br](), smallc()[:, 8 + br:9 + br], rstds[br](), ALU.mult, ALU.mult,
                    r=[Pos[br], rstds[br], smallc], w=[oTs[br]])
                tt(gT[br](), oTs[br](), szs[br](), ALU.mult, r=[oTs[br], szs[br]], w=[gT[br]])
            dump(19, gT[0], 512, bf=True); dump(20, gT[1], 512, bf=True)
            if DBG_PHASE < 5:
                return
            MARKS.append((ti, 'E', S.ncomp['dve']))
            dma(hres(), xsrc, w=[hres])
            for n in range(2):
                ns = slice(n * 512, (n + 1) * 512)
                Qa, Qb, Qc, Qd = ((P0, P1, P2, P3), (P4p, P5, P6, P0))[n]
                proj_tm(Qc, 0, 3608 + n * 512, 512)
                proj_tm(Qd, 0, 4632 + n * 512, 512)
                for h in range(4):
                    mm(Qa(), gT[0]()[:, h * 128:(h + 1) * 128], Wupa[:, h, ns], h == 0, h == 3, r=[gT[0], WUPA], w=[Qa])
                for h in range(4):
                    mm(Qb(), gT[1]()[:, h * 128:(h + 1) * 128], Wupb[:, h, ns], h == 0, h == 3, r=[gT[1], WUPB], w=[Qb])
                act(sga(), Qc(), AF.Sigmoid, r=[Qc], w=[sga])
                act(sgb(), Qd(), AF.Sigmoid, r=[Qd], w=[sgb])
                tt(sga(), sga(), Qa(), ALU.mult, r=[sga, Qa], w=[sga])
                tt(sgb(), sgb(), Qb(), ALU.mult, r=[sgb, Qb], w=[sgb])
                tt(mg()[:, ns], sga(), sgb(), ALU.add, r=[sga, sgb], w=[mg.sub(n * 512, (n + 1) * 512)])
            if KEEPWARM:
                for _ in range(KEEPWARM):
                    mm(P3(), zB(), Wout[:, 0, 0:512], r=[zB, WOUT], w=[P3])
            for k in range(8):
                tr(PT()[:, k * 128:(k + 1) * 128], mg()[:, k * 128:(k + 1) * 128], identB(), r=[mg, identB],
                   w=[PT.sub(k * 128, (k + 1) * 128)])
            cp(mT(), PT(), r=[PT], w=[mT], eng='act')
            if KEEPWARM:
                for _ in range(KEEPWARM // 2):
                    mm(P3(), zB(), Wout[:, 0, 0:512], r=[zB, WOUT], w=[P3])
            mT3 = mT("p (k t) -> p k t", k=8)
            for n in range(2):
                ns = slice(n * 512, (n + 1) * 512)
                Pn = (P1, P2)[n]
                for kc in range(8):
                    mm(Pn(), mT3[:, kc, :], Wout[:, kc, ns], kc == 0, kc == 7, r=[mT, WOUT], w=[Pn])
                tt(hres()[:, ns], hres()[:, ns], Pn(), ALU.add, r=[hres.sub(n * 512, (n + 1) * 512), Pn],
                   w=[hres.sub(n * 512, (n + 1) * 512)])
            cp(mg(), hres(), r=[hres], w=[mg])
            for k in range(8):
                tr(PT()[:, k * 128:(k + 1) * 128], mg()[:, k * 128:(k + 1) * 128], identB(), r=[mg, identB],
                   w=[PT.sub(k * 128, (k + 1) * 128)])
            cp(mT(), PT(), r=[PT], w=[mT], eng='act')
            PT3 = PTT("p (k t) -> p k t", k=2)
            for n in range(2):
                ns = slice(n * 512, (n + 1) * 512)
                Pa, Pp = ((P0, P1), (P2, P3))[n]
                for kc in range(8):
                    mm(Pa(), mT3[:, kc, :], Wpg[:, kc, ns], kc == 0, kc == 7, r=[mT, WPG], w=[Pa])
                for k in range(2):
                    mm(Pp(), PT3[:, k, :], Wple[:, k, ns], k == 0, k == 1, r=[PTT, WPLE], w=[Pp])
                sp_ = (sga, sgb)[n]
                act(sp_(), Pa(), AF.Sigmoid, r=[Pa], w=[sp_])
                tt(sp_(), sp_(), Pp(), ALU.mult, r=[sp_, Pp], w=[sp_])
                tt(hres()[:, ns], hres()[:, ns], sp_(), ALU.add, r=[hres.sub(n * 512, (n + 1) * 512), sp_],
                   w=[hres.sub(n * 512, (n + 1) * 512)])
            dump(21, hres, 1024)
            if nextA is not None:
                nextA()
            f0, f1, f2 = CO(3, 4), CO(4, 5), CO(5, 6)
            act(mg(), hres(), AF.Square, r=[hres], w=[mg, f0], accum_out=f0())
            act(f1(), f0(), AF.Ln, r=[f0], w=[f1], scale=1.0 / 1024, bias=EPS)
            act(f2(), f1(), AF.Exp, r=[f1], w=[f2], scale=-0.5)
            stt(yt(), hres(), f2(), fnwbc(), ALU.mult, ALU.mult, r=[hres, f2, fnwbc], w=[yt])
            dma(ydst, yt(), r=[yt], eng='pool')

        S.budget = DBG_BUDGET
        tl = list(range(NT + 1) if DBG_TILES is None else DBG_TILES)
        phaseA(tl[0])
        for i_, ti in enumerate(tl):
            tile(ti, (lambda t2=tl[i_ + 1]: phaseA(t2)) if i_ + 1 < len(tl) else None)
        print('op counts', S.ncomp, S.dma_cnt)
        global LAST_MARKS
        LAST_MARKS = MARKS
        S.final_wait_all('sp')
        S.emit()
    return nc


_NC = None


def kernel(x_prompt, x_sample, state_gla, state_gdn, state_conv, p_prompt, p_sample, norm_w, w_in, w_gla_gate,
           b_gla_gate, gla_norm_w, conv_w, gdn_a_log, gdn_dt_bias, gdn_norm_w, w_up_gla, w_up_gdn, w_out,
           w_ple_gate, w_ple, final_norm_w):
    global _NC
    f = lambda a: np.ascontiguousarray(np.asarray(a, dtype=np.float32))
    if _NC is None:
        _NC = build()
    nc = _NC
    shared = {
        "normw": f(norm_w).reshape(1, 1024), "win": f(w_in[0]), "wgg": f(w_gla_gate[0]), "bgg": f(b_gla_gate).reshape(1, 256),
        "glanw": f(gla_norm_w).reshape(1, 128), "convw": f(conv_w[0]), "alog": f(gdn_a_log).reshape(1, 4),
        "dtb": f(gdn_dt_bias).reshape(1, 4), "gdnnw": f(gdn_norm_w).reshape(1, 128), "wupa": f(w_up_gla[0]),
        "wupb": f(w_up_gdn[0]), "wout": f(w_out[0]), "wpg": f(w_ple_gate[0]), "wple": f(w_ple[0]),
        "fnw": f(final_norm_w).reshape(1, 1024),
    }
    xs_ = f(x_sample).reshape(8, 128, 1024)
    ps_ = f(p_sample[0]).reshape(8, 128, 256)
    in_maps = []
    for c in range(8):
        m = dict(shared)
        m["xp"] = f(x_prompt[c])
        m["xsm"] = xs_[c]
        m["pp"] = f(p_prompt[0, c])
        m["psm"] = ps_[c]
        m["sgla"] = f(state_gla[0, 16 * c:16 * c + 16])
        m["sgdn"] = f(state_gdn[0, 16 * c:16 * c + 16])
        m["sconv"] = f(state_conv[0, 16 * c:16 * c + 16]).reshape(48, 1536)
        in_maps.append(m)
    res = run_bass_kernel_spmd(nc, in_maps, core_ids=list(range(8)))
    R = res.results
    y_prompt = np.stack([R[c]["yp"] for c in range(8)], 0)
    y_sample = np.concatenate([R[c]["ysm"].reshape(16, 8, 1024) for c in range(8)], 0)
    gla_p = np.stack([R[c]["glap"] for c in range(8)], 0)[None]
    gdn_p = np.stack([R[c]["gdnp"] for c in range(8)], 0)[None]
    conv_p = np.stack([R[c]["convp"] for c in range(8)], 0)[None]
    gla_s = np.concatenate([R[c]["glas"] for c in range(8)], 0)[None]
    gdn_s = np.concatenate([R[c]["gdns"] for c in range(8)], 0)[None]
    conv_s = np.concatenate([R[c]["convs"] for c in range(8)], 0)[None]
    return (y_prompt, y_sample, gla_p, gdn_p, conv_p, gla_s, gdn_s, conv_s)
```

```python
import numpy as np
from contextlib import ExitStack
import concourse.bass as bass
import concourse.mybir as mybir
from concourse.bass_utils import run_bass_kernel_spmd

F32 = mybir.dt.float32
BF16 = mybir.dt.bfloat16
AF = mybir.ActivationFunctionType
ALU = mybir.AluOpType
EPS = 1e-6
BIG = 30000.0
NT = 16
DBG_TILES = None
DBG_PHASE = 9
DBG_NOW = False
DBG_A = 99
DBG_BUDGET = None
DBG_DUMP = False
DBG_NOILV = False
KEEPWARM = 0
INC = 5656


class V:
    def __init__(self, t, name, lo, hi):
        self.t, self.name, self.lo, self.hi = t, name, lo, hi
        self.key = (name, lo, hi)

    def __call__(self, pat=None, **kw):
        ap = self.t[:, self.lo:self.hi]
        return ap.rearrange(pat, **kw) if pat else ap

    def sub(self, a, b):
        return V(self.t, self.name, self.lo + a, self.lo + b)


PSUM_NAMES = {"P0", "P1", "P2", "P3", "P4", "P5", "P6", "PT"}


def _key(k):
    if isinstance(k, V):
        k = k.key
    if isinstance(k, tuple):
        if k[0] in PSUM_NAMES:
            return (k[0], 0, 1 << 30)
        return k
    return (k, 0, 1 << 30)


class Sched:
    ENGS = ('pe', 'act', 'dve', 'pool', 'sp')

    def __init__(self, nc, es):
        self.nc = nc
        self.streams = {e: [] for e in self.ENGS}
        self.ncomp = {e: 0 for e in self.ENGS}
        self.sems = {e: es.enter_context(nc.semaphore("s_" + e)) for e in self.ENGS}
        self.qsems = {'sp': list(range(0, 8)), 'pool': list(range(8, 14)), 'act': list(range(14, 16))}
        self.dma_sems = [es.enter_context(nc.semaphore("s_dma%d" % i)) for i in range(16)]
        self.dma_cnt = [0] * 16
        self.rr = {'sp': 0, 'pool': 0, 'act': 0}
        self.acc = {}
        self.known = {e: {} for e in self.ENGS}
        self.budget = None
        self._lane = None
        self._stack = []

    def _collect(self, r, w):
        deps = {}

        def add(tok):
            if tok is None:
                return
            s, v = tok
            if deps.get(s, 0) < v:
                deps[s] = v
        for k in r:
            n, lo, hi = _key(k)
            for ent in self.acc.get(n, ()):
                if ent[0] < hi and lo < ent[1]:
                    add(ent[2])
        for k in w:
            n, lo, hi = _key(k)
            for ent in self.acc.get(n, ()):
                if ent[0] < hi and lo < ent[1]:
                    add(ent[2])
                    for t in ent[3]:
                        add(t)
        return deps

    def _finish(self, tok, r, w):
        for k in r:
            n, lo, hi = _key(k)
            hit = False
            for ent in self.acc.setdefault(n, []):
                if ent[0] < hi and lo < ent[1]:
                    ent[3].append(tok)
                    hit = True
            if not hit:
                self.acc[n].append([lo, hi, None, [tok]])
        for k in w:
            n, lo, hi = _key(k)
            lst = self.acc.setdefault(n, [])
            new = []
            for ent in lst:
                if ent[0] < hi and lo < ent[1]:
                    if ent[0] < lo:
                        new.append([ent[0], lo, ent[2], list(ent[3])])
                    if hi < ent[1]:
                        new.append([hi, ent[1], ent[2], list(ent[3])])
                else:
                    new.append(ent)
            new.append([lo, hi, tok, []])
            self.acc[n] = new

    def _waits(self, eng, deps):
        kn = self.known[eng]
        waits = []
        for s, v in deps.items():
            if kn.get(s, 0) < v:
                kn[s] = v
                waits.append((s, v))
        return waits

    @staticmethod
    def _excl(r, w):
        r2, w2 = [], list(w)
        for k in r:
            kk = _key(k)
            if kk[0] in PSUM_NAMES:
                w2.append(k)
            else:
                r2.append(k)
        return r2, w2

    def begin_lane(self):
        self._stack.append(self._lane)
        self._lane = []

    def end_lane(self):
        l = self._lane
        self._lane = self._stack.pop()
        return l

    def emit_list(self, lst):
        for kind, eng, fn, r, w in lst:
            (self.op if kind == 'c' else self.dma)(eng, fn, r, w)

    def merge(self, lanes):
        idx = [0] * len(lanes)
        tot = [max(1, len(l)) for l in lanes]
        while any(idx[i] < len(lanes[i]) for i in range(len(lanes))):
            best = min((i for i in range(len(lanes)) if idx[i] < len(lanes[i])), key=lambda i: idx[i] / tot[i])
            kind, eng, fn, r, w = lanes[best][idx[best]]
            idx[best] += 1
            (self.op if kind == 'c' else self.dma)(eng, fn, r, w)

    def op(self, eng, fn, r=(), w=()):
        if self._lane is not None:
            self._lane.append(('c', eng, fn, r, w))
            return
        r, w = self._excl(r, w)
        if self.budget is not None:
            if self.budget <= 0:
                return
            self.budget -= 1
        deps = self._collect(r, w)
        if eng == 'pe':
            deps.pop('pe', None)
        self.ncomp[eng] += 1
        self.streams[eng].append((fn, self._waits(eng, deps), 'c', None))
        self._finish((eng, self.ncomp[eng]), r, w)

    def dma(self, eng, fn, r=(), w=()):
        if self._lane is not None:
            self._lane.append(('d', eng, fn, r, w))
            return
        if self.budget is not None:
            if self.budget <= 0:
                return
            self.budget -= 1
        deps = self._collect(r, w)
        q = self.qsems[eng]
        i = q[self.rr[eng] % len(q)]
        self.rr[eng] += 1
        sname = 'dma%d' % i
        if self.dma_cnt[i] > 0:
            deps[sname] = max(deps.get(sname, 0), 16 * self.dma_cnt[i])
        self.dma_cnt[i] += 1
        self.streams[eng].append((fn, self._waits(eng, deps), 'd', i))
        self._finish((sname, 16 * self.dma_cnt[i]), r, w)

    def _sem(self, s):
        if s.startswith('dma'):
            return self.dma_sems[int(s[3:])]
        return self.sems[s]

    def final_wait_all(self, eng='sp'):
        waits = []
        for i, c in enumerate(self.dma_cnt):
            if c:
                waits.append(('dma%d' % i, 16 * c))
        for e in self.ENGS:
            if self.ncomp[e] and e != eng:
                waits.append((e, self.ncomp[e]))
        self.streams[eng].append((None, waits, 'w', None))

    def emit(self):
        nc = self.nc
        with nc.Block() as block:
            def mk(ename):
                def body(e):
                    for fn, waits, kind, di in self.streams[ename]:
                        for s, v in waits:
                            e.wait_ge(self._sem(s), v)
                        if fn is None:
                            continue
                        ins = fn(e)
                        if kind == 'c':
                            ins.then_inc(self.sems[ename], 1)
                        else:
                            ins.then_inc(self.dma_sems[di], 16)
                return body
            block.tensor(mk('pe'))
            block.scalar(mk('act'))
            block.vector(mk('dve'))
            block.gpsimd(mk('pool'))
            block.sync(mk('sp'))


def build():
    nc = bass.Bass("TRN2", target_bir_lowering=False)
    di = lambda n, s: nc.dram_tensor(n, s, F32, kind="ExternalInput").ap()
    do = lambda n, s: nc.dram_tensor(n, s, F32, kind="ExternalOutput").ap()
    xp, xsm = di("xp", [2048, 1024]), di("xsm", [128, 1024])
    pp, psm = di("pp", [2048, 256]), di("psm", [128, 256])
    sgla, sgdn, sconv = di("sgla", [16, 4, 64, 128]), di("sgdn", [16, 4, 128, 128]), di("sconv", [48, 1536])
    normw, win = di("normw", [1, 1024]), di("win", [1024, INC])
    wgg, bgg, glanw = di("wgg", [16, 256]), di("bgg", [1, 256]), di("glanw", [1, 128])
    convw, alog, dtb, gdnnw = di("convw", [4, 1536]), di("alog", [1, 4]), di("dtb", [1, 4]), di("gdnnw", [1, 128])
    wupa, wupb = di("wupa", [512, 1024]), di("wupb", [512, 1024])
    wout, wpg, wple = di("wout", [1024, 1024]), di("wpg", [1024, 1024]), di("wple", [256, 1024])
    fnw = di("fnw", [1, 1024])
    yp, ysm = do("yp", [2048, 1024]), do("ysm", [128, 1024])
    glap, gdnp, convp = do("glap", [4, 64, 128]), do("gdnp", [4, 128, 128]), do("convp", [3, 1536])
    glas, gdns, convs = do("glas", [16, 4, 64, 128]), do("gdns", [16, 4, 128, 128]), do("convs", [16, 3, 1536])

    if DBG_DUMP:
        dbgf = nc.dram_tensor("dbgf", [24, 128, 1024], F32, kind="ExternalOutput").ap()
        dbgb = nc.dram_tensor("dbgb", [24, 128, 1024], BF16, kind="ExternalOutput").ap()
    with ExitStack() as es:
        S = Sched(nc, es)
        sbt = lambda n, s, d: es.enter_context(nc.sbuf_tensor(n, s, d))
        pst = lambda n, s, d: es.enter_context(nc.psum_tensor(n, s, d))

        def sv(n, w, d):
            return V(sbt(n, [128, w], d), n, 0, w)

        Win = sbt("Win", [128, 8, INC], BF16)
        Wupa, Wupb = sbt("Wupa", [128, 4, 1024], BF16), sbt("Wupb", [128, 4, 1024], BF16)
        Wout, Wpg = sbt("Wout", [128, 8, 1024], BF16), sbt("Wpg", [128, 8, 1024], BF16)
        Wple = sbt("Wple", [128, 2, 1024], BF16)
        fnwbc = sv("fnwbc", 1024, F32)
        identF, onesF = sv("identF", 128, F32), sv("onesF", 128, F32)
        identB, onesB, zB = sv("identB", 128, BF16), sv("onesB", 128, BF16), sv("zB", 128, BF16)
        Lm = [sv("L_P", 128, F32), sv("L_S", 128, F32)]
        Um = [sv("U_P", 128, F32), sv("U_S", 128, F32)]
        NEG2 = [sv("NEG2_P", 256, F32), sv("NEG2_S", 256, F32)]
        segm = sv("segm", 16, F32)
        hm = sv("hm", 2, F32)
        Wg = sbt("Wg", [32, 256], F32)
        gaug = sbt("gaug", [32, 128], F32)
        cw = sbt("cw", [128, 4, 12], F32)
        smallc = sv("smallc", 32, F32)
        cols = sbt("cols", [128, 256], F32)
        CO = lambda a, b: V(cols, "cols", a, b)
        X, PL = sv("X", 1024, F32), sv("PL", 256, F32)
        XT = sv("XT", 1024, BF16)
        PTTs = [sv("PTT", 256, BF16), sv("PTT2", 256, BF16)]
        XPAD, CARRY = sv("XPAD", 704, F32), sv("CARRY", 36, F32)
        SA, SAb = sv("SA", 256, F32), sv("SAb", 256, BF16)
        SB, SBb = sv("SB", 512, F32), sv("SBb", 512, BF16)
        Ft = sbt("F", [128, 4096], F32)
        Ht = sbt("H", [128, 8192], BF16)
        FV = lambda a, b: V(Ft, "F", a, b)
        HV = lambda a, b: V(Ht, "H", a, b)
        acc, rinv, qnT = FV(0, 512), FV(512, 1024), FV(1024, 1536)
        A4, B4, P4 = FV(1536, 2048), FV(2048, 2560), FV(2560, 3072)
        gbc, lbbc, E12, edbc = FV(3072, 3200), FV(3200, 3328), FV(3328, 3584), FV(3584, 3712)
        lsp, eb, enb, eend = FV(1536, 1792), FV(1792, 2048), FV(2048, 2304), FV(2304, 2560)
        oT, rstd, sga, sgb = FV(0, 512), FV(512, 1024), FV(1024, 1536), FV(1536, 2048)
        hres, yt = FV(2048, 3072), FV(3072, 4096)
        SS = [FV(0, 1024), FV(1024, 2048)]
        scv, tokq = FV(0, 1536), FV(2560, 4096)
        STG = [FV(0, 1024), FV(1024, 2048), FV(2048, 3072), FV(3072, 4096)]
        xs, pb = HV(0, 1024), HV(1792, 2048)
        qeT, keT, kend, va, attTm = HV(0, 512), HV(512, 768), HV(768, 1024), HV(1024, 1536), HV(1536, 2048)
        ebm = FV(2560, 3072)
        SSbd, kendmd = HV(0, 1024), HV(1024, 2048)
        sq, knT, qnTb, vTb = HV(2048, 2560), HV(2560, 3072), HV(3072, 3584), HV(3584, 4096)
        qkTm, TinvTb, kbd, vbeta = HV(4096, 4608), HV(4608, 5120), HV(5120, 5632), HV(5632, 6144)
        kendb, negwT, vnewb, wSTb = HV(6144, 6656), HV(6656, 7168), HV(7168, 7680), HV(7680, 8192)
        qeTb = HV(2048, 2560)
        SSbg, kendmg = HV(2048, 3072), HV(3072, 4096)
        osq, sz, gT = HV(2048, 2560), HV(2560, 3072), [HV(3072, 3584), HV(3584, 4096)]
        mg, mT = HV(4096, 5120), HV(5120, 6144)
        Pb = []
        for i in range(7):
            Pb.append(V(pst("P%d" % i, [128, 512], F32), "P%d" % i, 0, 512))
        PT = V(pst("PT", [128, 1024], BF16), "PT", 0, 1024)

        def mm(out, lhsT, rhs, start=True, stop=True, r=(), w=()):
            S.op('pe', lambda e: e.matmul(out, lhsT=lhsT, rhs=rhs, start=start, stop=stop), r, w)

        def tr(out, in_, ident, r=(), w=()):
            S.op('pe', lambda e: e.transpose(out=out, in_=in_, identity=ident), r, w)

        def act(out, in_, func, r=(), w=(), **kw):
            S.op('act', lambda e: e.activation(out=out, in_=in_, func=func, **kw), r, w)

        def tt(out, in0, in1, op, r=(), w=(), eng='dve'):
            S.op(eng, lambda e: e.tensor_tensor(out=out, in0=in0, in1=in1, op=op), r, w)

        def ts(out, in0, s1, op0, r=(), w=(), s2=None, op1=None, eng='dve'):
            if op1 is None:
                S.op(eng, lambda e: e.tensor_scalar(out=out, in0=in0, scalar1=s1, scalar2=None, op0=op0), r, w)
            else:
                S.op(eng, lambda e: e.tensor_scalar(out=out, in0=in0, scalar1=s1, scalar2=s2, op0=op0, op1=op1), r, w)

        def stt(out, in0, sc, in1, op0, op1, r=(), w=(), eng='dve'):
            S.op(eng, lambda e: e.scalar_tensor_tensor(out=out, in0=in0, scalar=sc, in1=in1, op0=op0, op1=op1), r, w)

        def cp(out, in_, r=(), w=(), eng='dve'):
            if eng == 'act':
                S.op('act', lambda e: e.copy(out=out, in_=in_), r, w)
            else:
                S.op(eng, lambda e: e.tensor_copy(out=out, in_=in_), r, w)

        def dma(out, in_, r=(), w=(), eng='sp', slow=False):
            if slow:
                S.dma(eng, lambda e: e.dma_start(out=out, in_=in_, allow_slow_non_contiguous=True), r, w)
            else:
                S.dma(eng, lambda e: e.dma_start(out=out, in_=in_), r, w)

        def dump(slot, v, n, bf=False, psum=False):
            if not DBG_DUMP:
                return
            if psum:
                cp(yt()[:, 0:n], v()[:, 0:n], r=[v], w=[yt])
                dma(dbgf[slot, :, 0:n], yt()[:, 0:n], r=[yt], eng='pool')
            elif bf:
                dma(dbgb[slot, :, 0:n], v()[:, 0:n], r=[v], eng='pool')
            else:
                dma(dbgf[slot, :, 0:n], v()[:, 0:n], r=[v], eng='pool')

        def zerofill(P):
            mm(P(), zB(), XT()[:, 0:512], True, False, r=[zB, XT], w=[P])

        def ms(v, val, eng='pool'):
            S.op(eng, lambda e: e.memset(v(), val), w=[v])

        def asel(v, pattern, base, cm, cmp, fill, view=None):
            ap = v() if view is None else view
            S.op('pool', lambda e: e.affine_select(out=ap, in_=ap, pattern=pattern, compare_op=cmp, fill=fill,
                                                   base=base, channel_multiplier=cm), r=[v], w=[v])
        t0_ = (list(range(NT + 1)) if DBG_TILES is None else list(DBG_TILES))[0]
        dma(X(), xsm[:, :] if t0_ == NT else xp[t0_ * 128:(t0_ + 1) * 128, :], w=[X])
        dma(PL(), psm[:, :] if t0_ == NT else pp[t0_ * 128:(t0_ + 1) * 128, :], w=[PL])
        PRELOADED = [t0_]
        ms(identF, 0.0)
        asel(identF, [[-1, 128]], 0, 1, ALU.not_equal, 1.0)
        ms(onesF, 1.0)
        cp(identB(), identF(), r=[identF], w=[identB])
        ms(onesB, 1.0)
        ms(zB, 0.0)
        ms(Lm[0], 1.0)
        asel(Lm[0], [[1, 128]], 0, -1, ALU.is_ge, 0.0)
        ms(Um[0], 1.0)
        asel(Um[0], [[-1, 128]], 0, 1, ALU.is_gt, 0.0)
        same = FV(0, 128)
        ms(same, 1.0)
        sview = same("p (a r) -> p a r", a=16)
        asel(same, [[-8, 16], [0, 8]], 0, 1, ALU.is_ge, 0.0, view=sview)
        asel(same, [[8, 16], [0, 8]], 7, -1, ALU.is_ge, 0.0, view=sview)
        ms(segm, 1.0)
        asel(segm, [[-8, 16]], 0, 1, ALU.is_ge, 0.0)
        asel(segm, [[8, 16]], 7, -1, ALU.is_ge, 0.0)
        ms(hm, 1.0)
        asel(hm, [[-64, 2]], 63, -1, ALU.is_ge, 0.0)
        hm1 = FV(256, 258)
        ms(hm1, 1.0)
        asel(hm1, [[64, 2]], -64, 1, ALU.is_ge, 0.0)
        S.op('dve', lambda e: e.tensor_copy(out=hm()[:, 1:2], in_=hm1()[:, 0:1]), r=[hm1, hm], w=[hm])
        tt(Lm[1](), Lm[0](), same(), ALU.mult, r=[Lm[0], same], w=[Lm[1]])
        tt(Um[1](), Um[0](), same(), ALU.mult, r=[Um[0], same], w=[Um[1]])
        ind = FV(128, 256)
        for v in range(2):
            ms(ind, 1.0)
            asel(ind, [[1, 128]], 0, -1, ALU.is_gt, 0.0)
            if v == 1:
                tt(ind(), ind(), same(), ALU.mult, r=[ind, same], w=[ind])
            ts(NEG2[v]()[:, 0:128], ind(), 1.0, ALU.subtract, s2=BIG, op1=ALU.mult, r=[ind], w=[NEG2[v].sub(0, 128)])
            ts(NEG2[v]()[:, 128:256], Lm[v](), 1.0, ALU.subtract, s2=BIG, op1=ALU.mult, r=[Lm[v]],
               w=[NEG2[v].sub(128, 256)])
        S.op('pool', lambda e: e.memset(gaug[:], 1.0), w=["gaug"])
        S.op('dve', lambda e: e.memset(Wg[:], 0.0), w=["Wg"])
        dma(Wg[0:16, :], wgg[:, :], r=["Wg"], w=["Wg"])
        dma(Wg[16:17, :], bgg[0:1, :], r=["Wg"], w=["Wg"])
        PR2, PRc = FV(1536, 1664), FV(0, 1536)
        dma(Ft[0:8, 1536:1664], normw[0].rearrange("(k p) -> k p", p=128), w=[PR2])
        dma(Ft[8:9, 1536:1664], glanw[0:1, :], r=[PR2], w=[PR2])
        dma(Ft[9:10, 1536:1664], gdnnw[0:1, :], r=[PR2], w=[PR2])
        dma(Ft[0:4, 0:1536], convw[:, :], w=[PRc])
        tr(Pb[0]()[:, 0:10], Ft[0:10, 1536:1664], identF.t[0:10, 0:10], r=[PR2, identF], w=[Pb[0]])
        cp(smallc()[:, 0:10], Pb[0]()[:, 0:10], r=[Pb[0]], w=[smallc.sub(0, 10)])
        for k in range(12):
            tr(Pb[1]()[:, 4 * k:4 * k + 4], Ft[0:4, k * 128:(k + 1) * 128], identF.t[0:4, 0:4], r=[PRc, identF], w=[Pb[1]])
        cp(cw[:].rearrange("p w k -> p k w"), Pb[1]()[:, 0:48].rearrange("p (k w) -> p k w", w=4), r=[Pb[1]], w=["cw"])
        dma(smallc()[:, 12:16], dtb[0:1, :].partition_broadcast(128), w=[smallc.sub(12, 16)])
        dma(smallc()[:, 16:20], alog[0:1, :].partition_broadcast(128), w=[smallc.sub(16, 20)])
        act(smallc()[:, 16:20], smallc()[:, 16:20], AF.Exp, r=[smallc.sub(16, 20)], w=[smallc.sub(16, 20)])
        ts(smallc()[:, 16:20], smallc()[:, 16:20], -1.0, ALU.mult, r=[smallc.sub(16, 20)], w=[smallc.sub(16, 20)])
        dma(fnwbc(), fnw[0:1, :].partition_broadcast(128), w=[fnwbc])
        ms(SA, 0.0), ms(SAb, 0.0), ms(SB, 0.0), ms(SBb, 0.0), ms(CARRY, 0.0)

        if not DBG_NOW:
            for (c0, c1) in ((0, 1552), (1552, 3608), (3608, INC)):
                for kc in range(8):
                    dma(Win[:, kc, c0:c1], win[kc * 128:(kc + 1) * 128, c0:c1],
                        w=[("Win", kc * INC + c0, kc * INC + c1)], eng='pool')
            dma(Wupa[:], wupa.rearrange("(h p) n -> p h n", p=128), w=["Wupa"], eng='pool')
            dma(Wupb[:], wupb.rearrange("(h p) n -> p h n", p=128), w=["Wupb"], eng='pool')
            for hk in range(2):
                dma(Wout[:, 4 * hk:4 * hk + 4, :], wout[512 * hk:512 * hk + 512, :].rearrange("(k p) n -> p k n", p=128),
                    w=[("Wout", 4 * hk, 4 * hk + 4)], eng='pool')
            for hk in range(2):
                dma(Wpg[:, 4 * hk:4 * hk + 4, :], wpg[512 * hk:512 * hk + 512, :].rearrange("(k p) n -> p k n", p=128),
                    w=[("Wpg", 4 * hk, 4 * hk + 4)], eng='pool')
            dma(Wple[:], wple.rearrange("(k p) n -> p k n", p=128), w=["Wple"], eng='pool')
        WIN, WUPA, WUPB, WOUT, WPG, WPLE = "Win", "Wupa", "Wupb", "Wout", "Wpg", "Wple"

        MARKS = []

        def phaseA(ti):
            smp = (ti == NT)
            xsrc = xsm[:, :] if smp else xp[ti * 128:(ti + 1) * 128, :]
            psrc = psm[:, :] if smp else pp[ti * 128:(ti + 1) * 128, :]
            PTT = PTTs[ti % 2]
            MARKS.append((ti, 'A', S.ncomp['dve']))
            if ti in PRELOADED:
                PRELOADED.remove(ti)
            else:
                dma(X(), xsrc, w=[X])
                dma(PL(), psrc, w=[PL])
            c0_, c1_, c2_ = CO(0, 1), CO(1, 2), CO(2, 3)
            act(xs(), X(), AF.Square, r=[X], w=[xs, c0_], accum_out=c0_())
            act(c1_(), c0_(), AF.Ln, r=[c0_], w=[c1_], scale=1.0 / 1024, bias=EPS)
            act(c2_(), c1_(), AF.Exp, r=[c1_], w=[c2_], scale=-0.5)
            ts(xs(), X(), c2_(), ALU.mult, r=[X, c2_], w=[xs])
            for k in range(8):
                tr(PT()[:, k * 128:(k + 1) * 128], xs()[:, k * 128:(k + 1) * 128], identB(), r=[xs, identB],
                   w=[PT.sub(k * 128, (k + 1) * 128)])
            tt(XT("p (k t) -> p k t", k=8), PT("p (k t) -> p k t", k=8),
               smallc()[:, 0:8].unsqueeze(2).broadcast_to([128, 8, 128]), ALU.mult, r=[PT, smallc], w=[XT])
            cp(pb(), PL(), r=[PL], w=[pb])
            for k in range(2):
                tr(PT()[:, k * 128:(k + 1) * 128], pb()[:, k * 128:(k + 1) * 128], identB(), r=[pb, identB],
                   w=[PT.sub(k * 128, (k + 1) * 128)])
            cp(PTT(), PT()[:, 0:256], r=[PT.sub(0, 256)], w=[PTT])

            if (not smp) and not DBG_NOILV:
                XT3a = XT("p (k t) -> p k t", k=8)
                for g in range(3):
                    for c in range(4):
                        c0 = 1552 + (4 * g + c) * 128
                        for kc in range(8):
                            mm(Pb[g]()[:, c * 128:(c + 1) * 128], Win[:, kc, c0:c0 + 128], XT3a[:, kc, :], kc == 0, kc == 7,
                               r=[XT, ("Win", kc * INC + c0, kc * INC + c0 + 128)], w=[Pb[g]])
                for kc in range(8):
                    mm(Pb[3]()[0:16, 0:128], Win[:, kc, 1024:1040], XT3a[:, kc, :], kc == 0, kc == 7,
                       r=[XT, ("Win", kc * INC + 1024, kc * INC + 1040)], w=[Pb[3]])
                for c in range(4):
                    for kc in range(8):
                        mm(Pb[5]()[:, c * 128:(c + 1) * 128], Win[:, kc, c * 128:(c + 1) * 128], XT3a[:, kc, :], kc == 0, kc == 7,
                           r=[XT, ("Win", kc * INC + c * 128, kc * INC + (c + 1) * 128)], w=[Pb[5]])

        def tile(ti, nextA=None):
            smp = (ti == NT)
            v = 1 if smp else 0
            L, U, NG = Lm[v], Um[v], NEG2[v]
            xsrc = xsm[:, :] if smp else xp[ti * 128:(ti + 1) * 128, :]
            psrc = psm[:, :] if smp else pp[ti * 128:(ti + 1) * 128, :]
            ydst = ysm[:, :] if smp else yp[ti * 128:(ti + 1) * 128, :]
            P0, P1, P2, P3, P4p, P5, P6 = Pb
            W = lambda c0, n: Win[:, :, c0:c0 + n]

            PTT = PTTs[ti % 2]
            XT3 = XT("p (k t) -> p k t", k=8)
            dump(0, XT, 1024, bf=True)

            def proj_fm(Pv, off, c0, M=128):
                for kc in range(8):
                    mm(Pv()[0:M, off:off + 128], Win[:, kc, c0:c0 + M], XT3[:, kc, :], kc == 0, kc == 7,
                       r=[XT, ("Win", kc * INC + c0, kc * INC + c0 + M)], w=[Pv.sub(off, off + 128)])

            def proj_tm(Pv, off, c0, n):
                for kc in range(8):
                    mm(Pv()[:, off:off + n], XT3[:, kc, :], Win[:, kc, c0:c0 + n], kc == 0, kc == 7,
                       r=[XT, ("Win", kc * INC + c0, kc * INC + c0 + n)], w=[Pv.sub(off, off + n)])

            MARKS.append((ti, 'B', S.ncomp['dve']))
            G0, G1, G2 = P3, P4p, P5
            ilv = (not smp) and not DBG_NOILV
            if ilv:
                S.begin_lane()
            if not ilv:
                proj_fm(G0, 0, 1024, M=16)
            S.op('dve', lambda e: e.tensor_copy(out=gaug[0:16, :], in_=G0()[0:16, 0:128]), r=[G0.sub(0, 128)], w=["gaug"])
            mm(G0()[:, 128:384], gaug[0:17, :], Wg[0:17, :], r=["gaug", "Wg"], w=[G0.sub(128, 384)])
            act(lsp(), G0()[:, 128:384], AF.Exp, r=[G0.sub(128, 384)], w=[lsp], scale=-1.0)
            act(lsp(), lsp(), AF.Ln, r=[lsp], w=[lsp], bias=1.0)
            for dc in range(2):
                mm(G1()[:, dc * 128:(dc + 1) * 128], lsp()[:, dc * 128:(dc + 1) * 128], L(), r=[lsp, L],
                   w=[G1.sub(dc * 128, (dc + 1) * 128)])
            mm(G1()[:, 256:512], U(), lsp(), r=[lsp, U], w=[G1.sub(256, 512)])
            act(eb(), G1()[:, 0:256], AF.Exp, r=[G1.sub(0, 256)], w=[eb], scale=-1.0 / 16)
            act(enb(), G1()[:, 0:256], AF.Exp, r=[G1.sub(0, 256)], w=[enb], scale=1.0 / 16)
            act(eend(), G1()[:, 256:512], AF.Exp, r=[G1.sub(256, 512)], w=[eend], scale=-1.0 / 16)
            dump(1, lsp, 256); dump(2, eb, 256); dump(3, eend, 256)
            for c in range(4):
                if not ilv:
                    proj_fm(G2, c * 128, c * 128)
            dump(4, G2, 512, psum=True)
            tt(ebm("p (c h t) -> p c h t", c=2, h=2), eb("p (c t) -> p c t", c=2).unsqueeze(2).broadcast_to([128, 2, 2, 128]),
               hm().unsqueeze(1).unsqueeze(3).broadcast_to([128, 2, 2, 128]), ALU.mult, r=[eb, hm], w=[ebm])
            for hh in range(2):
                stt(qeT("p (c h t) -> p c h t", c=2, h=2)[:, :, hh, :],
                    G2()[:, 0:256].rearrange("p (c t) -> p c t", c=2), 0.125,
                    ebm("p (c h t) -> p c h t", c=2, h=2)[:, :, hh, :], ALU.mult, ALU.mult, r=[G2.sub(0, 256), ebm], w=[qeT])
            tt(keT(), G2()[:, 256:512], enb(), ALU.mult, r=[G2.sub(256, 512), enb], w=[keT])
            proj_tm(P3, 0, 256, 512)
            proj_tm(P4p, 0, 768, 256)
            tt(kend(), P3()[:, 0:256], eend(), ALU.mult, r=[P3.sub(0, 256), eend], w=[kend])
            cp(va()[:, 0:256], P3()[:, 256:512], r=[P3.sub(256, 512)], w=[va.sub(0, 256)], eng='act')
            cp(va()[:, 256:512], P4p()[:, 0:256], r=[P4p.sub(0, 256)], w=[va.sub(256, 512)], eng='act')
            for h in range(4):
                dc, hp = h // 2, 64 * (h % 2)
                mm(P5()[:, h * 128:(h + 1) * 128], keT()[:, dc * 128:(dc + 1) * 128],
                   qeT()[:, h * 128:(h + 1) * 128], r=[keT, qeT], w=[P5.sub(h * 128, (h + 1) * 128)])
            tt(attTm("p (h c) -> p h c", h=4), P5("p (h c) -> p h c", h=4),
               L().unsqueeze(1).broadcast_to([128, 4, 128]), ALU.mult, r=[P5, L], w=[attTm])
            dump(5, qeT, 512, bf=True); dump(6, keT, 256, bf=True); dump(7, kend, 256, bf=True); dump(8, va, 512, bf=True); dump(9, attTm, 512, bf=True)
            zerofill(P6)
            if not smp:
                for h in range(4):
                    dc, hp = h // 2, 64 * (h % 2)
                    mm(P6()[:, h * 128:(h + 1) * 128], va()[:, h * 128:(h + 1) * 128], attTm()[:, h * 128:(h + 1) * 128],
                       False, False, r=[va, attTm], w=[P6])
                    mm(P6()[:, h * 128:(h + 1) * 128], SAb()[:, dc * 128:(dc + 1) * 128],
                       qeT()[:, h * 128:(h + 1) * 128], False, h == 3, r=[SAb, qeT], w=[P6])
                for h in range(4):
                    dc, hp = h // 2, 64 * (h % 2)
                    mm(P4p()[hp:hp + 64, 256 + dc * 128:256 + (dc + 1) * 128], kend()[:, h * 64:(h + 1) * 64],
                       va()[:, h * 128:(h + 1) * 128], r=[kend, va], w=[P4p.sub(256, 512)])
                for dc in range(2):
                    stt(SA()[:, dc * 128:(dc + 1) * 128], SA()[:, dc * 128:(dc + 1) * 128],
                        eb()[:, dc * 128 + 127:dc * 128 + 128], P4p()[:, 256 + dc * 128:256 + (dc + 1) * 128],
                        ALU.mult, ALU.add, r=[SA, eb, P4p.sub(256, 512)], w=[SA])
                cp(SAb(), SA(), r=[SA], w=[SAb], eng='act')
                if ti == NT - 1:
                    dma(glap.rearrange("(c h) d v -> (h d) c v", c=2), SA("p (c v) -> p c v", c=2), r=[SA], eng='pool')
            else:
                for h in range(4):
                    mm(P6()[:, h * 128:(h + 1) * 128], va()[:, h * 128:(h + 1) * 128], attTm()[:, h * 128:(h + 1) * 128],
                       False, False, r=[va, attTm], w=[P6])
                eb4 = eb("p (c s t) -> p c s t", c=2, s=16)
                SSg = [FV(0, 1024), FV(3072, 4096)]
                for dc in range(2):
                    for hf in range(2):
                        SSx = SSg[hf]
                        SS3 = SSx("p (s v) -> p s v", s=8)
                        dma(SS3, sgla[8 * hf:8 * hf + 8, 2 * dc:2 * dc + 2, :, :].rearrange("s h d v -> (h d) s v"),
                            w=[SSx])
                        SSbg_ = (SSbg, HV(4096, 5120))[(2 * dc + hf) % 2]
                        cp(SSbg_(), SSx(), r=[SSx], w=[SSbg_], eng=('act', 'dve')[(2 * dc + hf) % 2])
                        SSb3 = SSbg_("p (s v) -> p s v", s=8)
                        for hh in range(2):
                            h, hp = 2 * dc + hh, 64 * hh
                            for j in range(8):
                                s = 8 * hf + j
                                mm(P6()[:, h * 128 + 8 * s:h * 128 + 8 * s + 8], SSb3[:, j, :],
                                   qeT()[:, h * 128 + 8 * s:h * 128 + 8 * s + 8], False,
                                   (dc == 1 and hf == 1 and hh == 1 and j == 7), r=[SSbg_, qeT], w=[P6])
                        km3 = kendmg("p (s d) -> p s d", s=8)
                        tt(km3, kend()[:, dc * 128:(dc + 1) * 128].unsqueeze(1).broadcast_to([128, 8, 128]),
                           segm()[:, 8 * hf:8 * hf + 8].unsqueeze(2).broadcast_to([128, 8, 128]), ALU.mult,
                           r=[kend, segm], w=[kendmg])
                        for g in range(2):
                            Pq = (G0, G1)[g]
                            for jj in range(4):
                                j = 4 * g + jj
                                for hh in range(2):
                                    h, hp = 2 * dc + hh, 64 * hh
                                    mm(Pq()[hp:hp + 64, jj * 128:(jj + 1) * 128], km3[:, j, hp:hp + 64],
                                       va()[:, h * 128:(h + 1) * 128], r=[kendmg, va], w=[Pq])
                            sl = SS3[:, 4 * g:4 * g + 4, :]
                            s0 = 8 * hf + 4 * g
                            tt(sl, sl, eb4[:, dc, s0:s0 + 4, 7:8].broadcast_to([128, 4, 128]), ALU.mult,
                               r=[SSx, eb], w=[SSx])
                            tt(sl, sl, Pq("p (s v) -> p s v", s=4), ALU.add, r=[SSx, Pq], w=[SSx])
                        dma(glas[8 * hf:8 * hf + 8, 2 * dc:2 * dc + 2, :, :].rearrange("s h d v -> (h d) s v"), SS3,
                            r=[SSx], eng='pool')

            dump(10, P6, 512, psum=True)
            MARKS.append((ti, 'C', S.ncomp['dve']))
            laneB = S.end_lane() if ilv else None
            if ilv:
                S.begin_lane()
            if smp:
                for c3 in range(3):
                    Pc = (P3, P4p, P3)[c3]
                    proj_tm(Pc, 0, 1552 + c3 * 512, 512)
                    cp(tokq()[:, c3 * 512:(c3 + 1) * 512], Pc(), r=[Pc], w=[tokq.sub(c3 * 512, (c3 + 1) * 512)], eng='act')
                if smp:
                    for s in range(16):
                        dma(convs[s], tokq.t[8 * s + 5:8 * s + 8, 2560:4096], r=[tokq], eng='pool')
                else:
                    dma(convp[:, :], tokq.t[125:128, 2560:4096], r=[tokq], eng='pool')
            if smp:
                dma(scv.t[0:48, 0:1536], sconv[:, :], w=[scv])
                for k in range(12):
                    Px, c = (P5, k) if k < 8 else (P4p, k - 8)
                    tr(Px()[:, c * 48:(c + 1) * 48], scv.t[0:48, k * 128:(k + 1) * 128], identF.t[0:48, 0:48],
                       r=[scv, identF], w=[Px])
            PGT = P0 if ilv else P2
            S.begin_lane()
            proj_tm(PGT, 0, 3088, 8)
            t1, gcol, l2, beta = CO(8, 12), CO(12, 16), CO(16, 20), CO(20, 24)
            negdec, bed, Ecol, gm = CO(24, 28), CO(28, 32), CO(32, 104), CO(104, 168)
            tt(t1(), PGT()[:, 0:4], smallc()[:, 12:16], ALU.add, r=[PGT.sub(0, 8), smallc], w=[t1])
            act(t1(), t1(), AF.Exp, r=[t1], w=[t1])
            act(t1(), t1(), AF.Ln, r=[t1], w=[t1], bias=1.0)
            tt(gcol(), t1(), smallc()[:, 16:20], ALU.mult, r=[t1, smallc], w=[gcol])
            act(l2(), PGT()[:, 4:8], AF.Exp, r=[PGT.sub(0, 8)], w=[l2], scale=-1.0)
            act(l2(), l2(), AF.Ln, r=[l2], w=[l2], bias=1.0)
            act(beta(), l2(), AF.Exp, r=[l2], w=[beta], scale=-1.0)
            mm(PGT()[:, 8:12], L(), gcol(), r=[L, gcol], w=[PGT.sub(8, 12)])
            mm(PGT()[:, 12:16], U(), gcol(), r=[U, gcol], w=[PGT.sub(12, 16)])
            if not smp:
                mm(PGT()[:, 16:20], onesF(), gcol(), r=[onesF, gcol], w=[PGT.sub(16, 20)])
                ne = 12
            else:
                tt(gm("p (s h) -> p s h", s=16), gcol().unsqueeze(1).broadcast_to([128, 16, 4]),
                   segm().unsqueeze(2).broadcast_to([128, 16, 4]), ALU.mult, r=[gcol, segm], w=[gm])
                mm(PGT()[:, 16:80], onesF(), gm(), r=[onesF, gm], w=[PGT.sub(16, 80)])
                ne = 72
            act(Ecol()[:, 0:ne], PGT()[:, 8:8 + ne], AF.Exp, r=[PGT.sub(8, 8 + ne)], w=[Ecol])
            edec, eend4, etot = Ecol()[:, 0:4], Ecol()[:, 4:8], Ecol()[:, 8:ne]
            ts(negdec(), PGT()[:, 8:12], -1.0, ALU.mult, r=[PGT.sub(8, 12)], w=[negdec])
            tt(bed(), beta(), edec, ALU.mult, r=[beta, Ecol], w=[bed])
            for g in range(3):
                Pg = (P0, P1, P2)[g]
                if ilv:
                    continue
                for c in range(4):
                    proj_fm(Pg, c * 128, 1552 + (4 * g + c) * 128)
            laneGate = S.end_lane()
            if not ilv:
                S.emit_list(laneGate)
            stages = {}
            accs, rinvs = [FV(0, 512), FV(3072, 3584)], [FV(512, 1024), FV(3584, 4096)]
            for g in range(3):
                Pg = (P0, P1, P2)[g]
                acc, rinv = accs[g % 2], rinvs[g % 2]
                S.begin_lane()
                if not smp:
                    xp3 = XPAD()[:, 0:524].rearrange("p (c t) -> p c t", c=4)
                    car = CARRY("p (k w) -> p k w", k=12)[:, 4 * g:4 * g + 4, :]
                    cp(xp3[:, :, 0:3], car, r=[CARRY], w=[XPAD])
                    cp(xp3[:, :, 3:131], Pg("p (c t) -> p c t", c=4), r=[Pg], w=[XPAD], eng='act')
                    cp(car, xp3[:, :, 128:131], r=[XPAD], w=[CARRY])
                else:
                    xp4 = XPAD("p (c s t) -> p c s t", c=4, s=16)
                    Px, po = ((P5, 0), (P5, 192), (P4p, 0))[g]
                    cp(xp4[:, :, :, 0:3], Px()[:, po:po + 192].rearrange("p (c s w) -> p c s w", c=4, s=16), r=[Px], w=[XPAD])
                    cp(xp4[:, :, :, 3:11], Pg("p (c s t) -> p c s t", c=4, s=16), r=[Pg], w=[XPAD], eng='act')
                stages[g, 0] = S.end_lane()
                S.begin_lane()
                tmp = rinv
                for w_ in range(4):
                    cwv = cw[:, w_, 4 * g:4 * g + 4]
                    if not smp:
                        a3 = acc("p (c t) -> p c t", c=4)
                        t3 = tmp("p (c t) -> p c t", c=4)
                        xw = xp3[:, :, w_:w_ + 128]
                        bcw = cwv.unsqueeze(2).broadcast_to([128, 4, 128])
                    else:
                        a3 = acc("p (c s t) -> p c s t", c=4, s=16)
                        t3 = tmp("p (c s t) -> p c s t", c=4, s=16)
                        xw = xp4[:, :, :, w_:w_ + 8]
                        bcw = cwv.unsqueeze(2).unsqueeze(3).broadcast_to([128, 4, 16, 8])
                    if w_ == 0:
                        tt(a3, xw, bcw, ALU.mult, r=[XPAD, "cw"], w=[acc])
                    else:
                        tt(t3, xw, bcw, ALU.mult, r=[XPAD, "cw"], w=[tmp])
                        tt(a3, a3, t3, ALU.add, r=[acc, tmp], w=[acc])
                stages[g, 1] = S.end_lane()
                S.begin_lane()
                if g == 2:
                    act(vTb(), acc(), AF.Silu, r=[acc], w=[vTb])
                    stages[g, 2] = S.end_lane()
                    S.begin_lane()
                else:
                    act(acc(), acc(), AF.Silu, r=[acc], w=[acc])
                    tt(sq(), acc(), acc(), ALU.mult, r=[acc], w=[sq])
                    mm(Pg(), onesB(), sq(), r=[onesB, sq], w=[Pg])
                    stages[g, 2] = S.end_lane()
                    S.begin_lane()
                    act(rinv(), Pg(), AF.Ln, r=[Pg], w=[rinv], bias=EPS)
                    act(rinv(), rinv(), AF.Exp, r=[rinv], w=[rinv], scale=-0.5)
                    if g == 0:
                        stt(qnT(), acc(), 128.0 ** -0.5, rinv(), ALU.mult, ALU.mult, r=[acc, rinv], w=[qnT])
                    else:
                        tt(knT(), acc(), rinv(), ALU.mult, r=[acc, rinv], w=[knT])
                stages[g, 3] = S.end_lane()
            pre_keys = [(0, 0), (0, 1)] if ilv else []
            for key in [(0, 0), (0, 1), (1, 0), (0, 2), (1, 1), (0, 3), (2, 0), (1, 2), (2, 1), (1, 3), (2, 2), (2, 3)]:
                if key == (2, 0) and ilv:
                    S.emit_list(laneGate)
                if key not in pre_keys:
                    S.emit_list(stages[key])
            acc, rinv = accs[0], rinvs[0]
            cp(qnTb(), qnT(), r=[qnT], w=[qnTb], eng='act')
            if ilv:
                laneC = S.end_lane()
                for key in pre_keys:
                    S.emit_list(stages[key])
                S.merge([laneB, laneC])
            dump(11, qnT, 512); dump(12, knT, 512, bf=True); dump(13, vTb, 512, bf=True)
            for h in range(4):
                tr(PT()[:, h * 128:(h + 1) * 128], knT()[:, h * 128:(h + 1) * 128], identB(), r=[knT, identB],
                   w=[PT.sub(h * 128, (h + 1) * 128)])
                tr(PT()[:, 512 + h * 128:512 + (h + 1) * 128], vTb()[:, h * 128:(h + 1) * 128], identB(),
                   r=[vTb, identB], w=[PT.sub(512 + h * 128, 512 + (h + 1) * 128)])
            MARKS.append((ti, 'C8', S.ncomp['dve']))
            gbcs, lbbcs = [FV(3072, 3200), FV(3712, 3840)], [FV(3200, 3328), FV(3840, 3968)]
            edbcs, E12s = [FV(3584, 3712), FV(3968, 4096)], [FV(512, 768), FV(768, 1024)]
            hst = {}
            for h in range(4):
                RB = (P0, P1)[h % 2]
                KQ = (P2, P4p)[h % 2]
                gbc, lbbc, edbc, E12 = gbcs[h % 2], lbbcs[h % 2], edbcs[h % 2], E12s[h % 2]
                hs = slice(h * 128, (h + 1) * 128)
                S.begin_lane()
                cp(gbc(), gcol()[:, h:h + 1].broadcast_to([128, 128]), r=[gcol], w=[gbc])
                ts(lbbc(), l2()[:, h:h + 1].broadcast_to([128, 128]), -1.0, ALU.mult, r=[l2], w=[lbbc])
                mm(RB()[:, 0:128], gbc(), L(), True, True, r=[gbc, L], w=[RB])
                mm(RB()[:, 128:256], lbbc(), identF(), True, True, r=[lbbc, identF], w=[RB])
                mm(KQ()[:, 0:128], knT()[:, hs], knT()[:, hs], r=[knT], w=[KQ])
                mm(KQ()[:, 128:256], knT()[:, hs], qnTb()[:, hs], r=[knT, qnTb], w=[KQ])
                hst[h, 0] = S.end_lane()
                S.begin_lane()
                tt(E12()[:, 128:256], RB()[:, 0:128], NG()[:, 128:256], ALU.add, r=[RB, NG], w=[E12])
                tt(E12()[:, 0:128], RB()[:, 128:256], NG()[:, 0:128], ALU.add, r=[RB, NG], w=[E12])
                tt(E12()[:, 0:128], E12()[:, 0:128], RB()[:, 0:128], ALU.add, r=[RB, E12], w=[E12])
                act(E12(), E12(), AF.Exp, r=[E12, negdec], w=[E12], bias=negdec()[:, h:h + 1])
                act(edbc(), RB()[:, 0:128], AF.Exp, r=[RB], w=[edbc])
                tt(qeTb()[:, hs], qnT()[:, hs], edbc(), ALU.mult, r=[qnT, edbc], w=[qeTb.sub(h * 128, (h + 1) * 128)])
                hst[h, 1] = S.end_lane()
                S.begin_lane()
                tt(B4()[:, hs], KQ()[:, 0:128], E12()[:, 0:128], ALU.mult, r=[KQ, E12],
                   w=[B4.sub(h * 128, (h + 1) * 128)])
                tt(qkTm()[:, hs], KQ()[:, 128:256], E12()[:, 128:256], ALU.mult,
                   r=[KQ, E12], w=[qkTm.sub(h * 128, (h + 1) * 128)])
                tr(P3()[:, hs], B4()[:, hs], identF(), r=[B4.sub(h * 128, (h + 1) * 128), identF],
                   w=[P3.sub(h * 128, (h + 1) * 128)])
                hst[h, 2] = S.end_lane()
            for key in [(0, 0), (1, 0), (0, 1), (0, 2), (2, 0), (1, 1), (1, 2), (3, 0), (2, 1), (2, 2), (3, 1), (3, 2)]:
                S.emit_list(hst[key])
            h3 = "p (h c) -> p h c"

            class VB(V):
                def __call__(self):
                    return self.t[:, self.lo:self.hi].bitcast(BF16)

            def FB(lo):
                return [VB(Ft, "F", lo + 128 * q, lo + 128 * (q + 1)) for q in range(2)]
            Ab_, Bb_, Pb_, MTh_, MTl_, Rb_, Y0T_ = FB(1536), FB(1792), FB(2560), FB(2816), FB(512), FB(768), FB(1024)
            IA = FV(0, 512)
            tt(IA(h3, h=4), identF().unsqueeze(1).broadcast_to([128, 4, 128]), P3(h3, h=4), ALU.add, r=[identF, P3], w=[IA])
            for q in range(2):
                qs = slice(q * 256, (q + 1) * 256)
                cp(Ab_[q](), P3()[:, qs], r=[P3], w=[Ab_[q]], eng='act')
                cp(Bb_[q](), B4()[:, qs], r=[B4.sub(q * 256, (q + 1) * 256)], w=[Bb_[q]], eng='act')
                cp(MTh_[q](), IA()[:, qs], r=[IA.sub(q * 256, (q + 1) * 256)], w=[MTh_[q]], eng='act')
                tt(MTl_[q](), IA()[:, qs], MTh_[q](), ALU.subtract, r=[IA.sub(q * 256, (q + 1) * 256), MTh_[q]], w=[MTl_[q]])
                tt(Pb_[q]().rearrange("p (h c) -> p h c", h=2), identF().unsqueeze(1).broadcast_to([128, 2, 128]),
                   B4()[:, qs].rearrange("p (h c) -> p h c", h=2), ALU.subtract,
                   r=[identF, B4.sub(q * 256, (q + 1) * 256)], w=[Pb_[q]])
            MARKS.append((ti, 'C9', S.ncomp['dve']))
            nst = 2 if smp else 6
            half_lanes = []
            for hf2 in range(2):
                QA, QB, QZ = ((P0, P1, P2), (P3, P4p, P5))[hf2]
                Ab, Bb, Yb, MTh, MTl, Rb, Y0T = (Ab_[hf2], Bb_[hf2], Pb_[hf2], MTh_[hf2], MTl_[hf2], Rb_[hf2], Y0T_[hf2])
                ea, eb_ = (('act', 'dve'), ('dve', 'act'))[hf2]
                S.begin_lane()
                for k in range(1, nst + 1):
                    last = (k == nst)
                    if not last:
                        for hh in range(2):
                            hs = slice(hh * 128, (hh + 1) * 128)
                            mm(QB()[:, hs], Ab()[:, hs], Bb()[:, hs], r=[Ab, Bb], w=[QB])
                    for hh in range(2):
                        hs = slice(hh * 128, (hh + 1) * 128)
                        mm(QA()[:, hs], Bb()[:, hs], Ab()[:, hs], r=[Ab, Bb], w=[QA])
                    cp(Ab(), QA()[:, 0:256], r=[QA], w=[Ab], eng=ea)
                    if not last:
                        cp(Bb(), QB()[:, 0:256], r=[QB], w=[Bb], eng=eb_)
                    for hh in range(2):
                        hs = slice(hh * 128, (hh + 1) * 128)
                        mm(QZ()[:, hs], Ab()[:, hs], Yb()[:, hs], r=[Ab, Yb], w=[QZ])
                    tt(Yb(), Yb(), QZ()[:, 0:256], ALU.add, r=[Yb, QZ], w=[Yb])
                for hh in range(2):
                    hs = slice(hh * 128, (hh + 1) * 128)
                    mm(QA()[:, hs], MTh()[:, hs], Yb()[:, hs], True, False, r=[MTh, Yb], w=[QA])
                    mm(QA()[:, hs], MTl()[:, hs], Yb()[:, hs], False, True, r=[MTl, Yb], w=[QA])
                    mm(QB()[:, hs], Yb()[:, hs], identB(), r=[Yb, identB], w=[QB])
                tt(Rb().rearrange("p (h c) -> p h c", h=2), identF().unsqueeze(1).broadcast_to([128, 2, 128]),
                   QA()[:, 0:256].rearrange("p (h c) -> p h c", h=2), ALU.subtract, r=[identF, QA], w=[Rb])
                cp(Y0T(), QB()[:, 0:256], r=[QB], w=[Y0T], eng=ea)
                for hh in range(2):
                    hs = slice(hh * 128, (hh + 1) * 128)
                    mm(QZ()[:, hs], Y0T()[:, hs], Rb()[:, hs], r=[Y0T, Rb], w=[QZ])
                tt(TinvTb()[:, hf2 * 256:(hf2 + 1) * 256], Yb(), QZ()[:, 0:256], ALU.add, r=[Yb, QZ],
                   w=[TinvTb.sub(hf2 * 256, (hf2 + 1) * 256)])
                half_lanes.append(S.end_lane())
            S.merge(half_lanes)
            dump(14, P4, 512); dump(15, V(cols, 'cols', 0, 256), 256); dump(16, qkTm, 512, bf=True)
            MARKS.append((ti, 'C10', S.ncomp['dve']))
            bc4 = lambda ap: ap.unsqueeze(2).broadcast_to([128, 4, 128])
            tt(kbd(h3, h=4), PT()[:, 0:512].rearrange(h3, h=4), bc4(bed()), ALU.mult, r=[PT.sub(0, 512), bed], w=[kbd])
            tt(vbeta(h3, h=4), PT()[:, 512:1024].rearrange(h3, h=4), bc4(beta()), ALU.mult, r=[PT.sub(512, 1024), beta],
               w=[vbeta])
            tt(kendb(h3, h=4), PT()[:, 0:512].rearrange(h3, h=4), bc4(eend4), ALU.mult, r=[PT.sub(0, 512), Ecol],
               w=[kendb])
            for h in range(4):
                hs = slice(h * 128, (h + 1) * 128)
                mm(P3()[:, hs], kbd()[:, hs], TinvTb()[:, hs], r=[kbd, TinvTb], w=[P3.sub(h * 128, (h + 1) * 128)])
            S.op('act', lambda e: e.mul(out=negwT(), in_=P3(), mul=-1.0), r=[P3], w=[negwT])
            if not smp:
                for h in range(4):
                    hs = slice(h * 128, (h + 1) * 128)
                    mm(P4p()[:, hs], TinvTb()[:, hs], vbeta()[:, hs], True, False, r=[TinvTb, vbeta], w=[P4p])
                    mm(P4p()[:, hs], negwT()[:, hs], SBb()[:, hs], False, True, r=[negwT, SBb], w=[P4p])
                cp(vnewb(), P4p(), r=[P4p], w=[vnewb], eng='act')
                for h in range(4):
                    hs = slice(h * 128, (h + 1) * 128)
                    mm(P5()[:, hs], vnewb()[:, hs], qkTm()[:, hs], True, False, r=[vnewb, qkTm], w=[P5])
                    mm(P5()[:, hs], SBb()[:, hs], qeTb()[:, hs], False, True, r=[SBb, qeTb], w=[P5])
                for h in range(4):
                    hs = slice(h * 128, (h + 1) * 128)
                    mm(P0()[:, hs], kendb()[:, hs], vnewb()[:, hs], r=[kendb, vnewb], w=[P0.sub(h * 128, (h + 1) * 128)])
                tt(SB(h3, h=4), SB(h3, h=4), bc4(etot), ALU.mult, r=[SB, Ecol], w=[SB])
                tt(SB(), SB(), P0(), ALU.add, r=[SB, P0], w=[SB])
                cp(SBb(), SB(), r=[SB], w=[SBb], eng='act')
                if ti == NT - 1:
                    dma(gdnp.rearrange("h d v -> d h v"), SB(h3, h=4), r=[SB], eng='pool')
            else:
                zerofill(P0)
                zerofill(P5)
                SS4 = [FV(0, 1024), FV(1024, 2048), FV(2048, 3072), FV(3072, 4096)]
                SSbds = [SSbd, HV(2560, 3584)]
                ssi = 0
                for h in range(4):
                    for hf in range(2):
                        SSx = SS4[ssi % 4]
                        SSbd_ = SSbds[ssi % 2]
                        SSb3 = SSbd_("p (s v) -> p s v", s=8)
                        SS3 = SSx("p (s v) -> p s v", s=8)
                        dma(SS3, sgdn[8 * hf:8 * hf + 8, h, :, :].rearrange("s d v -> d s v"), w=[SSx])
                        cp(SSbd_(), SSx(), r=[SSx], w=[SSbd_], eng=('act', 'dve')[ssi % 2])
                        ssi += 1
                        for j in range(8):
                            s = 8 * hf + j
                            cs_ = slice(h * 128 + 8 * s, h * 128 + 8 * s + 8)
                            mm(P0()[:, cs_], SSb3[:, j, :], negwT()[:, cs_], False, (h == 3 and hf == 1 and j == 7),
                               r=[SSbd_, negwT], w=[P0])
                            mm(P5()[:, cs_], SSb3[:, j, :], qeTb()[:, cs_], False, False, r=[SSbd_, qeTb], w=[P5])
                cp(wSTb(), P0(), r=[P0], w=[wSTb], eng='act')
                for h in range(4):
                    hs = slice(h * 128, (h + 1) * 128)
                    mm(P4p()[:, hs], TinvTb()[:, hs], vbeta()[:, hs], True, False, r=[TinvTb, vbeta], w=[P4p])
                    mm(P4p()[:, hs], wSTb()[:, hs], identB(), False, True, r=[wSTb, identB], w=[P4p])
                cp(vnewb(), P4p(), r=[P4p], w=[vnewb], eng='act')
                for h in range(4):
                    hs = slice(h * 128, (h + 1) * 128)
                    mm(P5()[:, hs], vnewb()[:, hs], qkTm()[:, hs], False, h == 3, r=[vnewb, qkTm], w=[P5])
                et3 = Ecol()[:, 8:72].rearrange("p (s h) -> p s h", s=16)
                km3 = kendmd("p (s d) -> p s d", s=8)
                for h in range(4):
                    hs = slice(h * 128, (h + 1) * 128)
                    for hf in range(2):
                        SSx = SS4[ssi % 4]
                        ssi += 1
                        SS3 = SSx("p (s v) -> p s v", s=8)
                        dma(SS3, sgdn[8 * hf:8 * hf + 8, h, :, :].rearrange("s d v -> d s v"), w=[SSx])
                        tt(km3, kendb()[:, hs].unsqueeze(1).broadcast_to([128, 8, 128]),
                           segm()[:, 8 * hf:8 * hf + 8].unsqueeze(2).broadcast_to([128, 8, 128]), ALU.mult,
                           r=[kendb, segm], w=[kendmd])
                        for g in range(2):
                            Pq = ((P0, P1), (P2, P3))[hf][g]
                            for jj in range(4):
                                mm(Pq()[:, jj * 128:(jj + 1) * 128], km3[:, 4 * g + jj, :], vnewb()[:, hs],
                                   r=[kendmd, vnewb], w=[Pq.sub(jj * 128, (jj + 1) * 128)])
                            sl = SS3[:, 4 * g:4 * g + 4, :]
                            s0 = 8 * hf + 4 * g
                            tt(sl, sl, et3[:, s0:s0 + 4, h:h + 1].broadcast_to([128, 4, 128]), ALU.mult,
                               r=[SSx, Ecol], w=[SSx])
                            tt(sl, sl, Pq("p (s v) -> p s v", s=4), ALU.add, r=[SSx, Pq], w=[SSx])
                        dma(gdns[8 * hf:8 * hf + 8, h, :, :].rearrange("s d v -> d s v"), SS3, r=[SSx], eng='pool')

            dump(17, vnewb, 512, bf=True); dump(18, P5, 512, psum=True)
            if DBG_PHASE < 4:
                return
            MARKS.append((ti, 'D', S.ncomp['dve']))
            oTs, rstds = [FV(0, 512), FV(1024, 1536)], [FV(512, 1024), FV(1536, 2048)]
            osqs, szs = [HV(2048, 2560), HV(6144, 6656)], [HV(2560, 3072), HV(6656, 7168)]
            Pos, Pss, Pzs, zc0s = (P6, P5), (P1, P2), (P3, P4p), (1040, 3096)
            for br in range(2):
                act(osqs[br](), Pos[br](), AF.Square, r=[Pos[br]], w=[osqs[br]])
            for br in range(2):
                mm(Pss[br](), onesB(), osqs[br](), r=[onesB, osqs[br]], w=[Pss[br]])
            for br in range(2):
                for c in range(4):
                    proj_fm(Pzs[br], c * 128, zc0s[br] + c * 128)
            for br in range(2):
                act(rstds[br](), Pss[br](), AF.Ln, r=[Pss[br]], w=[rstds[br]], scale=1.0 / 128, bias=EPS)
            for br in range(2):
                act(rstds[br](), rstds[br](), AF.Exp, r=[rstds[br]], w=[rstds[br]], scale=-0.5)
            for br in range(2):
                act(szs[br](), Pzs[br](), AF.Silu, r=[Pzs[br]], w=[szs[br]])
            for br in range(2):
                stt(oTs[br](), Pos[br](), smallc()[:, 8 + br:9 + br], rstds[br](), ALU.mult, ALU.mult,
                    r=[Pos[br], rstds[br], smallc], w=[oTs[br]])
                tt(gT[br](), oTs[br](), szs[br](), ALU.mult, r=[oTs[br], szs[br]], w=[gT[br]])
            dump(19, gT[0], 512, bf=True); dump(20, gT[1], 512, bf=True)
            if DBG_PHASE < 5:
                return
            MARKS.append((ti, 'E', S.ncomp['dve']))
            if ti == NT - 1:
                tq = [FV(0, 512), FV(512, 1024), FV(3072, 3584)]
                for c3 in range(3):
                    Pc = (P4p, P5, P6)[c3]
                    proj_tm(Pc, 0, 1552 + c3 * 512, 512)
                    S.op('act', lambda e, o=tq[c3].t[96:128, tq[c3].lo:tq[c3].hi], i=Pc()[96:128, :]: e.copy(out=o, in_=i),
                         r=[Pc], w=[tq[c3]])
                    dma(convp[:, c3 * 512:(c3 + 1) * 512], tq[c3].t[125:128, tq[c3].lo:tq[c3].hi], r=[tq[c3]], eng='pool')
            dma(hres(), xsrc, w=[hres])
            for n in range(2):
                ns = slice(n * 512, (n + 1) * 512)
                Qa, Qb, Qc, Qd = ((P0, P1, P2, P3), (P4p, P5, P6, P0))[n]
                proj_tm(Qc, 0, 3608 + n * 512, 512)
                proj_tm(Qd, 0, 4632 + n * 512, 512)
                for h in range(4):
                    mm(Qa(), gT[0]()[:, h * 128:(h + 1) * 128], Wupa[:, h, ns], h == 0, h == 3, r=[gT[0], WUPA], w=[Qa])
                for h in range(4):
                    mm(Qb(), gT[1]()[:, h * 128:(h + 1) * 128], Wupb[:, h, ns], h == 0, h == 3, r=[gT[1], WUPB], w=[Qb])
                act(sga(), Qc(), AF.Sigmoid, r=[Qc], w=[sga])
                act(sgb(), Qd(), AF.Sigmoid, r=[Qd], w=[sgb])
                tt(sga(), sga(), Qa(), ALU.mult, r=[sga, Qa], w=[sga])
                tt(sgb(), sgb(), Qb(), ALU.mult, r=[sgb, Qb], w=[sgb])
                tt(mg()[:, ns], sga(), sgb(), ALU.add, r=[sga, sgb], w=[mg.sub(n * 512, (n + 1) * 512)])
            if KEEPWARM:
                for _ in range(KEEPWARM):
                    mm(P3(), zB(), Wout[:, 0, 0:512], r=[zB, WOUT], w=[P3])
            for k in range(8):
                tr(PT()[:, k * 128:(k + 1) * 128], mg()[:, k * 128:(k + 1) * 128], identB(), r=[mg, identB],
                   w=[PT.sub(k * 128, (k + 1) * 128)])
            cp(mT(), PT(), r=[PT], w=[mT], eng='act')
            if KEEPWARM:
                for _ in range(KEEPWARM // 2):
                    mm(P3(), zB(), Wout[:, 0, 0:512], r=[zB, WOUT], w=[P3])
            mT3 = mT("p (k t) -> p k t", k=8)
            for n in range(2):
                ns = slice(n * 512, (n + 1) * 512)
                Pn = (P1, P2)[n]
                for kc in range(8):
                    mm(Pn(), mT3[:, kc, :], Wout[:, kc, ns], kc == 0, kc == 7, r=[mT, WOUT], w=[Pn])
                tt(hres()[:, ns], hres()[:, ns], Pn(), ALU.add, r=[hres.sub(n * 512, (n + 1) * 512), Pn],
                   w=[hres.sub(n * 512, (n + 1) * 512)])
            cp(mg(), hres(), r=[hres], w=[mg])
            for k in range(8):
                tr(PT()[:, k * 128:(k + 1) * 128], mg()[:, k * 128:(k + 1) * 128], identB(), r=[mg, identB],
                   w=[PT.sub(k * 128, (k + 1) * 128)])
            cp(mT(), PT(), r=[PT], w=[mT], eng='act')
            PT3 = PTT("p (k t) -> p k t", k=2)
            for n in range(2):
                ns = slice(n * 512, (n + 1) * 512)
                Pa, Pp = ((P0, P1), (P2, P3))[n]
                for kc in range(8):
                    mm(Pa(), mT3[:, kc, :], Wpg[:, kc, ns], kc == 0, kc == 7, r=[mT, WPG], w=[Pa])
                for k in range(2):
                    mm(Pp(), PT3[:, k, :], Wple[:, k, ns], k == 0, k == 1, r=[PTT, WPLE], w=[Pp])
                sp_ = (sga, sgb)[n]
                act(sp_(), Pa(), AF.Sigmoid, r=[Pa], w=[sp_])
                tt(sp_(), sp_(), Pp(), ALU.mult, r=[sp_, Pp], w=[sp_])
                tt(hres()[:, ns], hres()[:, ns], sp_(), ALU.add, r=[hres.sub(n * 512, (n + 1) * 512), sp_],
                   w=[hres.sub(n * 512, (n + 1) * 512)])
            dump(21, hres, 1024)
            if nextA is not None:
                nextA()
            f0, f1, f2 = CO(3, 4), CO(4, 5), CO(5, 6)
            act(mg(), hres(), AF.Square, r=[hres], w=[mg, f0], accum_out=f0())
            act(f1(), f0(), AF.Ln, r=[f0], w=[f1], scale=1.0 / 1024, bias=EPS)
            act(f2(), f1(), AF.Exp, r=[f1], w=[f2], scale=-0.5)
            stt(yt(), hres(), f2(), fnwbc(), ALU.mult, ALU.mult, r=[hres, f2, fnwbc], w=[yt])
            dma(ydst, yt(), r=[yt], eng='pool')

        S.budget = DBG_BUDGET
        tl = list(range(NT + 1) if DBG_TILES is None else DBG_TILES)
        phaseA(tl[0])
        for i_, ti in enumerate(tl):
            tile(ti, (lambda t2=tl[i_ + 1]: phaseA(t2)) if i_ + 1 < len(tl) else None)
        print('op counts', S.ncomp, S.dma_cnt)
        global LAST_MARKS
        LAST_MARKS = MARKS
        S.final_wait_all('sp')
        S.emit()
    return nc


_NC = None


def kernel(x_prompt, x_sample, state_gla, state_gdn, state_conv, p_prompt, p_sample, norm_w, w_in, w_gla_gate,
           b_gla_gate, gla_norm_w, conv_w, gdn_a_log, gdn_dt_bias, gdn_norm_w, w_up_gla, w_up_gdn, w_out,
           w_ple_gate, w_ple, final_norm_w):
    global _NC
    f = lambda a: np.ascontiguousarray(np.asarray(a, dtype=np.float32))
    if _NC is None:
        _NC = build()
    nc = _NC
    shared = {
        "normw": f(norm_w).reshape(1, 1024), "win": f(w_in[0]), "wgg": f(w_gla_gate[0]), "bgg": f(b_gla_gate).reshape(1, 256),
        "glanw": f(gla_norm_w).reshape(1, 128), "convw": f(conv_w[0]), "alog": f(gdn_a_log).reshape(1, 4),
        "dtb": f(gdn_dt_bias).reshape(1, 4), "gdnnw": f(gdn_norm_w).reshape(1, 128), "wupa": f(w_up_gla[0]),
        "wupb": f(w_up_gdn[0]), "wout": f(w_out[0]), "wpg": f(w_ple_gate[0]), "wple": f(w_ple[0]),
        "fnw": f(final_norm_w).reshape(1, 1024),
    }
    xs_ = f(x_sample).reshape(8, 128, 1024)
    ps_ = f(p_sample[0]).reshape(8, 128, 256)
    in_maps = []
    for c in range(8):
        m = dict(shared)
        m["xp"] = f(x_prompt[c])
        m["xsm"] = xs_[c]
        m["pp"] = f(p_prompt[0, c])
        m["psm"] = ps_[c]
        m["sgla"] = f(state_gla[0, 16 * c:16 * c + 16])
        m["sgdn"] = f(state_gdn[0, 16 * c:16 * c + 16])
        m["sconv"] = f(state_conv[0, 16 * c:16 * c + 16]).reshape(48, 1536)
        in_maps.append(m)
    res = run_bass_kernel_spmd(nc, in_maps, core_ids=list(range(8)))
    R = res.results
    y_prompt = np.stack([R[c]["yp"] for c in range(8)], 0)
    y_sample = np.concatenate([R[c]["ysm"].reshape(16, 8, 1024) for c in range(8)], 0)
    gla_p = np.stack([R[c]["glap"] for c in range(8)], 0)[None]
    gdn_p = np.stack([R[c]["gdnp"] for c in range(8)], 0)[None]
    conv_p = np.stack([R[c]["convp"] for c in range(8)], 0)[None]
    gla_s = np.concatenate([R[c]["glas"] for c in range(8)], 0)[None]
    gdn_s = np.concatenate([R[c]["gdns"] for c in range(8)], 0)[None]
    conv_s = np.concatenate([R[c]["convs"] for c in range(8)], 0)[None]
    return (y_prompt, y_sample, gla_p, gdn_p, conv_p, gla_s, gdn_s, conv_s)
```
